# Optimizing a Trainium2 kernel written in Bass

```python
import math
import jax, jax.numpy as jnp
from jax import lax
import numpy as np

D_MODEL = 1024
BATCH = 8
SEQ = 4096
DEPTH = 2

N_MIXERS = 2
N_GLA_LAYERS = (DEPTH + 1) // 2
N_DIFF_LAYERS = DEPTH // 2
BRANCH = 2 * D_MODEL
EPS = 1e-6

GLA_HEADS = 4
GLA_DK = D_MODEL // 2 // GLA_HEADS
GLA_DV = BRANCH // GLA_HEADS
GLA_RANK = 16
GLA_GATE_NORM = 16.0
GLA_CHUNK = 64
GLA_QK = GLA_HEADS * GLA_DK
GLA_IN = 2 * GLA_QK + 2 * BRANCH + GLA_RANK

DIFF_HEAD_DIM = 64
DIFF_HEADS = BRANCH // (2 * DIFF_HEAD_DIM)
DIFF_QBLOCK = 128
DIFF_IN = 4 * BRANCH
ROPE_THETA = 10000.0
LAMBDA_STD = 0.1

kernel_name = "hybrid_gla_diffattn_interleaved"


def rms_norm(x, g):
    xf = x.astype(jnp.float32)
    y = xf * lax.rsqrt(jnp.mean(xf * xf, axis=-1, keepdims=True) + EPS)
    return (y * g.astype(jnp.float32)).astype(x.dtype)


def rope_tables(T, d):
    inv_freq = 1.0 / (ROPE_THETA ** (jnp.arange(0, d, 2, dtype=jnp.float32) / d))
    pos = jnp.arange(T, dtype=jnp.float32)
    ang = pos[:, None] * inv_freq[None, :]
    return jnp.cos(ang), jnp.sin(ang)


def apply_rope(t, cos, sin):
    c = cos[:, None, None, :]
    s = sin[:, None, None, :]
    t1, t2 = jnp.split(t, 2, axis=-1)
    return jnp.concatenate([t1 * c - t2 * s, t2 * c + t1 * s], axis=-1)


def gla_mixer(h, w_in, w_g2, b_g, norm_g, w_out):
    B, T, _ = h.shape
    C = GLA_CHUNK
    N = T // C
    proj = h @ w_in
    q, k, v, gate, low = jnp.split(
        proj, [GLA_QK, 2 * GLA_QK, 2 * GLA_QK + BRANCH, 2 * GLA_QK + 2 * BRANCH], axis=-1)
    f32 = jnp.float32
    q = q.astype(f32).reshape(B, N, C, GLA_HEADS, GLA_DK) * (GLA_DK ** -0.5)
    k = k.astype(f32).reshape(B, N, C, GLA_HEADS, GLA_DK)
    v = v.astype(f32).reshape(B, N, C, GLA_HEADS, GLA_DV)
    logg = jax.nn.log_sigmoid((low @ w_g2 + b_g).astype(f32)) / GLA_GATE_NORM
    logg = logg.reshape(B, N, C, GLA_HEADS, GLA_DK)
    bcum = jnp.cumsum(logg, axis=2)
    blast = bcum[:, :, -1]
    q_t = q * jnp.exp(bcum)
    k_t = k * jnp.exp(-bcum)
    k_end = k * jnp.exp(blast[:, :, None] - bcum)
    causal = jnp.tril(jnp.ones((C, C), dtype=bool))
    A = jnp.einsum('bnihk,bnjhk->bnhij', q_t, k_t)
    A = jnp.where(causal, A, 0.0)
    o_intra = jnp.einsum('bnhij,bnjhv->bnihv', A, v)

    def step(S, inp):
        qn, kn, vn, dn = inp
        o = jnp.einsum('bihk,bhkv->bihv', qn, S)
        S = dn[..., None] * S + jnp.einsum('bjhk,bjhv->bhkv', kn, vn)
        return S, o

    xs = (jnp.moveaxis(q_t, 1, 0), jnp.moveaxis(k_end, 1, 0),
          jnp.moveaxis(v, 1, 0), jnp.moveaxis(jnp.exp(blast), 1, 0))
    S0 = jnp.zeros((B, GLA_HEADS, GLA_DK, GLA_DV), f32)
    _, o_inter = lax.scan(step, S0, xs)
    o = o_intra + jnp.moveaxis(o_inter, 0, 1)
    o = rms_norm(o.reshape(B, T, GLA_HEADS, GLA_DV), norm_g)
    o = o.reshape(B, T, BRANCH) * jax.nn.silu(gate.astype(f32))
    return o.astype(h.dtype) @ w_out


def diff_mixer(h, w_in, lq1, lk1, lq2, lk2, norm_g, w_out, lambda_init):
    B, T, _ = h.shape
    d = DIFF_HEAD_DIM
    f32 = jnp.float32
    proj = h @ w_in
    q, k, v, gate = jnp.split(proj, 4, axis=-1)
    cos, sin = rope_tables(T, d)
    q = apply_rope(q.astype(f32).reshape(B, T, DIFF_HEADS, 2, d), cos, sin)
    k = apply_rope(k.astype(f32).reshape(B, T, DIFF_HEADS, 2, d), cos, sin)
    v = v.astype(f32).reshape(B, T, DIFF_HEADS, 2 * d)
    lam = (jnp.exp(jnp.sum(lq1.astype(f32) * lk1.astype(f32)))
           - jnp.exp(jnp.sum(lq2.astype(f32) * lk2.astype(f32))) + lambda_init)
    qh = q.transpose(0, 2, 3, 1, 4)
    kh = k.transpose(0, 2, 3, 1, 4)
    vh = v.transpose(0, 2, 1, 3)
    scale = d ** -0.5
    outs = []
    for blk in range(T // DIFF_QBLOCK):
        s = blk * DIFF_QBLOCK
        e = s + DIFF_QBLOCK
        sc = jnp.einsum('bhcqd,bhckd->bhcqk', qh[:, :, :, s:e], kh[:, :, :, :e]) * scale
        causal = np.arange(e)[None, :] <= np.arange(s, e)[:, None]
        sc = jnp.where(causal, sc, -jnp.inf)
        p = jax.nn.softmax(sc, axis=-1)
        attn = p[:, :, 0] - lam * p[:, :, 1]
        outs.append(jnp.einsum('bhqk,bhkv->bhqv', attn, vh[:, :, :e]))
    o = jnp.concatenate(outs, axis=2)
    o = rms_norm(o, norm_g) * (1.0 - lambda_init)
    o = o.transpose(0, 2, 1, 3).reshape(B, T, BRANCH) * jax.nn.silu(gate.astype(f32))
    return o.astype(h.dtype) @ w_out


def setup_inputs(seed: int = 0) -> dict:
    key = jax.random.key(seed)
    ks = jax.random.split(key, 20)
    f32 = jnp.float32
    nrm = lambda k, shape, s: jax.random.normal(k, shape, f32) * s
    NG, ND = N_GLA_LAYERS, N_DIFF_LAYERS
    return {
        "x": jax.random.normal(ks[0], (BATCH, SEQ, D_MODEL), f32),
        "pre_g": 1.0 + nrm(ks[1], (DEPTH, D_MODEL), 0.02),
        "post_g": 1.0 + nrm(ks[2], (DEPTH, D_MODEL), 0.02),
        "gla_w_in": nrm(ks[3], (NG, D_MODEL, GLA_IN), D_MODEL ** -0.5),
        "gla_w_g2": nrm(ks[4], (NG, GLA_RANK, GLA_QK), GLA_RANK ** -0.5),
        "gla_b_g": nrm(ks[5], (NG, GLA_QK), 0.01),
        "gla_norm_g": 1.0 + nrm(ks[6], (NG, GLA_DV), 0.02),
        "gla_w_out": nrm(ks[7], (NG, BRANCH, D_MODEL), BRANCH ** -0.5),
        "diff_w_in": nrm(ks[8], (ND, D_MODEL, DIFF_IN), D_MODEL ** -0.5),
        "diff_lam_q1": nrm(ks[9], (ND, DIFF_HEAD_DIM), LAMBDA_STD),
        "diff_lam_k1": nrm(ks[10], (ND, DIFF_HEAD_DIM), LAMBDA_STD),
        "diff_lam_q2": nrm(ks[11], (ND, DIFF_HEAD_DIM), LAMBDA_STD),
        "diff_lam_k2": nrm(ks[12], (ND, DIFF_HEAD_DIM), LAMBDA_STD),
        "diff_norm_g": 1.0 + nrm(ks[13], (ND, 2 * DIFF_HEAD_DIM), 0.02),
        "diff_w_out": nrm(ks[14], (ND, BRANCH, D_MODEL), BRANCH ** -0.5),
    }


def reference(x, pre_g, post_g, gla_w_in, gla_w_g2, gla_b_g, gla_norm_g, gla_w_out,
              diff_w_in, diff_lam_q1, diff_lam_k1, diff_lam_q2, diff_lam_k2,
              diff_norm_g, diff_w_out):
    for i in range(DEPTH):
        h = rms_norm(x, pre_g[i])
        j = i // N_MIXERS
        if i % N_MIXERS == 0:
            y = gla_mixer(h, gla_w_in[j], gla_w_g2[j], gla_b_g[j], gla_norm_g[j], gla_w_out[j])
        else:
            lambda_init = 0.8 - 0.6 * math.exp(-0.3 * i)
            y = diff_mixer(h, diff_w_in[j], diff_lam_q1[j], diff_lam_k1[j], diff_lam_q2[j],
                           diff_lam_k2[j], diff_norm_g[j], diff_w_out[j], lambda_init)
        x = x + rms_norm(y, post_g[i])
    return x
```

```python
import math
from contextlib import ExitStack

import numpy as np
import ml_dtypes
import concourse.bass as bass
import concourse.mybir as mybir
from concourse.bass_utils import run_bass_kernel_spmd

F32 = mybir.dt.float32
BF16 = mybir.dt.bfloat16
AF = mybir.ActivationFunctionType
ALU = mybir.AluOpType

T = 4096
D = 1024
BR = 2048
NT = T // 128
GIN = 5136
EPS = 1e-6
LAM_INIT = 0.8 - 0.6 * math.exp(-0.3 * 1)
NCORES = 8
DBG = {"heads": 16, "nqt": 8, "inproj": True, "att": True, "epi": True, "F1": True, "nkb": 99, "ipp": "qgv", "rope": True}

ENGS = ("pe", "act", "dve", "pool", "sp")


class Buf:
    __slots__ = ("name", "w", "r", "dsem", "dcnt", "excl")

    def __init__(self, name):
        self.name = name
        self.excl = False
        self.w = None
        self.r = []
        self.dsem = None
        self.dcnt = 0


class Rec:
    __slots__ = ("waits", "fn", "inc", "dma_inc", "pos", "val")

    def __init__(self, waits, fn):
        self.waits = waits
        self.fn = fn
        self.inc = False
        self.dma_inc = None
        self.pos = -1
        self.val = 0


class Sched:
    def __init__(self, nc, stack):
        self.nc = nc
        self.stack = stack
        self.ops = {e: [] for e in ENGS}
        self.sem = {e: stack.enter_context(nc.semaphore("sem_" + e)) for e in ENGS}
        self.waited = {e: {} for e in ENGS}
        self.allbufs = []
        self.same_engine_sync = True
        self.same_engine_all = True
        self.nops = 0
        self.nbar = 0

    def buf(self, name):
        b = Buf("%s_%d" % (name, len(self.allbufs)))
        self.allbufs.append(b)
        return b

    def _dsem(self, b):
        if b.dsem is None:
            b.dsem = self.stack.enter_context(self.nc.semaphore("d_" + b.name))
        return b.dsem

    def _push(self, eng, rec):
        rec.pos = len(self.ops[eng])
        self.ops[eng].append(rec)

    def _resolve(self, eng, tok, waits, is_raw):
        if tok[0] == "e":
            _, e2, rec = tok
            if e2 == eng:
                if eng in ("pe", "sp") or not (self.same_engine_sync and (is_raw or self.same_engine_all)):
                    return
            key = "e_" + e2
            if self.waited[eng].get(key, -1) >= rec.pos:
                return
            self.waited[eng][key] = rec.pos
            rec.inc = True
            waits.append(("e", e2, rec))
        else:
            _, b, val = tok
            key = "d_" + b.name
            if self.waited[eng].get(key, 0) >= val:
                return
            self.waited[eng][key] = val
            waits.append(("d", b.dsem, val))

    def _deps(self, eng, R, W):
        waits = []
        for b in R:
            if b.w is not None:
                self._resolve(eng, b.w, waits, True)
            if b.excl:
                for t in b.r:
                    if t[0] != "e" or t[1] != eng:
                        self._resolve(eng, t, waits, False)
        for b in W:
            if b.w is not None:
                self._resolve(eng, b.w, waits, False)
            for t in b.r:
                self._resolve(eng, t, waits, False)
        return waits

    def op(self, eng, fn, R=(), W=()):
        waits = self._deps(eng, R, W)
        rec = Rec(waits, fn)
        self._push(eng, rec)
        tok = ("e", eng, rec)
        for b in R:
            b.r = [t for t in b.r if not (t[0] == "e" and t[1] == eng)]
            b.r.append(tok)
        for b in W:
            b.w = tok
            b.r = []
        self.nops += 1
        return rec

    def dma(self, q, out_ap, in_ap, R=(), W=()):
        waits = self._deps(q, R, W)
        owner = W[0] if W else R[0]
        sem = self._dsem(owner)
        if not W and owner.dcnt > 0:
            self._resolve(q, ("d", owner, owner.dcnt), waits, False)
        owner.dcnt += 16
        tok = ("d", owner, owner.dcnt)
        rec = Rec(waits, lambda e: e.dma_start(out=out_ap, in_=in_ap))
        rec.dma_inc = sem
        self._push(q, rec)
        for b in R:
            b.r.append(tok)
        for b in W:
            b.w = tok
            b.r = []
        self.nops += 1
        return rec

    def barrier(self):
        waits = []
        comp = ("pe", "act", "dve", "pool")
        last = {}
        for e in comp:
            recs = [r for r in self.ops[e] if r.fn is not None]
            if recs:
                last[e] = recs[-1]
                if self.waited["sp"].get("e_" + e, -1) < recs[-1].pos:
                    self.waited["sp"]["e_" + e] = recs[-1].pos
                    recs[-1].inc = True
                    waits.append(("e", e, recs[-1]))
        for b in self.allbufs:
            if b.dsem is not None and b.dcnt > self.waited["sp"].get("d_" + b.name, 0):
                self.waited["sp"]["d_" + b.name] = b.dcnt
                waits.append(("d", b.dsem, b.dcnt))
        sem_sp = self.sem["sp"]
        rec = Rec(waits, lambda e: e.sem_inc(sem_sp, 1))
        self._push("sp", rec)
        self.nbar += 1
        v = self.nbar
        for e in comp:
            self._push(e, Rec([("s", sem_sp, v)], None))
            for e2 in comp:
                if e2 in last:
                    self.waited[e]["e_" + e2] = last[e2].pos
            for b in self.allbufs:
                if b.dsem is not None:
                    self.waited[e]["d_" + b.name] = b.dcnt
        for b in self.allbufs:
            b.w = None
            b.r = []

    def replay(self, block):
        S = self
        for eng in ENGS:
            n = 0
            for rec in S.ops[eng]:
                if rec.inc:
                    n += 1
                    rec.val = n
        self.nsig = {eng: sum(1 for r in S.ops[eng] if r.inc) for eng in ENGS}

        def run(eng, e):
            for rec in S.ops[eng]:
                for w in rec.waits:
                    if w[0] == "e":
                        assert w[2].inc and w[2].val > 0
                        e.wait_ge(S.sem[w[1]], w[2].val)
                    else:
                        e.wait_ge(w[1], w[2])
                if rec.fn is None:
                    continue
                ins = rec.fn(e)
                if rec.dma_inc is not None:
                    ins.then_inc(rec.dma_inc, 16)
                elif rec.inc:
                    ins.then_inc(S.sem[eng], 1)

        @block.tensor
        def _(e):
            run("pe", e)

        @block.scalar
        def _(e):
            run("act", e)

        @block.vector
        def _(e):
            run("dve", e)

        @block.gpsimd
        def _(e):
            run("pool", e)

        @block.sync
        def _(e):
            run("sp", e)


class Arena:
    def __init__(self, ap, n):
        self.ap = ap
        self.n = n
        self.top = 0
        self.peak = 0

    def alloc(self, n):
        a = self.ap[:, self.top:self.top + n]
        self.top += (n + 15) // 16 * 16
        self.peak = max(self.peak, self.top)
        assert self.top <= self.n, ("arena overflow", self.top, self.n)
        return a

    def mark(self):
        return self.top

    def release(self, m):
        self.top = m


class Tile:
    __slots__ = ("ap", "b")

    def __init__(self, ap, b):
        self.ap = ap
        self.b = b


class K:
    def __init__(self, nc, S, AFa, ABa, ps):
        self.nc, self.S, self.AF, self.AB, self.ps = nc, S, AFa, ABa, ps
        self.bankb = [S.buf("bank%d" % i) for i in range(8)]
        for b in self.bankb:
            b.excl = True
        self.rr = 0
        self.cast_rr = 0

    def f32(self, n, name):
        return Tile(self.AF.alloc(n), self.S.buf(name))

    def b16(self, n, name):
        return Tile(self.AB.alloc(n), self.S.buf(name))

    def banks(self, i, n=1):
        return self.ps[:, i * 512:(i + n) * 512], [self.bankb[j] for j in range(i, i + n)]

    def nextbank(self, n=1, lo=0, hi=8):
        if self.rr < lo or self.rr + n > hi:
            self.rr = lo
        i = self.rr
        self.rr += n
        return self.banks(i, n)

    def mm(self, out, lhsT, rhs, start, stop, R, W):
        self.S.op("pe", lambda e: e.matmul(out, lhsT=lhsT, rhs=rhs, start=start, stop=stop), R, W)

    def mmt(self, out, lhsT, rhs, start, stop, tpos, R, W):
        self.S.op("pe", lambda e: e.matmul(out, lhsT=lhsT, rhs=rhs, start=start, stop=stop, tile_position=tpos), R, W)

    def tr(self, out, in_, ident, R, W):
        self.S.op("pe", lambda e: e.transpose(out=out, in_=in_, identity=ident), R, W)

    def act(self, out, in_, func, R, W, scale=None, bias=None, accum=None):
        kw = {}
        if scale is not None:
            kw["scale"] = scale
        if bias is not None:
            kw["bias"] = bias
        if accum is not None:
            kw["accum_out"] = accum
        self.S.op("act", lambda e: e.activation(out=out, in_=in_, func=func, **kw), R, W)

    def tt(self, eng, out, a, b, op, R, W):
        self.S.op(eng, lambda e: e.tensor_tensor(out=out, in0=a, in1=b, op=op), R, W)

    def stt(self, eng, out, a, sc, b, op0, op1, R, W, accum=None):
        if accum is None:
            self.S.op(eng, lambda e: e.scalar_tensor_tensor(out=out, in0=a, scalar=sc, in1=b, op0=op0, op1=op1), R, W)
        else:
            self.S.op(eng, lambda e: e.scalar_tensor_tensor(out=out, in0=a, scalar=sc, in1=b, op0=op0, op1=op1,
                                                           accum_out=accum), R, W)

    def ts(self, eng, out, a, s1, op0, R, W, s2=None, op1=None):
        if s2 is None:
            self.S.op(eng, lambda e: e.tensor_scalar(out=out, in0=a, scalar1=s1, scalar2=None, op0=op0), R, W)
        else:
            self.S.op(eng, lambda e: e.tensor_scalar(out=out, in0=a, scalar1=s1, scalar2=s2, op0=op0, op1=op1), R, W)

    def copy(self, eng, out, in_, R, W):
        if eng == "act":
            self.S.op("act", lambda e: e.copy(out=out, in_=in_), R, W)
        else:
            self.S.op(eng, lambda e: e.tensor_copy(out=out, in_=in_), R, W)

    def cast_any(self, out, in_, R, W):
        eng = ("pool", "dve", "act")[self.cast_rr % 3]
        self.cast_rr += 1
        self.copy(eng, out, in_, R, W)

    def memset(self, eng, ap, val, W):
        self.S.op(eng, lambda e: e.memset(ap, val), (), W)

    def rstd(self, ssq, tmp, out, n, eps_ap, R, W):
        self.act(tmp, ssq, AF.Ln, R, W, scale=1.0 / n, bias=eps_ap)
        self.act(out, tmp, AF.Exp, W, W, scale=-0.5)


def v3(ap, a):
    return ap.rearrange("p (a b) -> p a b", a=a)


def load_consts(k, dr):
    S = k.S
    c = {}
    c["ident"] = k.b16(128, "ident")
    c["rperm"] = k.b16(128, "rperm")
    c["ones"] = k.b16(128, "ones")
    c["maskT2"] = k.b16(256, "maskT2")
    c["negm"] = k.b16(128, "negm")
    c["selb"] = k.b16(256, "selb")
    for i, nm in enumerate(("ident", "rperm", "ones")):
        S.dma("sp", c[nm].ap, dr["cst_bf"][:, i * 128:(i + 1) * 128], W=[c[nm].b])
    S.dma("sp", c["maskT2"].ap, dr["cst_bf"][:, 384:640], W=[c["maskT2"].b])
    S.dma("sp", c["negm"].ap, dr["cst_bf"][:, 640:768], W=[c["negm"].b])
    S.dma("sp", c["selb"].ap, dr["cst_bf"][:, 768:1024], W=[c["selb"].b])
    c["triU"] = k.f32(128, "triU")
    c["triL"] = k.f32(128, "triL")
    c["maskU4"] = k.f32(512, "maskU4")
    c["sel"] = k.f32(256, "sel")
    S.dma("sp", c["triU"].ap, dr["cst_f"][:, 0:128], W=[c["triU"].b])
    S.dma("sp", c["triL"].ap, dr["cst_f"][:, 128:256], W=[c["triL"].b])
    S.dma("sp", c["maskU4"].ap, dr["cst_f"][:, 256:768], W=[c["maskU4"].b])
    S.dma("sp", c["sel"].ap, dr["cst_f"][:, 768:1024], W=[c["sel"].b])
    c["eps"] = k.f32(16, "eps")
    k.memset("pool", c["eps"].ap[:, 0:1], EPS, [c["eps"].b])
    c["small"] = k.f32(64, "small")
    c["junk"] = k.b16(2048, "junk")
    return c


def bcast_load(k, tile_ap, b, dram_row):
    k.S.dma("sp", tile_ap, dram_row.partition_broadcast(128), W=[b])


def phase_A0(k, c, dr):
    S = k.S
    mF, mB = k.AF.mark(), k.AB.mark()
    w_in = k.b16(8 * GIN, "w_in")
    w3 = v3(w_in.ap, 8)
    m2 = k.AF.mark()
    HW = GIN // 2
    stg = [k.f32(HW, "stg%d" % i) for i in range(4)]
    n = 0
    for kc in range(8):
        for half in range(2):
            st = stg[n % 4]
            n += 1
            S.dma("sp", st.ap, dr["gla_w_in"][kc * 128:(kc + 1) * 128, half * HW:(half + 1) * HW], W=[st.b])
            k.cast_any(w3[:, kc, half * HW:(half + 1) * HW], st.ap, [st.b], [w_in.b])
    S.barrier()
    k.AF.release(m2)
    pre0 = k.f32(D, "pre0")
    bcast_load(k, pre0.ap, pre0.b, dr["pre_g"][0:1, :])
    ngf = k.f32(BR, "ngf")
    for h in range(4):
        bcast_load(k, ngf.ap[:, h * 512:(h + 1) * 512], ngf.b, dr["gla_norm_g"][0:1, :])
    wg2 = k.f32(512, "wg2")
    S.dma("sp", wg2.ap[0:16, :], dr["gla_w_g2"], W=[wg2.b])
    S.dma("sp", wg2.ap[16:17, :], dr["gla_b_g"], W=[wg2.b])
    lowT = [k.f32(128, "lowT%d" % i) for i in range(2)]
    for t in lowT:
        k.memset("pool", t.ap[0:32, :], 1.0, [t.b])
    xs = [k.f32(D, "xs%d" % i) for i in range(2)]
    ez = k.f32(512, "ez")
    sp = k.f32(512, "sp")
    ebT = k.f32(512, "ebT")
    enT = k.f32(512, "enT")
    erev = k.f32(512, "erev")
    Sf = [k.f32(512, "Sf%d" % h) for h in range(4)]
    Sb = [k.b16(512, "Sb%d" % h) for h in range(4)]
    for h in range(4):
        k.memset("pool", Sf[h].ap, 0.0, [Sf[h].b])
        k.memset("pool", Sb[h].ap, 0.0, [Sb[h].b])
    sm = [k.f32(16, "sm%d" % i) for i in range(2)]
    hb = [k.b16(D, "hb%d" % i) for i in range(2)]
    hT = [k.b16(D, "hT%d" % i) for i in range(2)]
    qtT = [k.b16(512, "qtT")] * 2
    ktT = [k.b16(512, "ktT")] * 2
    kend = [k.b16(512, "kend")] * 2
    vb = [k.b16(BR, "vb")] * 2
    sg = k.b16(BR, "sg")
    sgn = [k.b16(BR, "sgn")] * 2
    ATb = [k.b16(512, "ATb")] * 2
    og = [k.b16(BR, "og%d" % i) for i in range(2)]
    ogT = [k.b16(BR, "ogT%d" % i) for i in range(2)]
    ident = c["ident"]
    junk = c["junk"]
    eps = c["eps"]
    SCALE = 128 ** -0.5

    smp = [k.f32(16, "smp%d" % i) for i in range(2)]

    def pro_a(i):
        p = i % 2
        x = xs[p]
        s = smp[p]
        S.dma("sp", x.ap, dr["x"][i * 128:(i + 1) * 128, :], W=[x.b])
        k.act(junk.ap[:, 0:D], x.ap, AF.Square, [x.b], [junk.b, s.b], accum=s.ap[:, 0:1])
        k.rstd(s.ap[:, 0:1], s.ap[:, 1:2], s.ap[:, 2:3], D, eps.ap[:, 0:1], [s.b, eps.b], [s.b])
        h = hb[p]
        k.stt("dve", h.ap, x.ap, s.ap[:, 2:3], pre0.ap, ALU.mult, ALU.mult, [x.b, s.b, pre0.b], [h.b])

    def pro_b(i):
        p = i % 2
        h = hb[p]
        pb, pbb = k.banks(7)
        pv = pb.bitcast(BF16)
        for kc in range(8):
            k.tr(pv[:, kc * 128:(kc + 1) * 128], h.ap[:, kc * 128:(kc + 1) * 128], ident.ap, [h.b, ident.b], pbb)
        ht = hT[p]
        k.copy("act", ht.ap, pv, pbb, [ht.b])

    osb = k.f32(BR, "osb")
    smh = [k.f32(16, "smh%d" % i) for i in range(4)]
    prev_tail = [None]

    def tail(i, o_):
        pt, ptb = k.banks(5, 2)
        ptv = pt.bitcast(BF16)
        for cc in range(16):
            k.tr(ptv[:, cc * 128:(cc + 1) * 128], o_.ap[:, cc * 128:(cc + 1) * 128], ident.ap, [o_.b, ident.b], ptb)
        ot = ogT[i % 2]
        k.copy("act", ot.ap, ptv, ptb, [ot.b])
        S.dma("sp", dr["ogT"][i], ot.ap, R=[ot.b])

    def low(i, bank=5):
        ht_ = hT[i % 2]
        pl, plb = k.banks(bank)
        for kc in range(8):
            k.mm(pl[0:16, 0:128], w3[:, kc, 5120:5136], ht_.ap[:, kc * 128:(kc + 1) * 128], kc == 0, kc == 7,
                 [ht_.b, w_in.b], plb)
        k.copy("act", lowT[i % 2].ap[0:16, :], pl[0:16, 0:128], plb, [lowT[i % 2].b])

    pro_a(0)
    pro_b(0)
    low(0)
    for i in range(NT):
        p = i % 2
        ht = hT[p]
        hk = lambda kc: ht.ap[:, kc * 128:(kc + 1) * 128]
        lt = lowT[p]
        pkt, pktb = k.banks(4)
        for kc in range(8):
            k.mm(pkt, hk(kc), w3[:, kc, 512:1024], kc == 0, kc == 7, [ht.b, w_in.b], pktb)
        pz, pzb = k.banks(5)
        k.mm(pz, lt.ap[0:17, :], wg2.ap[0:17, :], True, True, [lt.b, wg2.b], pzb)
        k.act(ez.ap, pz, AF.Exp, pzb, [ez.b], scale=-1.0)
        k.act(sp.ap, ez.ap, AF.Ln, [ez.b], [sp.b], bias=1.0)
        pv4, pv4b = k.banks(0, 4)
        for cc in range(4):
            for kc in range(8):
                k.mm(pv4[:, cc * 512:(cc + 1) * 512], hk(kc), w3[:, kc, 1024 + cc * 512:1024 + (cc + 1) * 512],
                     kc == 0, kc == 7, [ht.b, w_in.b], [pv4b[cc]])
        v = vb[p]
        k.copy("act", v.ap, pv4, pv4b, [v.b])
        if prev_tail[0] is not None:
            tail(*prev_tail[0])
            prev_tail[0] = None
        pc, pcb = k.banks(5)
        for hh in range(4):
            k.mm(pc[:, hh * 128:(hh + 1) * 128], sp.ap[:, hh * 128:(hh + 1) * 128], c["triU"].ap, True, True,
                 [sp.b, c["triU"].b], pcb)
        pr, prb = k.banks(6)
        k.mm(pr, c["triL"].ap, sp.ap, True, True, [sp.b, c["triL"].b], prb)
        k.act(ebT.ap, pc, AF.Exp, pcb, [ebT.b])
        k.act(enT.ap, pc, AF.Exp, pcb, [enT.b], scale=-1.0)
        k.act(erev.ap, pr, AF.Exp, prb, [erev.b])
        ke = kend[p]
        k.tt("dve", ke.ap, pkt, erev.ap, ALU.mult, pktb + [erev.b], [ke.b])
        if i + 1 < NT:
            pro_a(i + 1)
        pq, pqb = k.banks(5)
        for hh in range(4):
            for kc in range(8):
                k.mm(pq[:, hh * 128:(hh + 1) * 128], w3[:, kc, hh * 128:(hh + 1) * 128], hk(kc), kc == 0, kc == 7,
                     [ht.b, w_in.b], pqb)
        pk, pkb = k.banks(6)
        for hh in range(4):
            for kc in range(8):
                k.mm(pk[:, hh * 128:(hh + 1) * 128], w3[:, kc, 512 + hh * 128:512 + (hh + 1) * 128], hk(kc), kc == 0,
                     kc == 7, [ht.b, w_in.b], pkb)
        qt_, kt_ = qtT[p], ktT[p]
        k.stt("dve", qt_.ap, pq, SCALE, ebT.ap, ALU.mult, ALU.mult, pqb + [ebT.b], [qt_.b])
        k.tt("dve", kt_.ap, pk, enT.ap, ALU.mult, pkb + [enT.b], [kt_.b])
        pg4, pg4b = k.banks(0, 4)
        for cc in range(4):
            for kc in range(8):
                k.mm(pg4[:, cc * 512:(cc + 1) * 512], hk(kc), w3[:, kc, 3072 + cc * 512:3072 + (cc + 1) * 512],
                     kc == 0, kc == 7, [ht.b, w_in.b], [pg4b[cc]])
        if i + 1 < NT:
            pro_b(i + 1)
        k.act(sg.ap, pg4, AF.Silu, pg4b, [sg.b])
        sn = sgn[p]
        k.tt("pool", sn.ap, sg.ap, ngf.ap, ALU.mult, [sg.b, ngf.b], [sn.b])
        pa, pab = k.banks(4)
        for hh in range(4):
            sl = slice(hh * 128, (hh + 1) * 128)
            k.mm(pa[:, sl], kt_.ap[:, sl], qt_.ap[:, sl], True, True, [kt_.b, qt_.b], pab)
        at = ATb[p]
        k.tt("dve", at.ap, pa, c["maskU4"].ap, ALU.mult, pab + [c["maskU4"].b], [at.b])
        pkvs = []
        if i < NT - 1:
            for hh in range(4):
                sl = slice(hh * 128, (hh + 1) * 128)
                vs = slice(hh * 512, (hh + 1) * 512)
                pkv, pkvb = k.banks(4 + hh)
                k.mm(pkv, ke.ap[:, sl], v.ap[:, vs], True, True, [ke.b, v.b], pkvb)
                pkvs.append((pkv, pkvb))
        po, pob = k.banks(0, 4)
        for hh in range(4):
            sl = slice(hh * 128, (hh + 1) * 128)
            vs = slice(hh * 512, (hh + 1) * 512)
            k.mm(po[:, vs], at.ap[:, sl], v.ap[:, vs], True, False, [at.b, v.b], [pob[hh]])
            k.mm(po[:, vs], qt_.ap[:, sl], Sb[hh].ap, False, True, [qt_.b, Sb[hh].b], [pob[hh]])
        for hh, (pkv, pkvb) in enumerate(pkvs):
            k.stt("dve", Sf[hh].ap, Sf[hh].ap, ebT.ap[:, hh * 128 + 127:hh * 128 + 128], pkv, ALU.mult, ALU.add,
                  [Sf[hh].b, ebT.b] + pkvb, [Sf[hh].b])
            k.copy("pool", Sb[hh].ap, Sf[hh].ap, [Sf[hh].b], [Sb[hh].b])
        if i + 1 < NT:
            low(i + 1, 4)
        o_ = og[p]
        for hh in range(4):
            vs = slice(hh * 512, (hh + 1) * 512)
            sh = smh[hh]
            k.act(junk.ap[:, 0:512], po[:, vs], AF.Square, [pob[hh]], [junk.b, sh.b], accum=sh.ap[:, 0:1])
            k.copy("dve", osb.ap[:, vs], po[:, vs], [pob[hh]], [osb.b])
            k.rstd(sh.ap[:, 0:1], sh.ap[:, 1:2], sh.ap[:, 2:3], 512, eps.ap[:, 0:1], [sh.b, eps.b], [sh.b])
        for hh in range(4):
            vs = slice(hh * 512, (hh + 1) * 512)
            k.stt("dve", o_.ap[:, vs], osb.ap[:, vs], smh[hh].ap[:, 2:3], sn.ap[:, vs], ALU.mult, ALU.mult,
                  [osb.b, smh[hh].b, sn.b], [o_.b])
        prev_tail[0] = (i, o_)
    tail(*prev_tail[0])
    S.barrier()
    k.AF.release(mF)
    k.AB.release(mB)


def phase_F(k, c, dr, w_out_d, post_row, xin_d, out_d, pre_row=None, h1T=None):
    S = k.S
    mF, mB = k.AF.mark(), k.AB.mark()
    w_out = k.b16(16 * D, "w_out")
    wo3 = v3(w_out.ap, 16)
    wbuf = [S.buf("w_out_p%d" % j) for j in range(8)]
    stg = [k.f32(2 * D, "stgF%d" % i) for i in range(2)]
    for j in range(8):
        st = stg[j % 2]
        S.dma("sp", v3(st.ap, 2), w_out_d[j * 256:(j + 1) * 256, :].rearrange("(c p) n -> p c n", p=128), W=[st.b])
        k.copy(("dve", "act")[j % 2], w_out.ap[:, j * 2 * D:(j + 1) * 2 * D], st.ap, [st.b], [wbuf[j]])
    postg = k.f32(D, "postg")
    bcast_load(k, postg.ap, postg.b, post_row)
    if h1T is not None:
        preg = k.f32(D, "preg")
        bcast_load(k, preg.ap, preg.b, pre_row)
        h1 = [k.b16(D, "h1_%d" % i) for i in range(2)]
    ogs = [k.b16(BR, "ogs%d" % i) for i in range(4)]
    xs = [k.f32(D, "xsF%d" % i) for i in range(4)]
    tmp = [k.f32(D, "tmpF%d" % i) for i in range(2)]
    x1 = [k.f32(D, "x1F%d" % i) for i in range(2)]
    sm = [k.f32(16, "smF%d" % i) for i in range(2)]
    junk, eps, ident = c["junk"], c["eps"], c["ident"]

    def load(i):
        o = ogs[i % 4]
        S.dma("sp", o.ap, dr["ogT"][i], W=[o.b])
        x = xs[i % 4]
        S.dma("sp", x.ap, xin_d[i * 128:(i + 1) * 128, :], W=[x.b])

    def ymm(i):
        o = ogs[i % 4]
        py, pyb = k.banks(2 * (i % 3), 2)
        for half in range(2):
            for cc in range(16):
                k.mm(py[:, half * 512:(half + 1) * 512], o.ap[:, cc * 128:(cc + 1) * 128],
                     wo3[:, cc, half * 512:(half + 1) * 512], cc == 0, cc == 15, [o.b, wbuf[cc // 2]], [pyb[half]])
        return py, pyb

    load(0)
    load(1)
    load(2)
    yq = [ymm(0), ymm(1)]
    for i in range(NT):
        if i + 3 < NT:
            load(i + 3)
        x, s = xs[i % 4], sm[i % 2]
        py, pyb = yq.pop(0)
        if i + 2 < NT:
            yq.append(ymm(i + 2))
        k.act(junk.ap[:, 0:D], py, AF.Square, pyb, [junk.b, s.b], accum=s.ap[:, 0:1])
        k.rstd(s.ap[:, 0:1], s.ap[:, 1:2], s.ap[:, 2:3], D, eps.ap[:, 0:1], [s.b, eps.b], [s.b])
        t = tmp[i % 2]
        k.stt("dve", t.ap, py, s.ap[:, 2:3], postg.ap, ALU.mult, ALU.mult, pyb + [s.b, postg.b], [t.b])
        xo = x1[i % 2]
        k.tt("pool", xo.ap, t.ap, x.ap, ALU.add, [t.b, x.b], [xo.b])
        S.dma("sp", out_d[i * 128:(i + 1) * 128, :], xo.ap, R=[xo.b])
        if h1T is not None:
            k.act(junk.ap[:, 0:D], xo.ap, AF.Square, [xo.b], [junk.b, s.b], accum=s.ap[:, 4:5])
            k.rstd(s.ap[:, 4:5], s.ap[:, 5:6], s.ap[:, 6:7], D, eps.ap[:, 0:1], [s.b, eps.b], [s.b])
            hh = h1[i % 2]
            k.stt("dve", hh.ap, xo.ap, s.ap[:, 6:7], preg.ap, ALU.mult, ALU.mult, [xo.b, s.b, preg.b], [hh.b])
            pb, pbb = k.banks(6 + (i % 2))
            pv = pb.bitcast(BF16)
            for kc in range(8):
                k.tr(pv[:, kc * 128:(kc + 1) * 128], hh.ap[:, kc * 128:(kc + 1) * 128], ident.ap, [hh.b, ident.b], pbb)
            k.copy("act", v3(h1T.ap, 8)[:, :, i * 128:(i + 1) * 128], v3(pv, 8), pbb, [h1T.b])
    S.barrier()
    k.AF.release(mF)
    k.AB.release(mB)


def phase_A1(k, c, dr, h1T):
    S = k.S
    mF, mB = k.AF.mark(), k.AB.mark()
    h3 = v3(h1T.ap, 8)
    cst = [k.f32(512, "cosT%d" % i) for i in range(2)]
    snt = [k.f32(512, "sinT%d" % i) for i in range(2)]
    lv = k.f32(256, "lv")
    for j, nm in enumerate(("diff_lam_q1", "diff_lam_k1", "diff_lam_q2", "diff_lam_k2")):
        bcast_load(k, lv.ap[:, j * 64:(j + 1) * 64], lv.b, dr[nm])
    sc = k.f32(16, "scA1")
    junkf = k.f32(64, "junkf")
    k.stt("dve", junkf.ap, lv.ap[:, 0:64], 1.0, lv.ap[:, 64:128], ALU.mult, ALU.mult, [lv.b], [junkf.b, sc.b],
          accum=sc.ap[:, 0:1])
    k.stt("dve", junkf.ap, lv.ap[:, 128:192], 1.0, lv.ap[:, 192:256], ALU.mult, ALU.mult, [lv.b], [junkf.b, sc.b],
          accum=sc.ap[:, 1:2])
    k.act(sc.ap[:, 2:4], sc.ap[:, 0:2], AF.Exp, [sc.b], [sc.b])
    k.tt("dve", sc.ap[:, 4:5], sc.ap[:, 3:4], sc.ap[:, 2:3], ALU.subtract, [sc.b], [sc.b])
    k.ts("dve", sc.ap[:, 5:6], sc.ap[:, 4:5], -LAM_INIT, ALU.add, [sc.b], [sc.b])
    S.dma("sp", sc.ap[:, 6:7], dr["diff_norm_g"].rearrange("o v -> v o"), W=[sc.b])
    k.ts("dve", sc.ap[:, 7:8], sc.ap[:, 6:7], 1.0 - LAM_INIT, ALU.mult, [sc.b], [sc.b])
    neglam = sc.ap[:, 5:6]
    gcol = sc.ap[:, 7:8]

    stg = [k.f32(4 * 512, "stgA%d" % i) for i in range(2)]
    wh = [k.b16(8 * 512, "wh%d" % i) for i in range(2)]
    QT = k.b16(T, "QT")
    KT = k.b16(T, "KT")
    V = k.b16(T, "V")
    sgT = k.b16(T, "sgT")
    qraw = [k.b16(512, "qraw%d" % i) for i in range(2)]
    t1 = [k.f32(512, "t1_%d" % i) for i in range(2)]
    t2 = [k.f32(512, "t2_%d" % i) for i in range(2)]
    PT = [k.b16(1024, "PT%d" % i) for i in range(3)]
    rlb = k.b16(512, "rlb")
    lnl = k.f32(1024, "lnl")
    o01 = k.f32(1024, "o01")
    of = k.f32(512, "of")
    sq = k.b16(512, "sq")
    lnt = k.f32(512, "lnt")
    rst = k.f32(512, "rst")
    tf = k.f32(512, "tf")
    ogh = [k.b16(512, "ogh%d" % i) for i in range(2)]
    ones, rperm, maskT2, eps = c["ones"], c["rperm"], c["maskT2"], c["eps"]
    ident, negm = c["ident"], c["negm"]
    pO, pOb = k.banks(4, 2)
    pL, pLb = k.banks(6, 1)
    pX, pXb = k.banks(7, 1)
    selb = c["selb"]
    nq = 0
    npt = 0
    nog = 0

    def dma_w(h):
        for half in range(2):
            st = stg[half]
            s3 = v3(st.ap, 4)
            for blk in range(4):
                S.dma("sp", s3[:, :, blk * 128:(blk + 1) * 128],
                      dr["diff_w_in"][half * 512:(half + 1) * 512, blk * BR + h * 128:blk * BR + (h + 1) * 128]
                      .rearrange("(k p) n -> p k n", p=128), W=[st.b])

    def cast_w(h):
        for half in range(2):
            st = stg[half]
            k.copy("dve", wh[h % 2].ap[:, half * 2048:(half + 1) * 2048], st.ap, [st.b], [wh[h % 2].b])

    ntab = [0]

    def load_tab(tt_):
        j = tt_ % 2
        S.dma("sp", cst[j].ap, dr["cosT"][:, tt_ * 512:(tt_ + 1) * 512], W=[cst[j].b])
        S.dma("sp", snt[j].ap, dr["sinT"][:, tt_ * 512:(tt_ + 1) * 512], W=[snt[j].b])

    dma_w(0)
    cast_w(0)
    load_tab(0)
    load_tab(1)
    pend = []
    for h in range(DBG["heads"]):
        w3 = v3(wh[h % 2].ap, 8)
        wb = wh[h % 2].b
        for tt_ in range(8 if DBG["inproj"] else 0):
            ts_ = slice(tt_ * 512, (tt_ + 1) * 512)
            cosT, sinT = cst[tt_ % 2], snt[tt_ % 2]
            if tt_ >= 1:
                for _ in range(3):
                    if pend:
                        _, fn, q_, h_ = pend.pop(0)
                        fn(q_, h_)
            pqs = []
            for blk in (0, 1):
                pq, pqb = k.nextbank(1, 0, 7)
                for kc in range(8):
                    k.mm(pq, w3[:, kc, blk * 128:(blk + 1) * 128], h3[:, kc, ts_], kc == 0, kc == 7, [wb, h1T.b], pqb)
                pqs.append((pq, pqb))
            pg, pgb = k.nextbank(1, 0, 7)
            for kc in range(8):
                k.mm(pg, w3[:, kc, 384:512], h3[:, kc, ts_], kc == 0, kc == 7, [wb, h1T.b], pgb)
            pv, pvb = k.nextbank(1, 0, 7)
            for s_ in range(4):
                for kc in range(8):
                    k.mm(pv[:, s_ * 128:(s_ + 1) * 128], h3[:, kc, tt_ * 512 + s_ * 128:tt_ * 512 + (s_ + 1) * 128],
                         w3[:, kc, 256:384], kc == 0, kc == 7, [wb, h1T.b], pvb)
            rots = []
            for blk, dst in ((0, QT), (1, KT)):
                pq, pqb = pqs[blk]
                qr = qraw[nq % 2]
                a1, a2 = t1[nq % 2], t2[nq % 2]
                nq += 1
                k.copy("act", qr.ap, pq, pqb, [qr.b])
                rots.append((pq, pqb, qr, a1, a2, dst))
            k.act(sgT.ap[:, ts_], pg, AF.Silu, pgb, [sgT.b])
            k.copy("act", V.ap[:, ts_], pv, pvb, [V.b])
            for (pq, pqb, qr, a1, a2, dst) in rots:
                pr, prb = k.nextbank(1, 0, 7)
                k.mm(pr, rperm.ap, qr.ap, True, True, [rperm.b, qr.b], prb)
                k.tt("dve", a1.ap, pq, cosT.ap, ALU.mult, pqb + [cosT.b], [a1.b])
                k.tt("dve", a2.ap, pr, sinT.ap, ALU.mult, prb + [sinT.b], [a2.b])
                k.tt("pool", dst.ap[:, ts_], a1.ap, a2.ap, ALU.add, [a1.b, a2.b], [dst.b])
            if tt_ + 2 < 8:
                load_tab(tt_ + 2)
        if h + 1 < 16:
            dma_w(h + 1)
        blocks = [(qt, kb) for qt in range(DBG["nqt"] if DBG["att"] else 0) for kb in range(4 * qt + 4)]

        def qk_exp(qt, kb):
            nonlocal npt
            j = kb - 4 * qt
            c0 = max(j, 0) * 128
            ps2, ps2b = k.banks(2 * (npt % 2), 2)
            pt_ = PT[npt % 3]
            npt += 1
            for cc in range(2):
                k.mm(ps2[:, cc * 512 + c0:(cc + 1) * 512], KT.ap[cc * 64:(cc + 1) * 64, kb * 128:(kb + 1) * 128],
                     QT.ap[cc * 64:(cc + 1) * 64, qt * 512 + c0:(qt + 1) * 512], True, j < 0, [KT.b, QT.b],
                     [ps2b[cc]])
            if j >= 0:
                for cc in range(2):
                    k.mm(ps2[:, cc * 512 + c0:cc * 512 + c0 + 128], ident.ap, negm.ap, False, True,
                         [ident.b, negm.b], [ps2b[cc]])
            p3 = v3(pt_.ap, 2)
            k.act(p3[:, :, c0:512], v3(ps2, 2)[:, :, c0:512], AF.Exp, ps2b, [pt_.b], scale=0.125)
            return pt_, c0

        def av(qt, kb, pt_, c0):
            nkb = 4 * qt + 4
            for cc in range(2):
                k.mmt(pL[32 * cc:32 * cc + 32, c0:512], ones.ap[:, 0:32], pt_.ap[:, cc * 512 + c0:(cc + 1) * 512],
                      kb == 0, kb == nkb - 1, (0, 32 * cc), [ones.b, pt_.b], pLb)
            for cc in range(2):
                k.mm(pO[:, cc * 512 + c0:(cc + 1) * 512], V.ap[:, kb * 128:(kb + 1) * 128],
                     pt_.ap[:, cc * 512 + c0:(cc + 1) * 512], kb == 0, kb == nkb - 1, [V.b, pt_.b], [pOb[cc]])

        def epi1(qt, hd):
            k.act(lnl.ap[0:64, 0:512], pL[0:64, :], AF.Ln, pLb, [lnl.b])
            k.copy("dve", o01.ap[:, 0:512], pO[:, 0:512], [pOb[0]], [o01.b])
            k.copy("dve", o01.ap[:, 512:1024], pO[:, 512:1024], [pOb[1]], [o01.b])
            k.act(rlb.ap[0:64, 0:512], lnl.ap[0:64, 0:512], AF.Exp, [lnl.b], [rlb.b], scale=-1.0)

        def st_b0(qt, hd):
            k.mm(pX, selb.ap[0:64, 0:128], rlb.ap[0:64, 0:512], True, True, [selb.b, rlb.b], pXb)

        def st_m0(qt, hd):
            k.tt("dve", o01.ap[:, 0:512], o01.ap[:, 0:512], pX, ALU.mult, [o01.b] + pXb, [o01.b])

        def st_b1(qt, hd):
            k.mm(pX, selb.ap[0:64, 128:256], rlb.ap[0:64, 0:512], True, True, [selb.b, rlb.b], pXb)

        def st_m1(qt, hd):
            k.tt("dve", o01.ap[:, 512:1024], o01.ap[:, 512:1024], pX, ALU.mult, [o01.b] + pXb, [o01.b])
            k.stt("dve", of.ap, o01.ap[:, 512:1024], neglam, o01.ap[:, 0:512], ALU.mult, ALU.add, [o01.b, sc.b], [of.b])

        def st_sq(qt, hd):
            k.act(sq.ap, of.ap, AF.Square, [of.b], [sq.b])

        def st_ss(qt, hd):
            k.mm(pX, ones.ap, sq.ap, True, True, [ones.b, sq.b], pXb)

        def st_ln(qt, hd):
            k.act(lnt.ap, pX, AF.Ln, pXb + [eps.b], [lnt.b], scale=1.0 / 128, bias=eps.ap[:, 0:1])

        def st_ex(qt, hd):
            k.act(rst.ap, lnt.ap, AF.Exp, [lnt.b], [rst.b], scale=-0.5)

        def st_fin(qt, hd):
            nonlocal nog
            qs = slice(qt * 512, (qt + 1) * 512)
            k.tt("dve", tf.ap, of.ap, rst.ap, ALU.mult, [of.b, rst.b], [tf.b])
            og_ = ogh[nog % 2]
            nog += 1
            k.stt("dve", og_.ap, tf.ap, gcol, sgT.ap[:, qs], ALU.mult, ALU.mult, [tf.b, sc.b, sgT.b], [og_.b])
            S.dma("sp", dr["ogT"][qt * 4:(qt + 1) * 4, :, hd * 128:(hd + 1) * 128].rearrange("s p v -> p s v"),
                  v3(og_.ap, 4), R=[og_.b])

        STAGES0 = ((2, st_b0), (3, st_m0), (4, st_b1), (5, st_m1), (6, st_sq), (7, st_ss), (8, st_ln), (8, st_ex),
                   (8, st_fin))
        STAGES = ((2, st_b0), (3, st_m0), (4, st_b1), (5, st_m1), (6, st_sq), (7, st_ss), (8, st_ln), (9, st_ex),
                  (10, st_fin))

        cur = qk_exp(*blocks[0]) if blocks else None
        for bi, (qt, kb) in enumerate(blocks):
            nxt = qk_exp(*blocks[bi + 1]) if bi + 1 < len(blocks) else None
            av(qt, kb, *cur)
            while pend and bi >= pend[0][0]:
                _, fn, q_, h_ = pend.pop(0)
                fn(q_, h_)
            if kb == 4 * qt + 3 and DBG["epi"]:
                epi1(qt, h)
                for off, fn in (STAGES0 if qt == 0 else STAGES):
                    pend.append((bi + off, fn, qt, h))
            if bi == 10 and h + 1 < 16:
                cast_w(h + 1)
            if bi == 20 and h + 1 < 16:
                load_tab(0)
                load_tab(1)
            cur = nxt
        pend = [(-1, fn, q_, h_) for (_, fn, q_, h_) in pend]
    for _, fn, q_, h_ in pend:
        fn(q_, h_)
    S.barrier()
    k.AF.release(mF)
    k.AB.release(mB)


NAF = 16 * 1024
NAB = 68 * 1024

_INPUT_NAMES = ("x", "pre_g", "post_g", "gla_w_in", "gla_w_g2", "gla_b_g", "gla_norm_g", "gla_w_out",
                "diff_w_in", "diff_lam_q1", "diff_lam_k1", "diff_lam_q2", "diff_lam_k2", "diff_norm_g", "diff_w_out")
_SHAPES = {
    "x": [T, D], "pre_g": [2, D], "post_g": [2, D], "gla_w_in": [D, GIN], "gla_w_g2": [16, 512],
    "gla_b_g": [1, 512], "gla_norm_g": [1, 512], "gla_w_out": [BR, D], "diff_w_in": [D, 4 * BR],
    "diff_lam_q1": [1, 64], "diff_lam_k1": [1, 64], "diff_lam_q2": [1, 64], "diff_lam_k2": [1, 64],
    "diff_norm_g": [1, 128], "diff_w_out": [BR, D],
}


def build(mode="both"):
    nc = bass.Bass("TRN2", target_bir_lowering=False)
    dr = {}
    for nm in _INPUT_NAMES:
        if mode == "L1" and nm == "x":
            continue
        dr[nm] = nc.dram_tensor(nm, _SHAPES[nm], F32, kind="ExternalInput").ap()
    dr["cst_bf"] = nc.dram_tensor("cst_bf", [128, 1024], BF16, kind="ExternalInput").ap()
    dr["cst_f"] = nc.dram_tensor("cst_f", [128, 1024], F32, kind="ExternalInput").ap()
    dr["cosT"] = nc.dram_tensor("cosT", [128, T], F32, kind="ExternalInput").ap()
    dr["sinT"] = nc.dram_tensor("sinT", [128, T], F32, kind="ExternalInput").ap()
    dr["ogT"] = nc.dram_tensor("ogT_scr", [NT, 128, BR], BF16, kind="Internal").ap()
    if mode == "both":
        dr["x1"] = nc.dram_tensor("x1_scr", [T, D], F32, kind="Internal").ap()
        dr["out"] = nc.dram_tensor("out", [T, D], F32, kind="ExternalOutput").ap()
    elif mode == "L0":
        dr["x1"] = nc.dram_tensor("x1", [T, D], F32, kind="ExternalOutput").ap()
        dr["h1T_d"] = nc.dram_tensor("h1T", [128, 8 * T], BF16, kind="ExternalOutput").ap()
    else:
        dr["x1"] = nc.dram_tensor("x1", [T, D], F32, kind="ExternalInput").ap()
        dr["h1T_d"] = nc.dram_tensor("h1T", [128, 8 * T], BF16, kind="ExternalInput").ap()
        dr["out"] = nc.dram_tensor("out", [T, D], F32, kind="ExternalOutput").ap()

    with ExitStack() as st:
        S = Sched(nc, st)
        af = st.enter_context(nc.sbuf_tensor("arena_f", [128, NAF], F32))
        ab = st.enter_context(nc.sbuf_tensor("arena_b", [128, NAB], BF16))
        ps = st.enter_context(nc.psum_tensor("ps", [128, 4096], F32))
        k = K(nc, S, Arena(af, NAF), Arena(ab, NAB), ps)
        c = load_consts(k, dr)
        if mode in ("both", "L0"):
            phase_A0(k, c, dr)
        h1T = k.b16(8 * T, "h1T")
        if mode in ("both", "L0"):
            phase_F(k, c, dr, dr["gla_w_out"], dr["post_g"][0:1, :], dr["x"], dr["x1"],
                    pre_row=dr["pre_g"][1:2, :], h1T=h1T)
        if mode == "L0":
            for j in range(8):
                S.dma("sp", dr["h1T_d"][:, j * T:(j + 1) * T], h1T.ap[:, j * T:(j + 1) * T], R=[h1T.b])
            S.barrier()
        if mode == "L1":
            for j in range(8):
                S.dma("sp", h1T.ap[:, j * T:(j + 1) * T], dr["h1T_d"][:, j * T:(j + 1) * T], W=[h1T.b])
        if mode in ("both", "L1"):
            phase_A1(k, c, dr, h1T)
            if DBG["F1"]:
                phase_F(k, c, dr, dr["diff_w_out"], dr["post_g"][1:2, :], dr["x1"], dr["out"])
        S.barrier()
        print("ops", S.nops, "arena peaks f32 %d bf16 %d" % (k.AF.peak, k.AB.peak), "sems", sum(1 for b in S.allbufs if b.dsem is not None))
        with nc.Block() as block:
            S.replay(block)
    return nc


def _consts():
    bf = ml_dtypes.bfloat16
    p = np.arange(128)
    ident = np.eye(128, dtype=np.float32)
    perm = np.where((p % 64) < 32, p + 32, p - 32)
    rperm = np.zeros((128, 128), np.float32)
    rperm[perm, p] = 1.0
    ones = np.ones((128, 128), np.float32)
    maskT = (p[None, :] >= p[:, None]).astype(np.float32)
    negm = np.where(p[None, :] >= p[:, None], 0.0, -30000.0).astype(np.float32)
    selb = np.zeros((128, 256), np.float32)
    selb[0, 0:128] = 1.0
    selb[32, 128:256] = 1.0
    cst_bf = np.concatenate([ident, rperm, ones, maskT, maskT, negm, selb], 1).astype(bf)
    triU = np.where(p[:, None] <= p[None, :], -1.0 / 16.0, 0.0).astype(np.float32)
    triL = np.where(p[:, None] > p[None, :], -1.0 / 16.0, 0.0).astype(np.float32)
    maskU = (p[:, None] <= p[None, :]).astype(np.float32)
    sel = np.zeros((128, 256), np.float32)
    sel[0, 0:128] = 1.0
    sel[32, 128:256] = 1.0
    cst_f = np.concatenate([triU, triL, maskU, maskU, maskU, maskU, sel], 1).astype(np.float32)
    inv_freq = (1.0 / (np.float32(10000.0) ** (np.arange(0, 64, 2, dtype=np.float32) / np.float32(64)))).astype(np.float32)
    pos = np.arange(T, dtype=np.float32)
    ang = (pos[:, None] * inv_freq[None, :]).astype(np.float32)
    cos = np.cos(ang).astype(np.float32).T
    sin = np.sin(ang).astype(np.float32).T
    cosT = np.concatenate([cos, cos, cos, cos], 0)
    sinT = np.concatenate([-sin, sin, -sin, sin], 0)
    return {"cst_bf": np.ascontiguousarray(cst_bf), "cst_f": np.ascontiguousarray(cst_f),
            "cosT": np.ascontiguousarray(cosT), "sinT": np.ascontiguousarray(sinT)}


_NC_CACHE = {}


def _get_nc(mode):
    if mode not in _NC_CACHE:
        _NC_CACHE[mode] = build(mode)
    return _NC_CACHE[mode]


def _shared_maps(inputs):
    m = dict(_consts())
    for nm in _INPUT_NAMES:
        if nm == "x":
            continue
        a = np.asarray(inputs[nm], dtype=np.float32)
        if nm in ("pre_g", "post_g"):
            m[nm] = np.ascontiguousarray(a)
        else:
            m[nm] = np.ascontiguousarray(a.reshape(_SHAPES[nm]))
    return m


FUSED = True


def kernel(**inputs):
    x = np.asarray(inputs["x"], dtype=np.float32)
    shared = _shared_maps(inputs)
    cores = list(range(NCORES))
    if FUSED:
        nc = _get_nc("both")
        in_maps = [dict(shared, x=np.ascontiguousarray(x[b])) for b in cores]
        res = run_bass_kernel_spmd(nc, in_maps, core_ids=cores)
        return np.stack([np.asarray(r["out"]) for r in res.results], 0).astype(np.float32)
    nc0 = _get_nc("L0")
    in_maps = [dict(shared, x=np.ascontiguousarray(x[b])) for b in cores]
    r0 = run_bass_kernel_spmd(nc0, in_maps, core_ids=cores).results
    nc1 = _get_nc("L1")
    in_maps = [dict(shared, x1=np.asarray(r0[b]["x1"]), h1T=np.asarray(r0[b]["h1T"])) for b in cores]
    r1 = run_bass_kernel_spmd(nc1, in_maps, core_ids=cores).results
    return np.stack([np.asarray(r["out"]) for r in r1], 0).astype(np.float32)
```

```python
import math
from contextlib import ExitStack

import numpy as np
import ml_dtypes
import concourse.bass as bass
import concourse.mybir as mybir
from concourse.bass_utils import run_bass_kernel_spmd

F32 = mybir.dt.float32
BF16 = mybir.dt.bfloat16
AF = mybir.ActivationFunctionType
ALU = mybir.AluOpType

T = 4096
D = 1024
BR = 2048
NT = T // 128
GIN = 5136
EPS = 1e-6
LAM_INIT = 0.8 - 0.6 * math.exp(-0.3 * 1)
NCORES = 8
DBG = {"heads": 16, "nqt": 8, "inproj": True, "att": True, "epi": True, "F1": True, "nkb": 99, "ipp": "qgv", "rope": True}

ENGS = ("pe", "act", "dve", "pool", "sp")


class Buf:
    __slots__ = ("name", "w", "r", "dsem", "dcnt", "excl")

    def __init__(self, name):
        self.name = name
        self.excl = False
        self.w = None
        self.r = []
        self.dsem = None
        self.dcnt = 0


class Rec:
    __slots__ = ("waits", "fn", "inc", "dma_inc", "pos", "val")

    def __init__(self, waits, fn):
        self.waits = waits
        self.fn = fn
        self.inc = False
        self.dma_inc = None
        self.pos = -1
        self.val = 0


class Sched:
    def __init__(self, nc, stack):
        self.nc = nc
        self.stack = stack
        self.ops = {e: [] for e in ENGS}
        self.sem = {e: stack.enter_context(nc.semaphore("sem_" + e)) for e in ENGS}
        self.waited = {e: {} for e in ENGS}
        self.allbufs = []
        self.same_engine_sync = True
        self.same_engine_all = True
        self.nops = 0
        self.nbar = 0

    def buf(self, name):
        b = Buf("%s_%d" % (name, len(self.allbufs)))
        self.allbufs.append(b)
        return b

    def _dsem(self, b):
        if b.dsem is None:
            b.dsem = self.stack.enter_context(self.nc.semaphore("d_" + b.name))
        return b.dsem

    def _push(self, eng, rec):
        rec.pos = len(self.ops[eng])
        self.ops[eng].append(rec)

    def _resolve(self, eng, tok, waits, is_raw):
        if tok[0] == "e":
            _, e2, rec = tok
            if e2 == eng:
                if eng in ("pe", "sp") or not (self.same_engine_sync and (is_raw or self.same_engine_all)):
                    return
            key = "e_" + e2
            if self.waited[eng].get(key, -1) >= rec.pos:
                return
            self.waited[eng][key] = rec.pos
            rec.inc = True
            waits.append(("e", e2, rec))
        else:
            _, b, val = tok
            key = "d_" + b.name
            if self.waited[eng].get(key, 0) >= val:
                return
            self.waited[eng][key] = val
            waits.append(("d", b.dsem, val))

    def _deps(self, eng, R, W):
        waits = []
        for b in R:
            if b.w is not None:
                self._resolve(eng, b.w, waits, True)
            if b.excl:
                for t in b.r:
                    if t[0] != "e" or t[1] != eng:
                        self._resolve(eng, t, waits, False)
        for b in W:
            if b.w is not None:
                self._resolve(eng, b.w, waits, False)
            for t in b.r:
                self._resolve(eng, t, waits, False)
        return waits

    def op(self, eng, fn, R=(), W=()):
        waits = self._deps(eng, R, W)
        rec = Rec(waits, fn)
        self._push(eng, rec)
        tok = ("e", eng, rec)
        for b in R:
            b.r = [t for t in b.r if not (t[0] == "e" and t[1] == eng)]
            b.r.append(tok)
        for b in W:
            b.w = tok
            b.r = []
        self.nops += 1
        return rec

    def dma(self, q, out_ap, in_ap, R=(), W=()):
        waits = self._deps(q, R, W)
        owner = W[0] if W else R[0]
        sem = self._dsem(owner)
        if not W and owner.dcnt > 0:
            self._resolve(q, ("d", owner, owner.dcnt), waits, False)
        owner.dcnt += 16
        tok = ("d", owner, owner.dcnt)
        rec = Rec(waits, lambda e: e.dma_start(out=out_ap, in_=in_ap))
        rec.dma_inc = sem
        self._push(q, rec)
        for b in R:
            b.r.append(tok)
        for b in W:
            b.w = tok
            b.r = []
        self.nops += 1
        return rec

    def barrier(self):
        waits = []
        comp = ("pe", "act", "dve", "pool")
        last = {}
        for e in comp:
            recs = [r for r in self.ops[e] if r.fn is not None]
            if recs:
                last[e] = recs[-1]
                if self.waited["sp"].get("e_" + e, -1) < recs[-1].pos:
                    self.waited["sp"]["e_" + e] = recs[-1].pos
                    recs[-1].inc = True
                    waits.append(("e", e, recs[-1]))
        for b in self.allbufs:
            if b.dsem is not None and b.dcnt > self.waited["sp"].get("d_" + b.name, 0):
                self.waited["sp"]["d_" + b.name] = b.dcnt
                waits.append(("d", b.dsem, b.dcnt))
        sem_sp = self.sem["sp"]
        rec = Rec(waits, lambda e: e.sem_inc(sem_sp, 1))
        self._push("sp", rec)
        self.nbar += 1
        v = self.nbar
        for e in comp:
            self._push(e, Rec([("s", sem_sp, v)], None))
            for e2 in comp:
                if e2 in last:
                    self.waited[e]["e_" + e2] = last[e2].pos
            for b in self.allbufs:
                if b.dsem is not None:
                    self.waited[e]["d_" + b.name] = b.dcnt
        for b in self.allbufs:
            b.w = None
            b.r = []

    def replay(self, block):
        S = self
        for eng in ENGS:
            n = 0
            for rec in S.ops[eng]:
                if rec.inc:
                    n += 1
                    rec.val = n
        self.nsig = {eng: sum(1 for r in S.ops[eng] if r.inc) for eng in ENGS}

        def run(eng, e):
            for rec in S.ops[eng]:
                for w in rec.waits:
                    if w[0] == "e":
                        assert w[2].inc and w[2].val > 0
                        e.wait_ge(S.sem[w[1]], w[2].val)
                    else:
                        e.wait_ge(w[1], w[2])
                if rec.fn is None:
                    continue
                ins = rec.fn(e)
                if rec.dma_inc is not None:
                    ins.then_inc(rec.dma_inc, 16)
                elif rec.inc:
                    ins.then_inc(S.sem[eng], 1)

        @block.tensor
        def _(e):
            run("pe", e)

        @block.scalar
        def _(e):
            run("act", e)

        @block.vector
        def _(e):
            run("dve", e)

        @block.gpsimd
        def _(e):
            run("pool", e)

        @block.sync
        def _(e):
            run("sp", e)


class Arena:
    def __init__(self, ap, n):
        self.ap = ap
        self.n = n
        self.top = 0
        self.peak = 0

    def alloc(self, n):
        a = self.ap[:, self.top:self.top + n]
        self.top += (n + 15) // 16 * 16
        self.peak = max(self.peak, self.top)
        assert self.top <= self.n, ("arena overflow", self.top, self.n)
        return a

    def mark(self):
        return self.top

    def release(self, m):
        self.top = m


class Tile:
    __slots__ = ("ap", "b")

    def __init__(self, ap, b):
        self.ap = ap
        self.b = b


class K:
    def __init__(self, nc, S, AFa, ABa, ps):
        self.nc, self.S, self.AF, self.AB, self.ps = nc, S, AFa, ABa, ps
        self.bankb = [S.buf("bank%d" % i) for i in range(8)]
        for b in self.bankb:
            b.excl = True
        self.rr = 0
        self.cast_rr = 0

    def f32(self, n, name):
        return Tile(self.AF.alloc(n), self.S.buf(name))

    def b16(self, n, name):
        return Tile(self.AB.alloc(n), self.S.buf(name))

    def banks(self, i, n=1):
        return self.ps[:, i * 512:(i + n) * 512], [self.bankb[j] for j in range(i, i + n)]

    def nextbank(self, n=1, lo=0, hi=8):
        if self.rr < lo or self.rr + n > hi:
            self.rr = lo
        i = self.rr
        self.rr += n
        return self.banks(i, n)

    def mm(self, out, lhsT, rhs, start, stop, R, W):
        self.S.op("pe", lambda e: e.matmul(out, lhsT=lhsT, rhs=rhs, start=start, stop=stop), R, W)

    def mmt(self, out, lhsT, rhs, start, stop, tpos, R, W):
        self.S.op("pe", lambda e: e.matmul(out, lhsT=lhsT, rhs=rhs, start=start, stop=stop, tile_position=tpos), R, W)

    def tr(self, out, in_, ident, R, W):
        self.S.op("pe", lambda e: e.transpose(out=out, in_=in_, identity=ident), R, W)

    def act(self, out, in_, func, R, W, scale=None, bias=None, accum=None):
        kw = {}
        if scale is not None:
            kw["scale"] = scale
        if bias is not None:
            kw["bias"] = bias
        if accum is not None:
            kw["accum_out"] = accum
        self.S.op("act", lambda e: e.activation(out=out, in_=in_, func=func, **kw), R, W)

    def tt(self, eng, out, a, b, op, R, W):
        self.S.op(eng, lambda e: e.tensor_tensor(out=out, in0=a, in1=b, op=op), R, W)

    def stt(self, eng, out, a, sc, b, op0, op1, R, W, accum=None):
        if accum is None:
            self.S.op(eng, lambda e: e.scalar_tensor_tensor(out=out, in0=a, scalar=sc, in1=b, op0=op0, op1=op1), R, W)
        else:
            self.S.op(eng, lambda e: e.scalar_tensor_tensor(out=out, in0=a, scalar=sc, in1=b, op0=op0, op1=op1,
                                                           accum_out=accum), R, W)

    def ts(self, eng, out, a, s1, op0, R, W, s2=None, op1=None):
        if s2 is None:
            self.S.op(eng, lambda e: e.tensor_scalar(out=out, in0=a, scalar1=s1, scalar2=None, op0=op0), R, W)
        else:
            self.S.op(eng, lambda e: e.tensor_scalar(out=out, in0=a, scalar1=s1, scalar2=s2, op0=op0, op1=op1), R, W)

    def copy(self, eng, out, in_, R, W):
        if eng == "act":
            self.S.op("act", lambda e: e.copy(out=out, in_=in_), R, W)
        else:
            self.S.op(eng, lambda e: e.tensor_copy(out=out, in_=in_), R, W)

    def cast_any(self, out, in_, R, W):
        eng = ("pool", "dve", "act")[self.cast_rr % 3]
        self.cast_rr += 1
        self.copy(eng, out, in_, R, W)

    def memset(self, eng, ap, val, W):
        self.S.op(eng, lambda e: e.memset(ap, val), (), W)

    def rstd(self, ssq, tmp, out, n, eps_ap, R, W):
        self.act(tmp, ssq, AF.Ln, R, W, scale=1.0 / n, bias=eps_ap)
        self.act(out, tmp, AF.Exp, W, W, scale=-0.5)


def v3(ap, a):
    return ap.rearrange("p (a b) -> p a b", a=a)


def load_consts(k, dr):
    S = k.S
    c = {}
    c["ident"] = k.b16(128, "ident")
    c["rperm"] = k.b16(128, "rperm")
    c["ones"] = k.b16(128, "ones")
    c["maskT2"] = k.b16(256, "maskT2")
    c["negm"] = k.b16(128, "negm")
    c["selb"] = k.b16(256, "selb")
    for i, nm in enumerate(("ident", "rperm", "ones")):
        S.dma("sp", c[nm].ap, dr["cst_bf"][:, i * 128:(i + 1) * 128], W=[c[nm].b])
    S.dma("sp", c["maskT2"].ap, dr["cst_bf"][:, 384:640], W=[c["maskT2"].b])
    S.dma("sp", c["negm"].ap, dr["cst_bf"][:, 640:768], W=[c["negm"].b])
    S.dma("sp", c["selb"].ap, dr["cst_bf"][:, 768:1024], W=[c["selb"].b])
    c["triU"] = k.f32(128, "triU")
    c["triL"] = k.f32(128, "triL")
    c["maskU4"] = k.f32(512, "maskU4")
    c["sel"] = k.f32(256, "sel")
    S.dma("sp", c["triU"].ap, dr["cst_f"][:, 0:128], W=[c["triU"].b])
    S.dma("sp", c["triL"].ap, dr["cst_f"][:, 128:256], W=[c["triL"].b])
    S.dma("sp", c["maskU4"].ap, dr["cst_f"][:, 256:768], W=[c["maskU4"].b])
    S.dma("sp", c["sel"].ap, dr["cst_f"][:, 768:1024], W=[c["sel"].b])
    c["eps"] = k.f32(16, "eps")
    k.memset("pool", c["eps"].ap[:, 0:1], EPS, [c["eps"].b])
    c["small"] = k.f32(64, "small")
    c["junk"] = k.b16(2048, "junk")
    return c


def bcast_load(k, tile_ap, b, dram_row):
    k.S.dma("sp", tile_ap, dram_row.partition_broadcast(128), W=[b])


def phase_A0(k, c, dr):
    S = k.S
    mF, mB = k.AF.mark(), k.AB.mark()
    w_in = k.b16(8 * GIN, "w_in")
    w3 = v3(w_in.ap, 8)
    m2 = k.AF.mark()
    HW = GIN // 2
    stg = [k.f32(HW, "stg%d" % i) for i in range(4)]
    n = 0
    for kc in range(8):
        for half in range(2):
            st = stg[n % 4]
            n += 1
            S.dma("sp", st.ap, dr["gla_w_in"][kc * 128:(kc + 1) * 128, half * HW:(half + 1) * HW], W=[st.b])
            k.cast_any(w3[:, kc, half * HW:(half + 1) * HW], st.ap, [st.b], [w_in.b])
    S.barrier()
    k.AF.release(m2)
    pre0 = k.f32(D, "pre0")
    bcast_load(k, pre0.ap, pre0.b, dr["pre_g"][0:1, :])
    ngf = k.f32(BR, "ngf")
    for h in range(4):
        bcast_load(k, ngf.ap[:, h * 512:(h + 1) * 512], ngf.b, dr["gla_norm_g"][0:1, :])
    wg2 = k.f32(512, "wg2")
    S.dma("sp", wg2.ap[0:16, :], dr["gla_w_g2"], W=[wg2.b])
    S.dma("sp", wg2.ap[16:17, :], dr["gla_b_g"], W=[wg2.b])
    lowT = [k.f32(128, "lowT%d" % i) for i in range(2)]
    for t in lowT:
        k.memset("pool", t.ap[0:32, :], 1.0, [t.b])
    xs = [k.f32(D, "xs%d" % i) for i in range(2)]
    ez = k.f32(512, "ez")
    sp = k.f32(512, "sp")
    ebT = k.f32(512, "ebT")
    enT = k.f32(512, "enT")
    erev = k.f32(512, "erev")
    Sf = [k.f32(512, "Sf%d" % h) for h in range(4)]
    Sb = [k.b16(512, "Sb%d" % h) for h in range(4)]
    for h in range(4):
        k.memset("pool", Sf[h].ap, 0.0, [Sf[h].b])
        k.memset("pool", Sb[h].ap, 0.0, [Sb[h].b])
    sm = [k.f32(16, "sm%d" % i) for i in range(2)]
    hb = [k.b16(D, "hb%d" % i) for i in range(2)]
    hT = [k.b16(D, "hT%d" % i) for i in range(2)]
    qtT = [k.b16(512, "qtT")] * 2
    ktT = [k.b16(512, "ktT")] * 2
    kend = [k.b16(512, "kend")] * 2
    vb = [k.b16(BR, "vb")] * 2
    sg = k.b16(BR, "sg")
    sgn = [k.b16(BR, "sgn")] * 2
    ATb = [k.b16(512, "ATb")] * 2
    og = [k.b16(BR, "og%d" % i) for i in range(2)]
    ogT = [k.b16(BR, "ogT%d" % i) for i in range(2)]
    ident = c["ident"]
    junk = c["junk"]
    eps = c["eps"]
    SCALE = 128 ** -0.5

    smp = [k.f32(16, "smp%d" % i) for i in range(2)]

    def pro_a(i):
        p = i % 2
        x = xs[p]
        s = smp[p]
        S.dma("sp", x.ap, dr["x"][i * 128:(i + 1) * 128, :], W=[x.b])
        k.act(junk.ap[:, 0:D], x.ap, AF.Square, [x.b], [junk.b, s.b], accum=s.ap[:, 0:1])
        k.rstd(s.ap[:, 0:1], s.ap[:, 1:2], s.ap[:, 2:3], D, eps.ap[:, 0:1], [s.b, eps.b], [s.b])
        h = hb[p]
        k.stt("dve", h.ap, x.ap, s.ap[:, 2:3], pre0.ap, ALU.mult, ALU.mult, [x.b, s.b, pre0.b], [h.b])

    def pro_b(i):
        p = i % 2
        h = hb[p]
        pb, pbb = k.banks(7)
        pv = pb.bitcast(BF16)
        for kc in range(8):
            k.tr(pv[:, kc * 128:(kc + 1) * 128], h.ap[:, kc * 128:(kc + 1) * 128], ident.ap, [h.b, ident.b], pbb)
        ht = hT[p]
        k.copy("act", ht.ap, pv, pbb, [ht.b])

    osb = k.f32(BR, "osb")
    smh = [k.f32(16, "smh%d" % i) for i in range(4)]
    prev_tail = [None]

    def tail(i, o_):
        pt, ptb = k.banks(5, 2)
        ptv = pt.bitcast(BF16)
        for cc in range(16):
            k.tr(ptv[:, cc * 128:(cc + 1) * 128], o_.ap[:, cc * 128:(cc + 1) * 128], ident.ap, [o_.b, ident.b], ptb)
        ot = ogT[i % 2]
        k.copy("act", ot.ap, ptv, ptb, [ot.b])
        S.dma("sp", dr["ogT"][i], ot.ap, R=[ot.b])

    def low(i, bank=5):
        ht_ = hT[i % 2]
        pl, plb = k.banks(bank)
        for kc in range(8):
            k.mm(pl[0:16, 0:128], w3[:, kc, 5120:5136], ht_.ap[:, kc * 128:(kc + 1) * 128], kc == 0, kc == 7,
                 [ht_.b, w_in.b], plb)
        k.copy("act", lowT[i % 2].ap[0:16, :], pl[0:16, 0:128], plb, [lowT[i % 2].b])

    pro_a(0)
    pro_b(0)
    low(0)
    for i in range(NT):
        p = i % 2
        ht = hT[p]
        hk = lambda kc: ht.ap[:, kc * 128:(kc + 1) * 128]
        lt = lowT[p]
        pkt, pktb = k.banks(4)
        for kc in range(8):
            k.mm(pkt, hk(kc), w3[:, kc, 512:1024], kc == 0, kc == 7, [ht.b, w_in.b], pktb)
        pz, pzb = k.banks(5)
        k.mm(pz, lt.ap[0:17, :], wg2.ap[0:17, :], True, True, [lt.b, wg2.b], pzb)
        k.act(ez.ap, pz, AF.Exp, pzb, [ez.b], scale=-1.0)
        k.act(sp.ap, ez.ap, AF.Ln, [ez.b], [sp.b], bias=1.0)
        pv4, pv4b = k.banks(0, 4)
        for cc in range(4):
            for kc in range(8):
                k.mm(pv4[:, cc * 512:(cc + 1) * 512], hk(kc), w3[:, kc, 1024 + cc * 512:1024 + (cc + 1) * 512],
                     kc == 0, kc == 7, [ht.b, w_in.b], [pv4b[cc]])
        v = vb[p]
        k.copy("act", v.ap, pv4, pv4b, [v.b])
        if prev_tail[0] is not None:
            tail(*prev_tail[0])
            prev_tail[0] = None
        pc, pcb = k.banks(5)
        for hh in range(4):
            k.mm(pc[:, hh * 128:(hh + 1) * 128], sp.ap[:, hh * 128:(hh + 1) * 128], c["triU"].ap, True, True,
                 [sp.b, c["triU"].b], pcb)
        pr, prb = k.banks(6)
        k.mm(pr, c["triL"].ap, sp.ap, True, True, [sp.b, c["triL"].b], prb)
        k.act(ebT.ap, pc, AF.Exp, pcb, [ebT.b])
        k.act(enT.ap, pc, AF.Exp, pcb, [enT.b], scale=-1.0)
        k.act(erev.ap, pr, AF.Exp, prb, [erev.b])
        ke = kend[p]
        k.tt("dve", ke.ap, pkt, erev.ap, ALU.mult, pktb + [erev.b], [ke.b])
        if i + 1 < NT:
            pro_a(i + 1)
        pq, pqb = k.banks(5)
        for hh in range(4):
            for kc in range(8):
                k.mm(pq[:, hh * 128:(hh + 1) * 128], w3[:, kc, hh * 128:(hh + 1) * 128], hk(kc), kc == 0, kc == 7,
                     [ht.b, w_in.b], pqb)
        pk, pkb = k.banks(6)
        for hh in range(4):
            for kc in range(8):
                k.mm(pk[:, hh * 128:(hh + 1) * 128], w3[:, kc, 512 + hh * 128:512 + (hh + 1) * 128], hk(kc), kc == 0,
                     kc == 7, [ht.b, w_in.b], pkb)
        qt_, kt_ = qtT[p], ktT[p]
        k.stt("dve", qt_.ap, pq, SCALE, ebT.ap, ALU.mult, ALU.mult, pqb + [ebT.b], [qt_.b])
        k.tt("dve", kt_.ap, pk, enT.ap, ALU.mult, pkb + [enT.b], [kt_.b])
        pg4, pg4b = k.banks(0, 4)
        for cc in range(4):
            for kc in range(8):
                k.mm(pg4[:, cc * 512:(cc + 1) * 512], hk(kc), w3[:, kc, 3072 + cc * 512:3072 + (cc + 1) * 512],
                     kc == 0, kc == 7, [ht.b, w_in.b], [pg4b[cc]])
        if i + 1 < NT:
            pro_b(i + 1)
        k.act(sg.ap, pg4, AF.Silu, pg4b, [sg.b])
        sn = sgn[p]
        k.tt("pool", sn.ap, sg.ap, ngf.ap, ALU.mult, [sg.b, ngf.b], [sn.b])
        pa, pab = k.banks(4)
        for hh in range(4):
            sl = slice(hh * 128, (hh + 1) * 128)
            k.mm(pa[:, sl], kt_.ap[:, sl], qt_.ap[:, sl], True, True, [kt_.b, qt_.b], pab)
        at = ATb[p]
        k.tt("dve", at.ap, pa, c["maskU4"].ap, ALU.mult, pab + [c["maskU4"].b], [at.b])
        if i + 1 < NT:
            low(i + 1, 7)
        pkvs = []
        if i < NT - 1:
            for hh in range(4):
                sl = slice(hh * 128, (hh + 1) * 128)
                vs = slice(hh * 512, (hh + 1) * 512)
                pkv, pkvb = k.banks(4 + hh)
                k.mm(pkv, ke.ap[:, sl], v.ap[:, vs], True, True, [ke.b, v.b], pkvb)
                pkvs.append((pkv, pkvb))
        po, pob = k.banks(0, 4)
        for hh in range(4):
            sl = slice(hh * 128, (hh + 1) * 128)
            vs = slice(hh * 512, (hh + 1) * 512)
            k.mm(po[:, vs], at.ap[:, sl], v.ap[:, vs], True, False, [at.b, v.b], [pob[hh]])
            k.mm(po[:, vs], qt_.ap[:, sl], Sb[hh].ap, False, True, [qt_.b, Sb[hh].b], [pob[hh]])
        for hh, (pkv, pkvb) in enumerate(pkvs):
            k.stt("dve", Sf[hh].ap, Sf[hh].ap, ebT.ap[:, hh * 128 + 127:hh * 128 + 128], pkv, ALU.mult, ALU.add,
                  [Sf[hh].b, ebT.b] + pkvb, [Sf[hh].b])
            k.copy("pool", Sb[hh].ap, Sf[hh].ap, [Sf[hh].b], [Sb[hh].b])
        o_ = og[p]
        for hh in range(4):
            vs = slice(hh * 512, (hh + 1) * 512)
            sh = smh[hh]
            k.act(junk.ap[:, 0:512], po[:, vs], AF.Square, [pob[hh]], [junk.b, sh.b], accum=sh.ap[:, 0:1])
            k.copy("dve", osb.ap[:, vs], po[:, vs], [pob[hh]], [osb.b])
            k.rstd(sh.ap[:, 0:1], sh.ap[:, 1:2], sh.ap[:, 2:3], 512, eps.ap[:, 0:1], [sh.b, eps.b], [sh.b])
        for hh in range(4):
            vs = slice(hh * 512, (hh + 1) * 512)
            k.stt("dve", o_.ap[:, vs], osb.ap[:, vs], smh[hh].ap[:, 2:3], sn.ap[:, vs], ALU.mult, ALU.mult,
                  [osb.b, smh[hh].b, sn.b], [o_.b])
        prev_tail[0] = (i, o_)
    tail(*prev_tail[0])
    S.barrier()
    k.AF.release(mF)
    k.AB.release(mB)


def phase_F(k, c, dr, w_out_d, post_row, xin_d, out_d, pre_row=None, h1T=None):
    S = k.S
    mF, mB = k.AF.mark(), k.AB.mark()
    w_out = k.b16(16 * D, "w_out")
    wo3 = v3(w_out.ap, 16)
    wbuf = [S.buf("w_out_p%d" % j) for j in range(8)]
    stg = [k.f32(2 * D, "stgF%d" % i) for i in range(2)]
    for j in range(8):
        st = stg[j % 2]
        S.dma("sp", v3(st.ap, 2), w_out_d[j * 256:(j + 1) * 256, :].rearrange("(c p) n -> p c n", p=128), W=[st.b])
        k.copy(("dve", "act")[j % 2], w_out.ap[:, j * 2 * D:(j + 1) * 2 * D], st.ap, [st.b], [wbuf[j]])
    postg = k.f32(D, "postg")
    bcast_load(k, postg.ap, postg.b, post_row)
    if h1T is not None:
        preg = k.f32(D, "preg")
        bcast_load(k, preg.ap, preg.b, pre_row)
        h1 = [k.b16(D, "h1_%d" % i) for i in range(2)]
    ogs = [k.b16(BR, "ogs%d" % i) for i in range(4)]
    xs = [k.f32(D, "xsF%d" % i) for i in range(4)]
    tmp = [k.f32(D, "tmpF%d" % i) for i in range(2)]
    x1 = [k.f32(D, "x1F%d" % i) for i in range(2)]
    sm = [k.f32(16, "smF%d" % i) for i in range(2)]
    junk, eps, ident = c["junk"], c["eps"], c["ident"]

    def load(i):
        o = ogs[i % 4]
        S.dma("sp", o.ap, dr["ogT"][i], W=[o.b])
        x = xs[i % 4]
        S.dma("sp", x.ap, xin_d[i * 128:(i + 1) * 128, :], W=[x.b])

    def ymm(i):
        o = ogs[i % 4]
        py, pyb = k.banks(2 * (i % 3), 2)
        for half in range(2):
            for cc in range(16):
                k.mm(py[:, half * 512:(half + 1) * 512], o.ap[:, cc * 128:(cc + 1) * 128],
                     wo3[:, cc, half * 512:(half + 1) * 512], cc == 0, cc == 15, [o.b, wbuf[cc // 2]], [pyb[half]])
        return py, pyb

    load(0)
    load(1)
    load(2)
    yq = [ymm(0), ymm(1)]
    for i in range(NT):
        if i + 3 < NT:
            load(i + 3)
        x, s = xs[i % 4], sm[i % 2]
        py, pyb = yq.pop(0)
        if i + 2 < NT:
            yq.append(ymm(i + 2))
        k.act(junk.ap[:, 0:D], py, AF.Square, pyb, [junk.b, s.b], accum=s.ap[:, 0:1])
        k.rstd(s.ap[:, 0:1], s.ap[:, 1:2], s.ap[:, 2:3], D, eps.ap[:, 0:1], [s.b, eps.b], [s.b])
        t = tmp[i % 2]
        k.stt("dve", t.ap, py, s.ap[:, 2:3], postg.ap, ALU.mult, ALU.mult, pyb + [s.b, postg.b], [t.b])
        xo = x1[i % 2]
        k.tt("pool", xo.ap, t.ap, x.ap, ALU.add, [t.b, x.b], [xo.b])
        S.dma("sp", out_d[i * 128:(i + 1) * 128, :], xo.ap, R=[xo.b])
        if h1T is not None:
            k.act(junk.ap[:, 0:D], xo.ap, AF.Square, [xo.b], [junk.b, s.b], accum=s.ap[:, 4:5])
            k.rstd(s.ap[:, 4:5], s.ap[:, 5:6], s.ap[:, 6:7], D, eps.ap[:, 0:1], [s.b, eps.b], [s.b])
            hh = h1[i % 2]
            k.stt("dve", hh.ap, xo.ap, s.ap[:, 6:7], preg.ap, ALU.mult, ALU.mult, [xo.b, s.b, preg.b], [hh.b])
            pb, pbb = k.banks(6 + (i % 2))
            pv = pb.bitcast(BF16)
            for kc in range(8):
                k.tr(pv[:, kc * 128:(kc + 1) * 128], hh.ap[:, kc * 128:(kc + 1) * 128], ident.ap, [hh.b, ident.b], pbb)
            k.copy("act", v3(h1T.ap, 8)[:, :, i * 128:(i + 1) * 128], v3(pv, 8), pbb, [h1T.b])
    S.barrier()
    k.AF.release(mF)
    k.AB.release(mB)


def phase_A1(k, c, dr, h1T):
    S = k.S
    mF, mB = k.AF.mark(), k.AB.mark()
    h3 = v3(h1T.ap, 8)
    cst = [k.f32(512, "cosT%d" % i) for i in range(2)]
    snt = [k.f32(512, "sinT%d" % i) for i in range(2)]
    lv = k.f32(256, "lv")
    for j, nm in enumerate(("diff_lam_q1", "diff_lam_k1", "diff_lam_q2", "diff_lam_k2")):
        bcast_load(k, lv.ap[:, j * 64:(j + 1) * 64], lv.b, dr[nm])
    sc = k.f32(16, "scA1")
    junkf = k.f32(64, "junkf")
    k.stt("dve", junkf.ap, lv.ap[:, 0:64], 1.0, lv.ap[:, 64:128], ALU.mult, ALU.mult, [lv.b], [junkf.b, sc.b],
          accum=sc.ap[:, 0:1])
    k.stt("dve", junkf.ap, lv.ap[:, 128:192], 1.0, lv.ap[:, 192:256], ALU.mult, ALU.mult, [lv.b], [junkf.b, sc.b],
          accum=sc.ap[:, 1:2])
    k.act(sc.ap[:, 2:4], sc.ap[:, 0:2], AF.Exp, [sc.b], [sc.b])
    k.tt("dve", sc.ap[:, 4:5], sc.ap[:, 3:4], sc.ap[:, 2:3], ALU.subtract, [sc.b], [sc.b])
    k.ts("dve", sc.ap[:, 5:6], sc.ap[:, 4:5], -LAM_INIT, ALU.add, [sc.b], [sc.b])
    S.dma("sp", sc.ap[:, 6:7], dr["diff_norm_g"].rearrange("o v -> v o"), W=[sc.b])
    k.ts("dve", sc.ap[:, 7:8], sc.ap[:, 6:7], 1.0 - LAM_INIT, ALU.mult, [sc.b], [sc.b])
    neglam = sc.ap[:, 5:6]
    gcol = sc.ap[:, 7:8]

    stg = [k.f32(4 * 512, "stgA%d" % i) for i in range(2)]
    wh = [k.b16(8 * 512, "wh%d" % i) for i in range(2)]
    QT = k.b16(T, "QT")
    KT = k.b16(T, "KT")
    V = k.b16(T, "V")
    sgT = k.b16(T, "sgT")
    qraw = [k.b16(512, "qraw%d" % i) for i in range(2)]
    t1 = [k.f32(512, "t1_%d" % i) for i in range(2)]
    t2 = [k.f32(512, "t2_%d" % i) for i in range(2)]
    PT = [k.b16(1024, "PT%d" % i) for i in range(3)]
    rlb = k.b16(512, "rlb")
    lnl = k.f32(1024, "lnl")
    o01 = k.f32(1024, "o01")
    of = k.f32(512, "of")
    sq = k.b16(512, "sq")
    lnt = k.f32(512, "lnt")
    rst = k.f32(512, "rst")
    tf = k.f32(512, "tf")
    ogh = [k.b16(512, "ogh%d" % i) for i in range(2)]
    ones, rperm, maskT2, eps = c["ones"], c["rperm"], c["maskT2"], c["eps"]
    ident, negm = c["ident"], c["negm"]
    pO, pOb = k.banks(4, 2)
    pL, pLb = k.banks(6, 1)
    pX, pXb = k.banks(7, 1)
    selb = c["selb"]
    nq = 0
    npt = 0
    nog = 0

    def dma_w(h):
        for half in range(2):
            st = stg[half]
            s3 = v3(st.ap, 4)
            for blk in range(4):
                S.dma("sp", s3[:, :, blk * 128:(blk + 1) * 128],
                      dr["diff_w_in"][half * 512:(half + 1) * 512, blk * BR + h * 128:blk * BR + (h + 1) * 128]
                      .rearrange("(k p) n -> p k n", p=128), W=[st.b])

    def cast_w(h):
        for half in range(2):
            st = stg[half]
            k.copy("dve", wh[h % 2].ap[:, half * 2048:(half + 1) * 2048], st.ap, [st.b], [wh[h % 2].b])

    ntab = [0]

    def load_tab(tt_):
        j = tt_ % 2
        S.dma("sp", cst[j].ap, dr["cosT"][:, tt_ * 512:(tt_ + 1) * 512], W=[cst[j].b])
        S.dma("sp", snt[j].ap, dr["sinT"][:, tt_ * 512:(tt_ + 1) * 512], W=[snt[j].b])

    dma_w(0)
    cast_w(0)
    load_tab(0)
    load_tab(1)
    pend = []
    for h in range(DBG["heads"]):
        w3 = v3(wh[h % 2].ap, 8)
        wb = wh[h % 2].b
        for tt_ in range(8 if DBG["inproj"] else 0):
            ts_ = slice(tt_ * 512, (tt_ + 1) * 512)
            cosT, sinT = cst[tt_ % 2], snt[tt_ % 2]
            if tt_ >= 1:
                for _ in range(3):
                    if pend:
                        _, fn, q_, h_ = pend.pop(0)
                        fn(q_, h_)
            pqs = []
            for blk in (0, 1):
                pq, pqb = k.nextbank(1, 0, 7)
                for kc in range(8):
                    k.mm(pq, w3[:, kc, blk * 128:(blk + 1) * 128], h3[:, kc, ts_], kc == 0, kc == 7, [wb, h1T.b], pqb)
                pqs.append((pq, pqb))
            pg, pgb = k.nextbank(1, 0, 7)
            for kc in range(8):
                k.mm(pg, w3[:, kc, 384:512], h3[:, kc, ts_], kc == 0, kc == 7, [wb, h1T.b], pgb)
            pv, pvb = k.nextbank(1, 0, 7)
            for s_ in range(4):
                for kc in range(8):
                    k.mm(pv[:, s_ * 128:(s_ + 1) * 128], h3[:, kc, tt_ * 512 + s_ * 128:tt_ * 512 + (s_ + 1) * 128],
                         w3[:, kc, 256:384], kc == 0, kc == 7, [wb, h1T.b], pvb)
            rots = []
            for blk, dst in ((0, QT), (1, KT)):
                pq, pqb = pqs[blk]
                qr = qraw[nq % 2]
                a1, a2 = t1[nq % 2], t2[nq % 2]
                nq += 1
                k.copy("act", qr.ap, pq, pqb, [qr.b])
                rots.append((pq, pqb, qr, a1, a2, dst))
            k.act(sgT.ap[:, ts_], pg, AF.Silu, pgb, [sgT.b])
            k.copy("act", V.ap[:, ts_], pv, pvb, [V.b])
            for (pq, pqb, qr, a1, a2, dst) in rots:
                pr, prb = k.nextbank(1, 0, 7)
                k.mm(pr, rperm.ap, qr.ap, True, True, [rperm.b, qr.b], prb)
                k.tt("dve", a1.ap, pq, cosT.ap, ALU.mult, pqb + [cosT.b], [a1.b])
                k.tt("dve", a2.ap, pr, sinT.ap, ALU.mult, prb + [sinT.b], [a2.b])
                k.tt("pool", dst.ap[:, ts_], a1.ap, a2.ap, ALU.add, [a1.b, a2.b], [dst.b])
            if tt_ + 2 < 8:
                load_tab(tt_ + 2)
        if h + 1 < 16:
            dma_w(h + 1)
        blocks = [(qt, kb) for qt in range(DBG["nqt"] if DBG["att"] else 0) for kb in range(4 * qt + 4)]

        def qk_exp(qt, kb):
            nonlocal npt
            j = kb - 4 * qt
            c0 = max(j, 0) * 128
            ps2, ps2b = k.banks(2 * (npt % 2), 2)
            pt_ = PT[npt % 3]
            npt += 1
            for cc in range(2):
                k.mm(ps2[:, cc * 512 + c0:(cc + 1) * 512], KT.ap[cc * 64:(cc + 1) * 64, kb * 128:(kb + 1) * 128],
                     QT.ap[cc * 64:(cc + 1) * 64, qt * 512 + c0:(qt + 1) * 512], True, j < 0, [KT.b, QT.b],
                     [ps2b[cc]])
            if j >= 0:
                for cc in range(2):
                    k.mm(ps2[:, cc * 512 + c0:cc * 512 + c0 + 128], ident.ap, negm.ap, False, True,
                         [ident.b, negm.b], [ps2b[cc]])
            p3 = v3(pt_.ap, 2)
            k.act(p3[:, :, c0:512], v3(ps2, 2)[:, :, c0:512], AF.Exp, ps2b, [pt_.b], scale=0.125)
            return pt_, c0

        def av(qt, kb, pt_, c0):
            nkb = 4 * qt + 4
            for cc in range(2):
                k.mmt(pL[32 * cc:32 * cc + 32, c0:512], ones.ap[:, 0:32], pt_.ap[:, cc * 512 + c0:(cc + 1) * 512],
                      kb == 0, kb == nkb - 1, (0, 32 * cc), [ones.b, pt_.b], pLb)
            for cc in range(2):
                k.mm(pO[:, cc * 512 + c0:(cc + 1) * 512], V.ap[:, kb * 128:(kb + 1) * 128],
                     pt_.ap[:, cc * 512 + c0:(cc + 1) * 512], kb == 0, kb == nkb - 1, [V.b, pt_.b], [pOb[cc]])

        def epi1(qt, hd):
            k.act(lnl.ap[0:64, 0:512], pL[0:64, :], AF.Ln, pLb, [lnl.b])
            k.copy("dve", o01.ap[:, 0:512], pO[:, 0:512], [pOb[0]], [o01.b])
            k.copy("dve", o01.ap[:, 512:1024], pO[:, 512:1024], [pOb[1]], [o01.b])
            k.act(rlb.ap[0:64, 0:512], lnl.ap[0:64, 0:512], AF.Exp, [lnl.b], [rlb.b], scale=-1.0)

        def st_b0(qt, hd):
            k.mm(pX, selb.ap[0:64, 0:128], rlb.ap[0:64, 0:512], True, True, [selb.b, rlb.b], pXb)

        def st_m0(qt, hd):
            k.tt("dve", o01.ap[:, 0:512], o01.ap[:, 0:512], pX, ALU.mult, [o01.b] + pXb, [o01.b])

        def st_b1(qt, hd):
            k.mm(pX, selb.ap[0:64, 128:256], rlb.ap[0:64, 0:512], True, True, [selb.b, rlb.b], pXb)

        def st_m1(qt, hd):
            k.tt("dve", o01.ap[:, 512:1024], o01.ap[:, 512:1024], pX, ALU.mult, [o01.b] + pXb, [o01.b])
            k.stt("dve", of.ap, o01.ap[:, 512:1024], neglam, o01.ap[:, 0:512], ALU.mult, ALU.add, [o01.b, sc.b], [of.b])

        def st_sq(qt, hd):
            k.act(sq.ap, of.ap, AF.Square, [of.b], [sq.b])

        def st_ss(qt, hd):
            k.mm(pX, ones.ap, sq.ap, True, True, [ones.b, sq.b], pXb)

        def st_ln(qt, hd):
            k.act(lnt.ap, pX, AF.Ln, pXb + [eps.b], [lnt.b], scale=1.0 / 128, bias=eps.ap[:, 0:1])

        def st_ex(qt, hd):
            k.act(rst.ap, lnt.ap, AF.Exp, [lnt.b], [rst.b], scale=-0.5)

        def st_fin(qt, hd):
            nonlocal nog
            qs = slice(qt * 512, (qt + 1) * 512)
            k.tt("dve", tf.ap, of.ap, rst.ap, ALU.mult, [of.b, rst.b], [tf.b])
            og_ = ogh[nog % 2]
            nog += 1
            k.stt("dve", og_.ap, tf.ap, gcol, sgT.ap[:, qs], ALU.mult, ALU.mult, [tf.b, sc.b, sgT.b], [og_.b])
            S.dma("sp", dr["ogT"][qt * 4:(qt + 1) * 4, :, hd * 128:(hd + 1) * 128].rearrange("s p v -> p s v"),
                  v3(og_.ap, 4), R=[og_.b])

        STAGES0 = ((2, st_b0), (3, st_m0), (4, st_b1), (5, st_m1), (6, st_sq), (7, st_ss), (8, st_ln), (8, st_ex),
                   (8, st_fin))
        STAGES = ((2, st_b0), (3, st_m0), (4, st_b1), (5, st_m1), (6, st_sq), (7, st_ss), (8, st_ln), (9, st_ex),
                  (10, st_fin))

        cur = qk_exp(*blocks[0]) if blocks else None
        for bi, (qt, kb) in enumerate(blocks):
            nxt = qk_exp(*blocks[bi + 1]) if bi + 1 < len(blocks) else None
            av(qt, kb, *cur)
            while pend and bi >= pend[0][0]:
                _, fn, q_, h_ = pend.pop(0)
                fn(q_, h_)
            if kb == 4 * qt + 3 and DBG["epi"]:
                epi1(qt, h)
                for off, fn in (STAGES0 if qt == 0 else STAGES):
                    pend.append((bi + off, fn, qt, h))
            if bi == 10 and h + 1 < 16:
                cast_w(h + 1)
            if bi == 20 and h + 1 < 16:
                load_tab(0)
                load_tab(1)
            cur = nxt
        pend = [(-1, fn, q_, h_) for (_, fn, q_, h_) in pend]
    for _, fn, q_, h_ in pend:
        fn(q_, h_)
    S.barrier()
    k.AF.release(mF)
    k.AB.release(mB)


NAF = 16 * 1024
NAB = 68 * 1024

_INPUT_NAMES = ("x", "pre_g", "post_g", "gla_w_in", "gla_w_g2", "gla_b_g", "gla_norm_g", "gla_w_out",
                "diff_w_in", "diff_lam_q1", "diff_lam_k1", "diff_lam_q2", "diff_lam_k2", "diff_norm_g", "diff_w_out")
_SHAPES = {
    "x": [T, D], "pre_g": [2, D], "post_g": [2, D], "gla_w_in": [D, GIN], "gla_w_g2": [16, 512],
    "gla_b_g": [1, 512], "gla_norm_g": [1, 512], "gla_w_out": [BR, D], "diff_w_in": [D, 4 * BR],
    "diff_lam_q1": [1, 64], "diff_lam_k1": [1, 64], "diff_lam_q2": [1, 64], "diff_lam_k2": [1, 64],
    "diff_norm_g": [1, 128], "diff_w_out": [BR, D],
}


def build(mode="both"):
    nc = bass.Bass("TRN2", target_bir_lowering=False)
    dr = {}
    for nm in _INPUT_NAMES:
        if mode == "L1" and nm == "x":
            continue
        dr[nm] = nc.dram_tensor(nm, _SHAPES[nm], F32, kind="ExternalInput").ap()
    dr["cst_bf"] = nc.dram_tensor("cst_bf", [128, 1024], BF16, kind="ExternalInput").ap()
    dr["cst_f"] = nc.dram_tensor("cst_f", [128, 1024], F32, kind="ExternalInput").ap()
    dr["cosT"] = nc.dram_tensor("cosT", [128, T], F32, kind="ExternalInput").ap()
    dr["sinT"] = nc.dram_tensor("sinT", [128, T], F32, kind="ExternalInput").ap()
    dr["ogT"] = nc.dram_tensor("ogT_scr", [NT, 128, BR], BF16, kind="Internal").ap()
    if mode == "both":
        dr["x1"] = nc.dram_tensor("x1_scr", [T, D], F32, kind="Internal").ap()
        dr["out"] = nc.dram_tensor("out", [T, D], F32, kind="ExternalOutput").ap()
    elif mode == "L0":
        dr["x1"] = nc.dram_tensor("x1", [T, D], F32, kind="ExternalOutput").ap()
        dr["h1T_d"] = nc.dram_tensor("h1T", [128, 8 * T], BF16, kind="ExternalOutput").ap()
    else:
        dr["x1"] = nc.dram_tensor("x1", [T, D], F32, kind="ExternalInput").ap()
        dr["h1T_d"] = nc.dram_tensor("h1T", [128, 8 * T], BF16, kind="ExternalInput").ap()
        dr["out"] = nc.dram_tensor("out", [T, D], F32, kind="ExternalOutput").ap()

    with ExitStack() as st:
        S = Sched(nc, st)
        af = st.enter_context(nc.sbuf_tensor("arena_f", [128, NAF], F32))
        ab = st.enter_context(nc.sbuf_tensor("arena_b", [128, NAB], BF16))
        ps = st.enter_context(nc.psum_tensor("ps", [128, 4096], F32))
        k = K(nc, S, Arena(af, NAF), Arena(ab, NAB), ps)
        c = load_consts(k, dr)
        if mode in ("both", "L0"):
            phase_A0(k, c, dr)
        h1T = k.b16(8 * T, "h1T")
        if mode in ("both", "L0"):
            phase_F(k, c, dr, dr["gla_w_out"], dr["post_g"][0:1, :], dr["x"], dr["x1"],
                    pre_row=dr["pre_g"][1:2, :], h1T=h1T)
        if mode == "L0":
            for j in range(8):
                S.dma("sp", dr["h1T_d"][:, j * T:(j + 1) * T], h1T.ap[:, j * T:(j + 1) * T], R=[h1T.b])
            S.barrier()
        if mode == "L1":
            for j in range(8):
                S.dma("sp", h1T.ap[:, j * T:(j + 1) * T], dr["h1T_d"][:, j * T:(j + 1) * T], W=[h1T.b])
        if mode in ("both", "L1"):
            phase_A1(k, c, dr, h1T)
            if DBG["F1"]:
                phase_F(k, c, dr, dr["diff_w_out"], dr["post_g"][1:2, :], dr["x1"], dr["out"])
        S.barrier()
        print("ops", S.nops, "arena peaks f32 %d bf16 %d" % (k.AF.peak, k.AB.peak), "sems", sum(1 for b in S.allbufs if b.dsem is not None))
        with nc.Block() as block:
            S.replay(block)
    return nc


def _consts():
    bf = ml_dtypes.bfloat16
    p = np.arange(128)
    ident = np.eye(128, dtype=np.float32)
    perm = np.where((p % 64) < 32, p + 32, p - 32)
    rperm = np.zeros((128, 128), np.float32)
    rperm[perm, p] = 1.0
    ones = np.ones((128, 128), np.float32)
    maskT = (p[None, :] >= p[:, None]).astype(np.float32)
    negm = np.where(p[None, :] >= p[:, None], 0.0, -30000.0).astype(np.float32)
    selb = np.zeros((128, 256), np.float32)
    selb[0, 0:128] = 1.0
    selb[32, 128:256] = 1.0
    cst_bf = np.concatenate([ident, rperm, ones, maskT, maskT, negm, selb], 1).astype(bf)
    triU = np.where(p[:, None] <= p[None, :], -1.0 / 16.0, 0.0).astype(np.float32)
    triL = np.where(p[:, None] > p[None, :], -1.0 / 16.0, 0.0).astype(np.float32)
    maskU = (p[:, None] <= p[None, :]).astype(np.float32)
    sel = np.zeros((128, 256), np.float32)
    sel[0, 0:128] = 1.0
    sel[32, 128:256] = 1.0
    cst_f = np.concatenate([triU, triL, maskU, maskU, maskU, maskU, sel], 1).astype(np.float32)
    inv_freq = (1.0 / (np.float32(10000.0) ** (np.arange(0, 64, 2, dtype=np.float32) / np.float32(64)))).astype(np.float32)
    pos = np.arange(T, dtype=np.float32)
    ang = (pos[:, None] * inv_freq[None, :]).astype(np.float32)
    cos = np.cos(ang).astype(np.float32).T
    sin = np.sin(ang).astype(np.float32).T
    cosT = np.concatenate([cos, cos, cos, cos], 0)
    sinT = np.concatenate([-sin, sin, -sin, sin], 0)
    return {"cst_bf": np.ascontiguousarray(cst_bf), "cst_f": np.ascontiguousarray(cst_f),
            "cosT": np.ascontiguousarray(cosT), "sinT": np.ascontiguousarray(sinT)}


_NC_CACHE = {}


def _get_nc(mode):
    if mode not in _NC_CACHE:
        _NC_CACHE[mode] = build(mode)
    return _NC_CACHE[mode]


def _shared_maps(inputs):
    m = dict(_consts())
    for nm in _INPUT_NAMES:
        if nm == "x":
            continue
        a = np.asarray(inputs[nm], dtype=np.float32)
        if nm in ("pre_g", "post_g"):
            m[nm] = np.ascontiguousarray(a)
        else:
            m[nm] = np.ascontiguousarray(a.reshape(_SHAPES[nm]))
    return m


FUSED = True


def kernel(**inputs):
    x = np.asarray(inputs["x"], dtype=np.float32)
    shared = _shared_maps(inputs)
    cores = list(range(NCORES))
    if FUSED:
        nc = _get_nc("both")
        in_maps = [dict(shared, x=np.ascontiguousarray(x[b])) for b in cores]
        res = run_bass_kernel_spmd(nc, in_maps, core_ids=cores)
        return np.stack([np.asarray(r["out"]) for r in res.results], 0).astype(np.float32)
    nc0 = _get_nc("L0")
    in_maps = [dict(shared, x=np.ascontiguousarray(x[b])) for b in cores]
    r0 = run_bass_kernel_spmd(nc0, in_maps, core_ids=cores).results
    nc1 = _get_nc("L1")
    in_maps = [dict(shared, x1=np.asarray(r0[b]["x1"]), h1T=np.asarray(r0[b]["h1T"])) for b in cores]
    r1 = run_bass_kernel_spmd(nc1, in_maps, core_ids=cores).results
    return np.stack([np.asarray(r["out"]) for r in r1], 0).astype(np.float32)
```

```python
import math
from contextlib import ExitStack

import numpy as np
import ml_dtypes
import concourse.bass as bass
import concourse.mybir as mybir
from concourse.bass_utils import run_bass_kernel_spmd

F32 = mybir.dt.float32
BF16 = mybir.dt.bfloat16
AF = mybir.ActivationFunctionType
ALU = mybir.AluOpType

T = 4096
D = 1024
BR = 2048
NT = T // 128
GIN = 5136
EPS = 1e-6
LAM_INIT = 0.8 - 0.6 * math.exp(-0.3 * 1)
NCORES = 8
DBG = {"heads": 16, "nqt": 8, "inproj": True, "att": True, "epi": True, "F1": True, "nkb": 99, "ipp": "qgv", "rope": True}

ENGS = ("pe", "act", "dve", "pool", "sp")


class Buf:
    __slots__ = ("name", "w", "r", "dsem", "dcnt", "excl")

    def __init__(self, name):
        self.name = name
        self.excl = False
        self.w = None
        self.r = []
        self.dsem = None
        self.dcnt = 0


class Rec:
    __slots__ = ("waits", "fn", "inc", "dma_inc", "pos", "val")

    def __init__(self, waits, fn):
        self.waits = waits
        self.fn = fn
        self.inc = False
        self.dma_inc = None
        self.pos = -1
        self.val = 0


class Sched:
    def __init__(self, nc, stack):
        self.nc = nc
        self.stack = stack
        self.ops = {e: [] for e in ENGS}
        self.sem = {e: stack.enter_context(nc.semaphore("sem_" + e)) for e in ENGS}
        self.waited = {e: {} for e in ENGS}
        self.allbufs = []
        self.same_engine_sync = True
        self.same_engine_all = True
        self.nops = 0
        self.nbar = 0

    def buf(self, name):
        b = Buf("%s_%d" % (name, len(self.allbufs)))
        self.allbufs.append(b)
        return b

    def _dsem(self, b):
        if b.dsem is None:
            b.dsem = self.stack.enter_context(self.nc.semaphore("d_" + b.name))
        return b.dsem

    def _push(self, eng, rec):
        rec.pos = len(self.ops[eng])
        self.ops[eng].append(rec)

    def _resolve(self, eng, tok, waits, is_raw):
        if tok[0] == "e":
            _, e2, rec = tok
            if e2 == eng:
                if eng in ("pe", "sp") or not (self.same_engine_sync and (is_raw or self.same_engine_all)):
                    return
            key = "e_" + e2
            if self.waited[eng].get(key, -1) >= rec.pos:
                return
            self.waited[eng][key] = rec.pos
            rec.inc = True
            waits.append(("e", e2, rec))
        else:
            _, b, val = tok
            key = "d_" + b.name
            if self.waited[eng].get(key, 0) >= val:
                return
            self.waited[eng][key] = val
            waits.append(("d", b.dsem, val))

    def _deps(self, eng, R, W):
        waits = []
        for b in R:
            if b.w is not None:
                self._resolve(eng, b.w, waits, True)
            if b.excl:
                for t in b.r:
                    if t[0] != "e" or t[1] != eng:
                        self._resolve(eng, t, waits, False)
        for b in W:
            if b.w is not None:
                self._resolve(eng, b.w, waits, False)
            for t in b.r:
                self._resolve(eng, t, waits, False)
        return waits

    def op(self, eng, fn, R=(), W=()):
        waits = self._deps(eng, R, W)
        rec = Rec(waits, fn)
        self._push(eng, rec)
        tok = ("e", eng, rec)
        for b in R:
            b.r = [t for t in b.r if not (t[0] == "e" and t[1] == eng)]
            b.r.append(tok)
        for b in W:
            b.w = tok
            b.r = []
        self.nops += 1
        return rec

    def dma(self, q, out_ap, in_ap, R=(), W=()):
        waits = self._deps(q, R, W)
        owner = W[0] if W else R[0]
        sem = self._dsem(owner)
        if not W and owner.dcnt > 0:
            self._resolve(q, ("d", owner, owner.dcnt), waits, False)
        owner.dcnt += 16
        tok = ("d", owner, owner.dcnt)
        rec = Rec(waits, lambda e: e.dma_start(out=out_ap, in_=in_ap))
        rec.dma_inc = sem
        self._push(q, rec)
        for b in R:
            b.r.append(tok)
        for b in W:
            b.w = tok
            b.r = []
        self.nops += 1
        return rec

    def barrier(self):
        waits = []
        comp = ("pe", "act", "dve", "pool")
        last = {}
        for e in comp:
            recs = [r for r in self.ops[e] if r.fn is not None]
            if recs:
                last[e] = recs[-1]
                if self.waited["sp"].get("e_" + e, -1) < recs[-1].pos:
                    self.waited["sp"]["e_" + e] = recs[-1].pos
                    recs[-1].inc = True
                    waits.append(("e", e, recs[-1]))
        for b in self.allbufs:
            if b.dsem is not None and b.dcnt > self.waited["sp"].get("d_" + b.name, 0):
                self.waited["sp"]["d_" + b.name] = b.dcnt
                waits.append(("d", b.dsem, b.dcnt))
        sem_sp = self.sem["sp"]
        rec = Rec(waits, lambda e: e.sem_inc(sem_sp, 1))
        self._push("sp", rec)
        self.nbar += 1
        v = self.nbar
        for e in comp:
            self._push(e, Rec([("s", sem_sp, v)], None))
            for e2 in comp:
                if e2 in last:
                    self.waited[e]["e_" + e2] = last[e2].pos
            for b in self.allbufs:
                if b.dsem is not None:
                    self.waited[e]["d_" + b.name] = b.dcnt
        for b in self.allbufs:
            b.w = None
            b.r = []

    def replay(self, block):
        S = self
        for eng in ENGS:
            n = 0
            for rec in S.ops[eng]:
                if rec.inc:
                    n += 1
                    rec.val = n
        self.nsig = {eng: sum(1 for r in S.ops[eng] if r.inc) for eng in ENGS}

        def run(eng, e):
            for rec in S.ops[eng]:
                for w in rec.waits:
                    if w[0] == "e":
                        assert w[2].inc and w[2].val > 0
                        e.wait_ge(S.sem[w[1]], w[2].val)
                    else:
                        e.wait_ge(w[1], w[2])
                if rec.fn is None:
                    continue
                ins = rec.fn(e)
                if rec.dma_inc is not None:
                    ins.then_inc(rec.dma_inc, 16)
                elif rec.inc:
                    ins.then_inc(S.sem[eng], 1)

        @block.tensor
        def _(e):
            run("pe", e)

        @block.scalar
        def _(e):
            run("act", e)

        @block.vector
        def _(e):
            run("dve", e)

        @block.gpsimd
        def _(e):
            run("pool", e)

        @block.sync
        def _(e):
            run("sp", e)


class Arena:
    def __init__(self, ap, n):
        self.ap = ap
        self.n = n
        self.top = 0
        self.peak = 0

    def alloc(self, n):
        a = self.ap[:, self.top:self.top + n]
        self.top += (n + 15) // 16 * 16
        self.peak = max(self.peak, self.top)
        assert self.top <= self.n, ("arena overflow", self.top, self.n)
        return a

    def mark(self):
        return self.top

    def release(self, m):
        self.top = m


class Tile:
    __slots__ = ("ap", "b")

    def __init__(self, ap, b):
        self.ap = ap
        self.b = b


class K:
    def __init__(self, nc, S, AFa, ABa, ps):
        self.nc, self.S, self.AF, self.AB, self.ps = nc, S, AFa, ABa, ps
        self.bankb = [S.buf("bank%d" % i) for i in range(8)]
        for b in self.bankb:
            b.excl = True
        self.rr = 0
        self.cast_rr = 0

    def f32(self, n, name):
        return Tile(self.AF.alloc(n), self.S.buf(name))

    def b16(self, n, name):
        return Tile(self.AB.alloc(n), self.S.buf(name))

    def banks(self, i, n=1):
        return self.ps[:, i * 512:(i + n) * 512], [self.bankb[j] for j in range(i, i + n)]

    def nextbank(self, n=1, lo=0, hi=8):
        if self.rr < lo or self.rr + n > hi:
            self.rr = lo
        i = self.rr
        self.rr += n
        return self.banks(i, n)

    def mm(self, out, lhsT, rhs, start, stop, R, W):
        self.S.op("pe", lambda e: e.matmul(out, lhsT=lhsT, rhs=rhs, start=start, stop=stop), R, W)

    def mmt(self, out, lhsT, rhs, start, stop, tpos, R, W):
        self.S.op("pe", lambda e: e.matmul(out, lhsT=lhsT, rhs=rhs, start=start, stop=stop, tile_position=tpos), R, W)

    def tr(self, out, in_, ident, R, W):
        self.S.op("pe", lambda e: e.transpose(out=out, in_=in_, identity=ident), R, W)

    def act(self, out, in_, func, R, W, scale=None, bias=None, accum=None):
        kw = {}
        if scale is not None:
            kw["scale"] = scale
        if bias is not None:
            kw["bias"] = bias
        if accum is not None:
            kw["accum_out"] = accum
        self.S.op("act", lambda e: e.activation(out=out, in_=in_, func=func, **kw), R, W)

    def tt(self, eng, out, a, b, op, R, W):
        self.S.op(eng, lambda e: e.tensor_tensor(out=out, in0=a, in1=b, op=op), R, W)

    def stt(self, eng, out, a, sc, b, op0, op1, R, W, accum=None):
        if accum is None:
            self.S.op(eng, lambda e: e.scalar_tensor_tensor(out=out, in0=a, scalar=sc, in1=b, op0=op0, op1=op1), R, W)
        else:
            self.S.op(eng, lambda e: e.scalar_tensor_tensor(out=out, in0=a, scalar=sc, in1=b, op0=op0, op1=op1,
                                                           accum_out=accum), R, W)

    def ts(self, eng, out, a, s1, op0, R, W, s2=None, op1=None):
        if s2 is None:
            self.S.op(eng, lambda e: e.tensor_scalar(out=out, in0=a, scalar1=s1, scalar2=None, op0=op0), R, W)
        else:
            self.S.op(eng, lambda e: e.tensor_scalar(out=out, in0=a, scalar1=s1, scalar2=s2, op0=op0, op1=op1), R, W)

    def copy(self, eng, out, in_, R, W):
        if eng == "act":
            self.S.op("act", lambda e: e.copy(out=out, in_=in_), R, W)
        else:
            self.S.op(eng, lambda e: e.tensor_copy(out=out, in_=in_), R, W)

    def cast_any(self, out, in_, R, W):
        eng = ("pool", "dve", "act")[self.cast_rr % 3]
        self.cast_rr += 1
        self.copy(eng, out, in_, R, W)

    def memset(self, eng, ap, val, W):
        self.S.op(eng, lambda e: e.memset(ap, val), (), W)

    def rstd(self, ssq, tmp, out, n, eps_ap, R, W):
        self.act(tmp, ssq, AF.Ln, R, W, scale=1.0 / n, bias=eps_ap)
        self.act(out, tmp, AF.Exp, W, W, scale=-0.5)


def v3(ap, a):
    return ap.rearrange("p (a b) -> p a b", a=a)


def load_consts(k, dr):
    S = k.S
    c = {}
    c["ident"] = k.b16(128, "ident")
    c["rperm"] = k.b16(128, "rperm")
    c["ones"] = k.b16(128, "ones")
    c["maskT2"] = k.b16(256, "maskT2")
    c["negm"] = k.b16(128, "negm")
    c["selb"] = k.b16(256, "selb")
    for i, nm in enumerate(("ident", "rperm", "ones")):
        S.dma("sp", c[nm].ap, dr["cst_bf"][:, i * 128:(i + 1) * 128], W=[c[nm].b])
    S.dma("sp", c["maskT2"].ap, dr["cst_bf"][:, 384:640], W=[c["maskT2"].b])
    S.dma("sp", c["negm"].ap, dr["cst_bf"][:, 640:768], W=[c["negm"].b])
    S.dma("sp", c["selb"].ap, dr["cst_bf"][:, 768:1024], W=[c["selb"].b])
    c["triU"] = k.f32(128, "triU")
    c["triL"] = k.f32(128, "triL")
    c["maskU4"] = k.f32(512, "maskU4")
    c["sel"] = k.f32(256, "sel")
    S.dma("sp", c["triU"].ap, dr["cst_f"][:, 0:128], W=[c["triU"].b])
    S.dma("sp", c["triL"].ap, dr["cst_f"][:, 128:256], W=[c["triL"].b])
    S.dma("sp", c["maskU4"].ap, dr["cst_f"][:, 256:768], W=[c["maskU4"].b])
    S.dma("sp", c["sel"].ap, dr["cst_f"][:, 768:1024], W=[c["sel"].b])
    c["eps"] = k.f32(16, "eps")
    k.memset("pool", c["eps"].ap[:, 0:1], EPS, [c["eps"].b])
    c["small"] = k.f32(64, "small")
    c["junk"] = k.b16(2048, "junk")
    return c


def bcast_load(k, tile_ap, b, dram_row):
    k.S.dma("sp", tile_ap, dram_row.partition_broadcast(128), W=[b])


def phase_A0(k, c, dr):
    S = k.S
    mF, mB = k.AF.mark(), k.AB.mark()
    w_in = k.b16(8 * GIN, "w_in")
    w3 = v3(w_in.ap, 8)
    m2 = k.AF.mark()
    HW = GIN // 2
    stg = [k.f32(HW, "stg%d" % i) for i in range(4)]
    n = 0
    for kc in range(8):
        for half in range(2):
            st = stg[n % 4]
            n += 1
            S.dma("sp", st.ap, dr["gla_w_in"][kc * 128:(kc + 1) * 128, half * HW:(half + 1) * HW], W=[st.b])
            k.cast_any(w3[:, kc, half * HW:(half + 1) * HW], st.ap, [st.b], [w_in.b])
    S.barrier()
    k.AF.release(m2)
    pre0 = k.f32(D, "pre0")
    bcast_load(k, pre0.ap, pre0.b, dr["pre_g"][0:1, :])
    ngf = k.f32(BR, "ngf")
    for h in range(4):
        bcast_load(k, ngf.ap[:, h * 512:(h + 1) * 512], ngf.b, dr["gla_norm_g"][0:1, :])
    wg2 = k.f32(512, "wg2")
    S.dma("sp", wg2.ap[0:16, :], dr["gla_w_g2"], W=[wg2.b])
    S.dma("sp", wg2.ap[16:17, :], dr["gla_b_g"], W=[wg2.b])
    lowT = [k.f32(128, "lowT%d" % i) for i in range(2)]
    for t in lowT:
        k.memset("pool", t.ap[0:32, :], 1.0, [t.b])
    xs = [k.f32(D, "xs%d" % i) for i in range(2)]
    ez = k.f32(512, "ez")
    sp = k.f32(512, "sp")
    ebT = k.f32(512, "ebT")
    enT = k.f32(512, "enT")
    erev = k.f32(512, "erev")
    Sf = [k.f32(512, "Sf%d" % h) for h in range(4)]
    Sb = [k.b16(512, "Sb%d" % h) for h in range(4)]
    for h in range(4):
        k.memset("pool", Sf[h].ap, 0.0, [Sf[h].b])
        k.memset("pool", Sb[h].ap, 0.0, [Sb[h].b])
    sm = [k.f32(16, "sm%d" % i) for i in range(2)]
    hb = [k.b16(D, "hb%d" % i) for i in range(2)]
    hT = [k.b16(D, "hT%d" % i) for i in range(2)]
    qtT = [k.b16(512, "qtT")] * 2
    ktT = [k.b16(512, "ktT")] * 2
    kend = [k.b16(512, "kend")] * 2
    vb = [k.b16(BR, "vb")] * 2
    sg = k.b16(BR, "sg")
    sgn = [k.b16(BR, "sgn")] * 2
    ATb = [k.b16(512, "ATb")] * 2
    og = [k.b16(BR, "og%d" % i) for i in range(2)]
    ogT = [k.b16(BR, "ogT%d" % i) for i in range(2)]
    ident = c["ident"]
    junk = c["junk"]
    eps = c["eps"]
    SCALE = 128 ** -0.5

    smp = [k.f32(16, "smp%d" % i) for i in range(2)]

    def pro_a(i):
        p = i % 2
        x = xs[p]
        s = smp[p]
        S.dma("sp", x.ap, dr["x"][i * 128:(i + 1) * 128, :], W=[x.b])
        k.act(junk.ap[:, 0:D], x.ap, AF.Square, [x.b], [junk.b, s.b], accum=s.ap[:, 0:1])
        k.rstd(s.ap[:, 0:1], s.ap[:, 1:2], s.ap[:, 2:3], D, eps.ap[:, 0:1], [s.b, eps.b], [s.b])
        h = hb[p]
        k.stt("dve", h.ap, x.ap, s.ap[:, 2:3], pre0.ap, ALU.mult, ALU.mult, [x.b, s.b, pre0.b], [h.b])

    def pro_b(i):
        p = i % 2
        h = hb[p]
        pb, pbb = k.banks(7)
        pv = pb.bitcast(BF16)
        for kc in range(8):
            k.tr(pv[:, kc * 128:(kc + 1) * 128], h.ap[:, kc * 128:(kc + 1) * 128], ident.ap, [h.b, ident.b], pbb)
        ht = hT[p]
        k.copy("act", ht.ap, pv, pbb, [ht.b])

    osb = k.f32(BR, "osb")
    smh = [k.f32(16, "smh%d" % i) for i in range(4)]
    prev_tail = [None]

    def tail(i, o_):
        pt, ptb = k.banks(5, 2)
        ptv = pt.bitcast(BF16)
        for cc in range(16):
            k.tr(ptv[:, cc * 128:(cc + 1) * 128], o_.ap[:, cc * 128:(cc + 1) * 128], ident.ap, [o_.b, ident.b], ptb)
        ot = ogT[i % 2]
        k.copy("act", ot.ap, ptv, ptb, [ot.b])
        S.dma("sp", dr["ogT"][i], ot.ap, R=[ot.b])

    def low(i):
        ht_ = hT[i % 2]
        pl, plb = k.banks(5)
        for kc in range(8):
            k.mm(pl[0:16, 0:128], w3[:, kc, 5120:5136], ht_.ap[:, kc * 128:(kc + 1) * 128], kc == 0, kc == 7,
                 [ht_.b, w_in.b], plb)
        k.copy("act", lowT[i % 2].ap[0:16, :], pl[0:16, 0:128], plb, [lowT[i % 2].b])

    pro_a(0)
    pro_b(0)
    low(0)
    for i in range(NT):
        p = i % 2
        ht = hT[p]
        hk = lambda kc: ht.ap[:, kc * 128:(kc + 1) * 128]
        lt = lowT[p]
        pkt, pktb = k.banks(4)
        for kc in range(8):
            k.mm(pkt, hk(kc), w3[:, kc, 512:1024], kc == 0, kc == 7, [ht.b, w_in.b], pktb)
        pz, pzb = k.banks(5)
        k.mm(pz, lt.ap[0:17, :], wg2.ap[0:17, :], True, True, [lt.b, wg2.b], pzb)
        k.act(ez.ap, pz, AF.Exp, pzb, [ez.b], scale=-1.0)
        k.act(sp.ap, ez.ap, AF.Ln, [ez.b], [sp.b], bias=1.0)
        pv4, pv4b = k.banks(0, 4)
        for cc in range(4):
            for kc in range(8):
                k.mm(pv4[:, cc * 512:(cc + 1) * 512], hk(kc), w3[:, kc, 1024 + cc * 512:1024 + (cc + 1) * 512],
                     kc == 0, kc == 7, [ht.b, w_in.b], [pv4b[cc]])
        v = vb[p]
        k.copy("act", v.ap, pv4, pv4b, [v.b])
        if prev_tail[0] is not None:
            tail(*prev_tail[0])
            prev_tail[0] = None
        pc, pcb = k.banks(5)
        for hh in range(4):
            k.mm(pc[:, hh * 128:(hh + 1) * 128], sp.ap[:, hh * 128:(hh + 1) * 128], c["triU"].ap, True, True,
                 [sp.b, c["triU"].b], pcb)
        pr, prb = k.banks(6)
        k.mm(pr, c["triL"].ap, sp.ap, True, True, [sp.b, c["triL"].b], prb)
        k.act(ebT.ap, pc, AF.Exp, pcb, [ebT.b])
        k.act(enT.ap, pc, AF.Exp, pcb, [enT.b], scale=-1.0)
        k.act(erev.ap, pr, AF.Exp, prb, [erev.b])
        ke = kend[p]
        k.tt("dve", ke.ap, pkt, erev.ap, ALU.mult, pktb + [erev.b], [ke.b])
        if i + 1 < NT:
            pro_a(i + 1)
        pq, pqb = k.banks(5)
        for hh in range(4):
            for kc in range(8):
                k.mm(pq[:, hh * 128:(hh + 1) * 128], w3[:, kc, hh * 128:(hh + 1) * 128], hk(kc), kc == 0, kc == 7,
                     [ht.b, w_in.b], pqb)
        pk, pkb = k.banks(6)
        for hh in range(4):
            for kc in range(8):
                k.mm(pk[:, hh * 128:(hh + 1) * 128], w3[:, kc, 512 + hh * 128:512 + (hh + 1) * 128], hk(kc), kc == 0,
                     kc == 7, [ht.b, w_in.b], pkb)
        qt_, kt_ = qtT[p], ktT[p]
        k.stt("dve", qt_.ap, pq, SCALE, ebT.ap, ALU.mult, ALU.mult, pqb + [ebT.b], [qt_.b])
        k.tt("dve", kt_.ap, pk, enT.ap, ALU.mult, pkb + [enT.b], [kt_.b])
        pg4, pg4b = k.banks(0, 4)
        for cc in range(4):
            for kc in range(8):
                k.mm(pg4[:, cc * 512:(cc + 1) * 512], hk(kc), w3[:, kc, 3072 + cc * 512:3072 + (cc + 1) * 512],
                     kc == 0, kc == 7, [ht.b, w_in.b], [pg4b[cc]])
        if i + 1 < NT:
            pro_b(i + 1)
        k.act(sg.ap, pg4, AF.Silu, pg4b, [sg.b])
        sn = sgn[p]
        k.tt("pool", sn.ap, sg.ap, ngf.ap, ALU.mult, [sg.b, ngf.b], [sn.b])
        pa, pab = k.banks(4)
        for hh in range(4):
            sl = slice(hh * 128, (hh + 1) * 128)
            k.mm(pa[:, sl], kt_.ap[:, sl], qt_.ap[:, sl], True, True, [kt_.b, qt_.b], pab)
        at = ATb[p]
        k.tt("dve", at.ap, pa, c["maskU4"].ap, ALU.mult, pab + [c["maskU4"].b], [at.b])
        if i + 1 < NT:
            low(i + 1)
        pkvs = []
        if i < NT - 1:
            for hh in range(4):
                sl = slice(hh * 128, (hh + 1) * 128)
                vs = slice(hh * 512, (hh + 1) * 512)
                pkv, pkvb = k.banks(4 + hh)
                k.mm(pkv, ke.ap[:, sl], v.ap[:, vs], True, True, [ke.b, v.b], pkvb)
                pkvs.append((pkv, pkvb))
        po, pob = k.banks(0, 4)
        for hh in range(4):
            sl = slice(hh * 128, (hh + 1) * 128)
            vs = slice(hh * 512, (hh + 1) * 512)
            k.mm(po[:, vs], at.ap[:, sl], v.ap[:, vs], True, False, [at.b, v.b], [pob[hh]])
            k.mm(po[:, vs], qt_.ap[:, sl], Sb[hh].ap, False, True, [qt_.b, Sb[hh].b], [pob[hh]])
        for hh, (pkv, pkvb) in enumerate(pkvs):
            k.stt("dve", Sf[hh].ap, Sf[hh].ap, ebT.ap[:, hh * 128 + 127:hh * 128 + 128], pkv, ALU.mult, ALU.add,
                  [Sf[hh].b, ebT.b] + pkvb, [Sf[hh].b])
            k.copy("pool", Sb[hh].ap, Sf[hh].ap, [Sf[hh].b], [Sb[hh].b])
        o_ = og[p]
        for hh in range(4):
            vs = slice(hh * 512, (hh + 1) * 512)
            sh = smh[hh]
            k.act(junk.ap[:, 0:512], po[:, vs], AF.Square, [pob[hh]], [junk.b, sh.b], accum=sh.ap[:, 0:1])
            k.copy("dve", osb.ap[:, vs], po[:, vs], [pob[hh]], [osb.b])
            k.rstd(sh.ap[:, 0:1], sh.ap[:, 1:2], sh.ap[:, 2:3], 512, eps.ap[:, 0:1], [sh.b, eps.b], [sh.b])
        for hh in range(4):
            vs = slice(hh * 512, (hh + 1) * 512)
            k.stt("dve", o_.ap[:, vs], osb.ap[:, vs], smh[hh].ap[:, 2:3], sn.ap[:, vs], ALU.mult, ALU.mult,
                  [osb.b, smh[hh].b, sn.b], [o_.b])
        prev_tail[0] = (i, o_)
    tail(*prev_tail[0])
    S.barrier()
    k.AF.release(mF)
    k.AB.release(mB)


def phase_F(k, c, dr, w_out_d, post_row, xin_d, out_d, pre_row=None, h1T=None):
    S = k.S
    mF, mB = k.AF.mark(), k.AB.mark()
    w_out = k.b16(16 * D, "w_out")
    wo3 = v3(w_out.ap, 16)
    wbuf = [S.buf("w_out_p%d" % j) for j in range(8)]
    stg = [k.f32(2 * D, "stgF%d" % i) for i in range(2)]
    for j in range(8):
        st = stg[j % 2]
        S.dma("sp", v3(st.ap, 2), w_out_d[j * 256:(j + 1) * 256, :].rearrange("(c p) n -> p c n", p=128), W=[st.b])
        k.copy(("dve", "act")[j % 2], w_out.ap[:, j * 2 * D:(j + 1) * 2 * D], st.ap, [st.b], [wbuf[j]])
    postg = k.f32(D, "postg")
    bcast_load(k, postg.ap, postg.b, post_row)
    if h1T is not None:
        preg = k.f32(D, "preg")
        bcast_load(k, preg.ap, preg.b, pre_row)
        h1 = [k.b16(D, "h1_%d" % i) for i in range(2)]
    ogs = [k.b16(BR, "ogs%d" % i) for i in range(4)]
    xs = [k.f32(D, "xsF%d" % i) for i in range(4)]
    tmp = [k.f32(D, "tmpF%d" % i) for i in range(2)]
    x1 = [k.f32(D, "x1F%d" % i) for i in range(2)]
    sm = [k.f32(16, "smF%d" % i) for i in range(2)]
    junk, eps, ident = c["junk"], c["eps"], c["ident"]

    def load(i):
        o = ogs[i % 4]
        S.dma("sp", o.ap, dr["ogT"][i], W=[o.b])
        x = xs[i % 4]
        S.dma("pool", x.ap, xin_d[i * 128:(i + 1) * 128, :], W=[x.b])

    def ymm(i):
        o = ogs[i % 4]
        py, pyb = k.banks(2 * (i % 3), 2)
        for half in range(2):
            for cc in range(16):
                k.mm(py[:, half * 512:(half + 1) * 512], o.ap[:, cc * 128:(cc + 1) * 128],
                     wo3[:, cc, half * 512:(half + 1) * 512], cc == 0, cc == 15, [o.b, wbuf[cc // 2]], [pyb[half]])
        return py, pyb

    load(0)
    load(1)
    load(2)
    yq = [ymm(0), ymm(1)]
    for i in range(NT):
        if i + 3 < NT:
            load(i + 3)
        x, s = xs[i % 4], sm[i % 2]
        py, pyb = yq.pop(0)
        if i + 2 < NT:
            yq.append(ymm(i + 2))
        k.act(junk.ap[:, 0:D], py, AF.Square, pyb, [junk.b, s.b], accum=s.ap[:, 0:1])
        k.rstd(s.ap[:, 0:1], s.ap[:, 1:2], s.ap[:, 2:3], D, eps.ap[:, 0:1], [s.b, eps.b], [s.b])
        t = tmp[i % 2]
        k.stt("dve", t.ap, py, s.ap[:, 2:3], postg.ap, ALU.mult, ALU.mult, pyb + [s.b, postg.b], [t.b])
        xo = x1[i % 2]
        k.tt("pool", xo.ap, t.ap, x.ap, ALU.add, [t.b, x.b], [xo.b])
        S.dma("sp", out_d[i * 128:(i + 1) * 128, :], xo.ap, R=[xo.b])
        if h1T is not None:
            k.act(junk.ap[:, 0:D], xo.ap, AF.Square, [xo.b], [junk.b, s.b], accum=s.ap[:, 4:5])
            k.rstd(s.ap[:, 4:5], s.ap[:, 5:6], s.ap[:, 6:7], D, eps.ap[:, 0:1], [s.b, eps.b], [s.b])
            hh = h1[i % 2]
            k.stt("dve", hh.ap, xo.ap, s.ap[:, 6:7], preg.ap, ALU.mult, ALU.mult, [xo.b, s.b, preg.b], [hh.b])
            pb, pbb = k.banks(6 + (i % 2))
            pv = pb.bitcast(BF16)
            for kc in range(8):
                k.tr(pv[:, kc * 128:(kc + 1) * 128], hh.ap[:, kc * 128:(kc + 1) * 128], ident.ap, [hh.b, ident.b], pbb)
            k.copy("act", v3(h1T.ap, 8)[:, :, i * 128:(i + 1) * 128], v3(pv, 8), pbb, [h1T.b])
    S.barrier()
    k.AF.release(mF)
    k.AB.release(mB)


def phase_A1(k, c, dr, h1T):
    S = k.S
    mF, mB = k.AF.mark(), k.AB.mark()
    h3 = v3(h1T.ap, 8)
    cst = [k.f32(512, "cosT%d" % i) for i in range(2)]
    snt = [k.f32(512, "sinT%d" % i) for i in range(2)]
    lv = k.f32(256, "lv")
    for j, nm in enumerate(("diff_lam_q1", "diff_lam_k1", "diff_lam_q2", "diff_lam_k2")):
        bcast_load(k, lv.ap[:, j * 64:(j + 1) * 64], lv.b, dr[nm])
    sc = k.f32(16, "scA1")
    junkf = k.f32(64, "junkf")
    k.stt("dve", junkf.ap, lv.ap[:, 0:64], 1.0, lv.ap[:, 64:128], ALU.mult, ALU.mult, [lv.b], [junkf.b, sc.b],
          accum=sc.ap[:, 0:1])
    k.stt("dve", junkf.ap, lv.ap[:, 128:192], 1.0, lv.ap[:, 192:256], ALU.mult, ALU.mult, [lv.b], [junkf.b, sc.b],
          accum=sc.ap[:, 1:2])
    k.act(sc.ap[:, 2:4], sc.ap[:, 0:2], AF.Exp, [sc.b], [sc.b])
    k.tt("dve", sc.ap[:, 4:5], sc.ap[:, 3:4], sc.ap[:, 2:3], ALU.subtract, [sc.b], [sc.b])
    k.ts("dve", sc.ap[:, 5:6], sc.ap[:, 4:5], -LAM_INIT, ALU.add, [sc.b], [sc.b])
    S.dma("sp", sc.ap[:, 6:7], dr["diff_norm_g"].rearrange("o v -> v o"), W=[sc.b])
    k.ts("dve", sc.ap[:, 7:8], sc.ap[:, 6:7], 1.0 - LAM_INIT, ALU.mult, [sc.b], [sc.b])
    neglam = sc.ap[:, 5:6]
    gcol = sc.ap[:, 7:8]

    stg = [k.f32(4 * 512, "stgA%d" % i) for i in range(2)]
    wh = [k.b16(8 * 512, "wh%d" % i) for i in range(2)]
    QT = k.b16(T, "QT")
    KT = k.b16(T, "KT")
    V = k.b16(T, "V")
    sgT = k.b16(T, "sgT")
    qraw = [k.b16(512, "qraw%d" % i) for i in range(2)]
    t1 = [k.f32(512, "t1_%d" % i) for i in range(2)]
    t2 = [k.f32(512, "t2_%d" % i) for i in range(2)]
    PT = [k.b16(1024, "PT%d" % i) for i in range(3)]
    rlb = k.b16(512, "rlb")
    lnl = k.f32(1024, "lnl")
    o01 = k.f32(1024, "o01")
    of = k.f32(512, "of")
    sq = k.b16(512, "sq")
    lnt = k.f32(512, "lnt")
    rst = k.f32(512, "rst")
    tf = k.f32(512, "tf")
    ogh = [k.b16(512, "ogh%d" % i) for i in range(2)]
    ones, rperm, maskT2, eps = c["ones"], c["rperm"], c["maskT2"], c["eps"]
    ident, negm = c["ident"], c["negm"]
    pO, pOb = k.banks(4, 2)
    pL, pLb = k.banks(6, 1)
    pX, pXb = k.banks(7, 1)
    selb = c["selb"]
    nq = 0
    npt = 0
    nog = 0

    def dma_w(h):
        for half in range(2):
            st = stg[half]
            s3 = v3(st.ap, 4)
            for blk in range(4):
                S.dma("sp", s3[:, :, blk * 128:(blk + 1) * 128],
                      dr["diff_w_in"][half * 512:(half + 1) * 512, blk * BR + h * 128:blk * BR + (h + 1) * 128]
                      .rearrange("(k p) n -> p k n", p=128), W=[st.b])

    def cast_w(h):
        for half in range(2):
            st = stg[half]
            k.copy("dve", wh[h % 2].ap[:, half * 2048:(half + 1) * 2048], st.ap, [st.b], [wh[h % 2].b])

    ntab = [0]

    def load_tab(tt_):
        j = tt_ % 2
        S.dma("sp", cst[j].ap, dr["cosT"][:, tt_ * 512:(tt_ + 1) * 512], W=[cst[j].b])
        S.dma("sp", snt[j].ap, dr["sinT"][:, tt_ * 512:(tt_ + 1) * 512], W=[snt[j].b])

    dma_w(0)
    cast_w(0)
    load_tab(0)
    load_tab(1)
    pend = []
    for h in range(DBG["heads"]):
        w3 = v3(wh[h % 2].ap, 8)
        wb = wh[h % 2].b
        for tt_ in range(8 if DBG["inproj"] else 0):
            ts_ = slice(tt_ * 512, (tt_ + 1) * 512)
            cosT, sinT = cst[tt_ % 2], snt[tt_ % 2]
            if tt_ >= 1:
                for _ in range(3):
                    if pend:
                        _, fn, q_, h_ = pend.pop(0)
                        fn(q_, h_)
            pqs = []
            for blk in (0, 1):
                pq, pqb = k.nextbank(1, 0, 7)
                for kc in range(8):
                    k.mm(pq, w3[:, kc, blk * 128:(blk + 1) * 128], h3[:, kc, ts_], kc == 0, kc == 7, [wb, h1T.b], pqb)
                pqs.append((pq, pqb))
            pg, pgb = k.nextbank(1, 0, 7)
            for kc in range(8):
                k.mm(pg, w3[:, kc, 384:512], h3[:, kc, ts_], kc == 0, kc == 7, [wb, h1T.b], pgb)
            pv, pvb = k.nextbank(1, 0, 7)
            for s_ in range(4):
                for kc in range(8):
                    k.mm(pv[:, s_ * 128:(s_ + 1) * 128], h3[:, kc, tt_ * 512 + s_ * 128:tt_ * 512 + (s_ + 1) * 128],
                         w3[:, kc, 256:384], kc == 0, kc == 7, [wb, h1T.b], pvb)
            rots = []
            for blk, dst in ((0, QT), (1, KT)):
                pq, pqb = pqs[blk]
                qr = qraw[nq % 2]
                a1, a2 = t1[nq % 2], t2[nq % 2]
                nq += 1
                k.copy("act", qr.ap, pq, pqb, [qr.b])
                rots.append((pq, pqb, qr, a1, a2, dst))
            k.act(sgT.ap[:, ts_], pg, AF.Silu, pgb, [sgT.b])
            k.copy("act", V.ap[:, ts_], pv, pvb, [V.b])
            for (pq, pqb, qr, a1, a2, dst) in rots:
                pr, prb = k.nextbank(1, 0, 7)
                k.mm(pr, rperm.ap, qr.ap, True, True, [rperm.b, qr.b], prb)
                k.tt("dve", a1.ap, pq, cosT.ap, ALU.mult, pqb + [cosT.b], [a1.b])
                k.tt("dve", a2.ap, pr, sinT.ap, ALU.mult, prb + [sinT.b], [a2.b])
                k.tt("pool", dst.ap[:, ts_], a1.ap, a2.ap, ALU.add, [a1.b, a2.b], [dst.b])
            if tt_ + 2 < 8:
                load_tab(tt_ + 2)
        if h + 1 < 16:
            dma_w(h + 1)
        blocks = [(qt, kb) for qt in range(DBG["nqt"] if DBG["att"] else 0) for kb in range(4 * qt + 4)]

        def qk_exp(qt, kb):
            nonlocal npt
            j = kb - 4 * qt
            c0 = max(j, 0) * 128
            ps2, ps2b = k.banks(2 * (npt % 2), 2)
            pt_ = PT[npt % 3]
            npt += 1
            for cc in range(2):
                k.mm(ps2[:, cc * 512 + c0:(cc + 1) * 512], KT.ap[cc * 64:(cc + 1) * 64, kb * 128:(kb + 1) * 128],
                     QT.ap[cc * 64:(cc + 1) * 64, qt * 512 + c0:(qt + 1) * 512], True, j < 0, [KT.b, QT.b],
                     [ps2b[cc]])
            if j >= 0:
                for cc in range(2):
                    k.mm(ps2[:, cc * 512 + c0:cc * 512 + c0 + 128], ident.ap, negm.ap, False, True,
                         [ident.b, negm.b], [ps2b[cc]])
            p3 = v3(pt_.ap, 2)
            k.act(p3[:, :, c0:512], v3(ps2, 2)[:, :, c0:512], AF.Exp, ps2b, [pt_.b], scale=0.125)
            return pt_, c0

        def av(qt, kb, pt_, c0):
            nkb = 4 * qt + 4
            for cc in range(2):
                k.mmt(pL[32 * cc:32 * cc + 32, c0:512], ones.ap[:, 0:32], pt_.ap[:, cc * 512 + c0:(cc + 1) * 512],
                      kb == 0, kb == nkb - 1, (0, 32 * cc), [ones.b, pt_.b], pLb)
            for cc in range(2):
                k.mm(pO[:, cc * 512 + c0:(cc + 1) * 512], V.ap[:, kb * 128:(kb + 1) * 128],
                     pt_.ap[:, cc * 512 + c0:(cc + 1) * 512], kb == 0, kb == nkb - 1, [V.b, pt_.b], [pOb[cc]])

        def epi1(qt, hd):
            k.act(lnl.ap[0:64, 0:512], pL[0:64, :], AF.Ln, pLb, [lnl.b])
            k.copy("dve", o01.ap[:, 0:512], pO[:, 0:512], [pOb[0]], [o01.b])
            k.copy("dve", o01.ap[:, 512:1024], pO[:, 512:1024], [pOb[1]], [o01.b])
            k.act(rlb.ap[0:64, 0:512], lnl.ap[0:64, 0:512], AF.Exp, [lnl.b], [rlb.b], scale=-1.0)

        def st_b0(qt, hd):
            k.mm(pX, selb.ap[0:64, 0:128], rlb.ap[0:64, 0:512], True, True, [selb.b, rlb.b], pXb)

        def st_m0(qt, hd):
            k.tt("dve", o01.ap[:, 0:512], o01.ap[:, 0:512], pX, ALU.mult, [o01.b] + pXb, [o01.b])

        def st_b1(qt, hd):
            k.mm(pX, selb.ap[0:64, 128:256], rlb.ap[0:64, 0:512], True, True, [selb.b, rlb.b], pXb)

        def st_m1(qt, hd):
            k.tt("dve", o01.ap[:, 512:1024], o01.ap[:, 512:1024], pX, ALU.mult, [o01.b] + pXb, [o01.b])
            k.stt("dve", of.ap, o01.ap[:, 512:1024], neglam, o01.ap[:, 0:512], ALU.mult, ALU.add, [o01.b, sc.b], [of.b])

        def st_sq(qt, hd):
            k.act(sq.ap, of.ap, AF.Square, [of.b], [sq.b])

        def st_ss(qt, hd):
            k.mm(pX, ones.ap, sq.ap, True, True, [ones.b, sq.b], pXb)

        def st_ln(qt, hd):
            k.act(lnt.ap, pX, AF.Ln, pXb + [eps.b], [lnt.b], scale=1.0 / 128, bias=eps.ap[:, 0:1])

        def st_ex(qt, hd):
            k.act(rst.ap, lnt.ap, AF.Exp, [lnt.b], [rst.b], scale=-0.5)

        def st_fin(qt, hd):
            nonlocal nog
            qs = slice(qt * 512, (qt + 1) * 512)
            k.tt("dve", tf.ap, of.ap, rst.ap, ALU.mult, [of.b, rst.b], [tf.b])
            og_ = ogh[nog % 2]
            nog += 1
            k.stt("dve", og_.ap, tf.ap, gcol, sgT.ap[:, qs], ALU.mult, ALU.mult, [tf.b, sc.b, sgT.b], [og_.b])
            S.dma("sp", dr["ogT"][qt * 4:(qt + 1) * 4, :, hd * 128:(hd + 1) * 128].rearrange("s p v -> p s v"),
                  v3(og_.ap, 4), R=[og_.b])

        STAGES0 = ((2, st_b0), (3, st_m0), (4, st_b1), (5, st_m1), (6, st_sq), (7, st_ss), (8, st_ln), (8, st_ex),
                   (8, st_fin))
        STAGES = ((2, st_b0), (3, st_m0), (4, st_b1), (5, st_m1), (6, st_sq), (7, st_ss), (8, st_ln), (9, st_ex),
                  (10, st_fin))

        cur = qk_exp(*blocks[0]) if blocks else None
        for bi, (qt, kb) in enumerate(blocks):
            nxt = qk_exp(*blocks[bi + 1]) if bi + 1 < len(blocks) else None
            av(qt, kb, *cur)
            while pend and bi >= pend[0][0]:
                _, fn, q_, h_ = pend.pop(0)
                fn(q_, h_)
            if kb == 4 * qt + 3 and DBG["epi"]:
                epi1(qt, h)
                for off, fn in (STAGES0 if qt == 0 else STAGES):
                    pend.append((bi + off, fn, qt, h))
            if bi == 10 and h + 1 < 16:
                cast_w(h + 1)
            if bi == 20 and h + 1 < 16:
                load_tab(0)
                load_tab(1)
            cur = nxt
        pend = [(-1, fn, q_, h_) for (_, fn, q_, h_) in pend]
    for _, fn, q_, h_ in pend:
        fn(q_, h_)
    S.barrier()
    k.AF.release(mF)
    k.AB.release(mB)


NAF = 16 * 1024
NAB = 68 * 1024

_INPUT_NAMES = ("x", "pre_g", "post_g", "gla_w_in", "gla_w_g2", "gla_b_g", "gla_norm_g", "gla_w_out",
                "diff_w_in", "diff_lam_q1", "diff_lam_k1", "diff_lam_q2", "diff_lam_k2", "diff_norm_g", "diff_w_out")
_SHAPES = {
    "x": [T, D], "pre_g": [2, D], "post_g": [2, D], "gla_w_in": [D, GIN], "gla_w_g2": [16, 512],
    "gla_b_g": [1, 512], "gla_norm_g": [1, 512], "gla_w_out": [BR, D], "diff_w_in": [D, 4 * BR],
    "diff_lam_q1": [1, 64], "diff_lam_k1": [1, 64], "diff_lam_q2": [1, 64], "diff_lam_k2": [1, 64],
    "diff_norm_g": [1, 128], "diff_w_out": [BR, D],
}


def build(mode="both"):
    nc = bass.Bass("TRN2", target_bir_lowering=False)
    dr = {}
    for nm in _INPUT_NAMES:
        if mode == "L1" and nm == "x":
            continue
        dr[nm] = nc.dram_tensor(nm, _SHAPES[nm], F32, kind="ExternalInput").ap()
    dr["cst_bf"] = nc.dram_tensor("cst_bf", [128, 1024], BF16, kind="ExternalInput").ap()
    dr["cst_f"] = nc.dram_tensor("cst_f", [128, 1024], F32, kind="ExternalInput").ap()
    dr["cosT"] = nc.dram_tensor("cosT", [128, T], F32, kind="ExternalInput").ap()
    dr["sinT"] = nc.dram_tensor("sinT", [128, T], F32, kind="ExternalInput").ap()
    dr["ogT"] = nc.dram_tensor("ogT_scr", [NT, 128, BR], BF16, kind="Internal").ap()
    if mode == "both":
        dr["x1"] = nc.dram_tensor("x1_scr", [T, D], F32, kind="Internal").ap()
        dr["out"] = nc.dram_tensor("out", [T, D], F32, kind="ExternalOutput").ap()
    elif mode == "L0":
        dr["x1"] = nc.dram_tensor("x1", [T, D], F32, kind="ExternalOutput").ap()
        dr["h1T_d"] = nc.dram_tensor("h1T", [128, 8 * T], BF16, kind="ExternalOutput").ap()
    else:
        dr["x1"] = nc.dram_tensor("x1", [T, D], F32, kind="ExternalInput").ap()
        dr["h1T_d"] = nc.dram_tensor("h1T", [128, 8 * T], BF16, kind="ExternalInput").ap()
        dr["out"] = nc.dram_tensor("out", [T, D], F32, kind="ExternalOutput").ap()

    with ExitStack() as st:
        S = Sched(nc, st)
        af = st.enter_context(nc.sbuf_tensor("arena_f", [128, NAF], F32))
        ab = st.enter_context(nc.sbuf_tensor("arena_b", [128, NAB], BF16))
        ps = st.enter_context(nc.psum_tensor("ps", [128, 4096], F32))
        k = K(nc, S, Arena(af, NAF), Arena(ab, NAB), ps)
        c = load_consts(k, dr)
        if mode in ("both", "L0"):
            phase_A0(k, c, dr)
        h1T = k.b16(8 * T, "h1T")
        if mode in ("both", "L0"):
            phase_F(k, c, dr, dr["gla_w_out"], dr["post_g"][0:1, :], dr["x"], dr["x1"],
                    pre_row=dr["pre_g"][1:2, :], h1T=h1T)
        if mode == "L0":
            for j in range(8):
                S.dma("sp", dr["h1T_d"][:, j * T:(j + 1) * T], h1T.ap[:, j * T:(j + 1) * T], R=[h1T.b])
            S.barrier()
        if mode == "L1":
            for j in range(8):
                S.dma("sp", h1T.ap[:, j * T:(j + 1) * T], dr["h1T_d"][:, j * T:(j + 1) * T], W=[h1T.b])
        if mode in ("both", "L1"):
            phase_A1(k, c, dr, h1T)
            if DBG["F1"]:
                phase_F(k, c, dr, dr["diff_w_out"], dr["post_g"][1:2, :], dr["x1"], dr["out"])
        S.barrier()
        print("ops", S.nops, "arena peaks f32 %d bf16 %d" % (k.AF.peak, k.AB.peak), "sems", sum(1 for b in S.allbufs if b.dsem is not None))
        with nc.Block() as block:
            S.replay(block)
    return nc


def _consts():
    bf = ml_dtypes.bfloat16
    p = np.arange(128)
    ident = np.eye(128, dtype=np.float32)
    perm = np.where((p % 64) < 32, p + 32, p - 32)
    rperm = np.zeros((128, 128), np.float32)
    rperm[perm, p] = 1.0
    ones = np.ones((128, 128), np.float32)
    maskT = (p[None, :] >= p[:, None]).astype(np.float32)
    negm = np.where(p[None, :] >= p[:, None], 0.0, -30000.0).astype(np.float32)
    selb = np.zeros((128, 256), np.float32)
    selb[0, 0:128] = 1.0
    selb[32, 128:256] = 1.0
    cst_bf = np.concatenate([ident, rperm, ones, maskT, maskT, negm, selb], 1).astype(bf)
    triU = np.where(p[:, None] <= p[None, :], -1.0 / 16.0, 0.0).astype(np.float32)
    triL = np.where(p[:, None] > p[None, :], -1.0 / 16.0, 0.0).astype(np.float32)
    maskU = (p[:, None] <= p[None, :]).astype(np.float32)
    sel = np.zeros((128, 256), np.float32)
    sel[0, 0:128] = 1.0
    sel[32, 128:256] = 1.0
    cst_f = np.concatenate([triU, triL, maskU, maskU, maskU, maskU, sel], 1).astype(np.float32)
    inv_freq = (1.0 / (np.float32(10000.0) ** (np.arange(0, 64, 2, dtype=np.float32) / np.float32(64)))).astype(np.float32)
    pos = np.arange(T, dtype=np.float32)
    ang = (pos[:, None] * inv_freq[None, :]).astype(np.float32)
    cos = np.cos(ang).astype(np.float32).T
    sin = np.sin(ang).astype(np.float32).T
    cosT = np.concatenate([cos, cos, cos, cos], 0)
    sinT = np.concatenate([-sin, sin, -sin, sin], 0)
    return {"cst_bf": np.ascontiguousarray(cst_bf), "cst_f": np.ascontiguousarray(cst_f),
            "cosT": np.ascontiguousarray(cosT), "sinT": np.ascontiguousarray(sinT)}


_NC_CACHE = {}


def _get_nc(mode):
    if mode not in _NC_CACHE:
        _NC_CACHE[mode] = build(mode)
    return _NC_CACHE[mode]


def _shared_maps(inputs):
    m = dict(_consts())
    for nm in _INPUT_NAMES:
        if nm == "x":
            continue
        a = np.asarray(inputs[nm], dtype=np.float32)
        if nm in ("pre_g", "post_g"):
            m[nm] = np.ascontiguousarray(a)
        else:
            m[nm] = np.ascontiguousarray(a.reshape(_SHAPES[nm]))
    return m


FUSED = True


def kernel(**inputs):
    x = np.asarray(inputs["x"], dtype=np.float32)
    shared = _shared_maps(inputs)
    cores = list(range(NCORES))
    if FUSED:
        nc = _get_nc("both")
        in_maps = [dict(shared, x=np.ascontiguousarray(x[b])) for b in cores]
        res = run_bass_kernel_spmd(nc, in_maps, core_ids=cores)
        return np.stack([np.asarray(r["out"]) for r in res.results], 0).astype(np.float32)
    nc0 = _get_nc("L0")
    in_maps = [dict(shared, x=np.ascontiguousarray(x[b])) for b in cores]
    r0 = run_bass_kernel_spmd(nc0, in_maps, core_ids=cores).results
    nc1 = _get_nc("L1")
    in_maps = [dict(shared, x1=np.asarray(r0[b]["x1"]), h1T=np.asarray(r0[b]["h1T"])) for b in cores]
    r1 = run_bass_kernel_spmd(nc1, in_maps, core_ids=cores).results
    return np.stack([np.asarray(r["out"]) for r in r1], 0).astype(np.float32)
```

```python
import math
from contextlib import ExitStack

import numpy as np
import ml_dtypes
import concourse.bass as bass
import concourse.mybir as mybir
from concourse.bass_utils import run_bass_kernel_spmd

F32 = mybir.dt.float32
BF16 = mybir.dt.bfloat16
AF = mybir.ActivationFunctionType
ALU = mybir.AluOpType

T = 4096
D = 1024
BR = 2048
NT = T // 128
GIN = 5136
EPS = 1e-6
LAM_INIT = 0.8 - 0.6 * math.exp(-0.3 * 1)
NCORES = 8
DBG = {"heads": 16, "nqt": 8, "inproj": True, "att": True, "epi": True, "F1": True, "nkb": 99, "ipp": "qgv", "rope": True}

ENGS = ("pe", "act", "dve", "pool", "sp")


class Buf:
    __slots__ = ("name", "w", "r", "dsem", "dcnt", "excl")

    def __init__(self, name):
        self.name = name
        self.excl = False
        self.w = None
        self.r = []
        self.dsem = None
        self.dcnt = 0


class Rec:
    __slots__ = ("waits", "fn", "inc", "dma_inc", "pos", "val")

    def __init__(self, waits, fn):
        self.waits = waits
        self.fn = fn
        self.inc = False
        self.dma_inc = None
        self.pos = -1
        self.val = 0


class Sched:
    def __init__(self, nc, stack):
        self.nc = nc
        self.stack = stack
        self.ops = {e: [] for e in ENGS}
        self.sem = {e: stack.enter_context(nc.semaphore("sem_" + e)) for e in ENGS}
        self.waited = {e: {} for e in ENGS}
        self.allbufs = []
        self.same_engine_sync = True
        self.same_engine_all = True
        self.nops = 0
        self.nbar = 0

    def buf(self, name):
        b = Buf("%s_%d" % (name, len(self.allbufs)))
        self.allbufs.append(b)
        return b

    def _dsem(self, b):
        if b.dsem is None:
            b.dsem = self.stack.enter_context(self.nc.semaphore("d_" + b.name))
        return b.dsem

    def _push(self, eng, rec):
        rec.pos = len(self.ops[eng])
        self.ops[eng].append(rec)

    def _resolve(self, eng, tok, waits, is_raw):
        if tok[0] == "e":
            _, e2, rec = tok
            if e2 == eng:
                if eng in ("pe", "sp") or not (self.same_engine_sync and (is_raw or self.same_engine_all)):
                    return
            key = "e_" + e2
            if self.waited[eng].get(key, -1) >= rec.pos:
                return
            self.waited[eng][key] = rec.pos
            rec.inc = True
            waits.append(("e", e2, rec))
        else:
            _, b, val = tok
            key = "d_" + b.name
            if self.waited[eng].get(key, 0) >= val:
                return
            self.waited[eng][key] = val
            waits.append(("d", b.dsem, val))

    def _deps(self, eng, R, W):
        waits = []
        for b in R:
            if b.w is not None:
                self._resolve(eng, b.w, waits, True)
            if b.excl:
                for t in b.r:
                    if t[0] != "e" or t[1] != eng:
                        self._resolve(eng, t, waits, False)
        for b in W:
            if b.w is not None:
                self._resolve(eng, b.w, waits, False)
            for t in b.r:
                self._resolve(eng, t, waits, False)
        return waits

    def op(self, eng, fn, R=(), W=()):
        waits = self._deps(eng, R, W)
        rec = Rec(waits, fn)
        self._push(eng, rec)
        tok = ("e", eng, rec)
        for b in R:
            b.r = [t for t in b.r if not (t[0] == "e" and t[1] == eng)]
            b.r.append(tok)
        for b in W:
            b.w = tok
            b.r = []
        self.nops += 1
        return rec

    def dma(self, q, out_ap, in_ap, R=(), W=()):
        waits = self._deps(q, R, W)
        owner = W[0] if W else R[0]
        sem = self._dsem(owner)
        if not W and owner.dcnt > 0:
            self._resolve(q, ("d", owner, owner.dcnt), waits, False)
        owner.dcnt += 16
        tok = ("d", owner, owner.dcnt)
        rec = Rec(waits, lambda e: e.dma_start(out=out_ap, in_=in_ap))
        rec.dma_inc = sem
        self._push(q, rec)
        for b in R:
            b.r.append(tok)
        for b in W:
            b.w = tok
            b.r = []
        self.nops += 1
        return rec

    def barrier(self):
        waits = []
        comp = ("pe", "act", "dve", "pool")
        last = {}
        for e in comp:
            recs = [r for r in self.ops[e] if r.fn is not None]
            if recs:
                last[e] = recs[-1]
                if self.waited["sp"].get("e_" + e, -1) < recs[-1].pos:
                    self.waited["sp"]["e_" + e] = recs[-1].pos
                    recs[-1].inc = True
                    waits.append(("e", e, recs[-1]))
        for b in self.allbufs:
            if b.dsem is not None and b.dcnt > self.waited["sp"].get("d_" + b.name, 0):
                self.waited["sp"]["d_" + b.name] = b.dcnt
                waits.append(("d", b.dsem, b.dcnt))
        sem_sp = self.sem["sp"]
        rec = Rec(waits, lambda e: e.sem_inc(sem_sp, 1))
        self._push("sp", rec)
        self.nbar += 1
        v = self.nbar
        for e in comp:
            self._push(e, Rec([("s", sem_sp, v)], None))
            for e2 in comp:
                if e2 in last:
                    self.waited[e]["e_" + e2] = last[e2].pos
            for b in self.allbufs:
                if b.dsem is not None:
                    self.waited[e]["d_" + b.name] = b.dcnt
        for b in self.allbufs:
            b.w = None
            b.r = []

    def replay(self, block):
        S = self
        for eng in ENGS:
            n = 0
            for rec in S.ops[eng]:
                if rec.inc:
                    n += 1
                    rec.val = n
        self.nsig = {eng: sum(1 for r in S.ops[eng] if r.inc) for eng in ENGS}

        def run(eng, e):
            for rec in S.ops[eng]:
                for w in rec.waits:
                    if w[0] == "e":
                        assert w[2].inc and w[2].val > 0
                        e.wait_ge(S.sem[w[1]], w[2].val)
                    else:
                        e.wait_ge(w[1], w[2])
                if rec.fn is None:
                    continue
                ins = rec.fn(e)
                if rec.dma_inc is not None:
                    ins.then_inc(rec.dma_inc, 16)
                elif rec.inc:
                    ins.then_inc(S.sem[eng], 1)

        @block.tensor
        def _(e):
            run("pe", e)

        @block.scalar
        def _(e):
            run("act", e)

        @block.vector
        def _(e):
            run("dve", e)

        @block.gpsimd
        def _(e):
            run("pool", e)

        @block.sync
        def _(e):
            run("sp", e)


class Arena:
    def __init__(self, ap, n):
        self.ap = ap
        self.n = n
        self.top = 0
        self.peak = 0

    def alloc(self, n):
        a = self.ap[:, self.top:self.top + n]
        self.top += (n + 15) // 16 * 16
        self.peak = max(self.peak, self.top)
        assert self.top <= self.n, ("arena overflow", self.top, self.n)
        return a

    def mark(self):
        return self.top

    def release(self, m):
        self.top = m


class Tile:
    __slots__ = ("ap", "b")

    def __init__(self, ap, b):
        self.ap = ap
        self.b = b


class K:
    def __init__(self, nc, S, AFa, ABa, ps):
        self.nc, self.S, self.AF, self.AB, self.ps = nc, S, AFa, ABa, ps
        self.bankb = [S.buf("bank%d" % i) for i in range(8)]
        for b in self.bankb:
            b.excl = True
        self.rr = 0
        self.cast_rr = 0

    def f32(self, n, name):
        return Tile(self.AF.alloc(n), self.S.buf(name))

    def b16(self, n, name):
        return Tile(self.AB.alloc(n), self.S.buf(name))

    def banks(self, i, n=1):
        return self.ps[:, i * 512:(i + n) * 512], [self.bankb[j] for j in range(i, i + n)]

    def nextbank(self, n=1, lo=0, hi=8):
        if self.rr < lo or self.rr + n > hi:
            self.rr = lo
        i = self.rr
        self.rr += n
        return self.banks(i, n)

    def mm(self, out, lhsT, rhs, start, stop, R, W):
        self.S.op("pe", lambda e: e.matmul(out, lhsT=lhsT, rhs=rhs, start=start, stop=stop), R, W)

    def mmt(self, out, lhsT, rhs, start, stop, tpos, R, W):
        self.S.op("pe", lambda e: e.matmul(out, lhsT=lhsT, rhs=rhs, start=start, stop=stop, tile_position=tpos), R, W)

    def tr(self, out, in_, ident, R, W):
        self.S.op("pe", lambda e: e.transpose(out=out, in_=in_, identity=ident), R, W)

    def act(self, out, in_, func, R, W, scale=None, bias=None, accum=None):
        kw = {}
        if scale is not None:
            kw["scale"] = scale
        if bias is not None:
            kw["bias"] = bias
        if accum is not None:
            kw["accum_out"] = accum
        self.S.op("act", lambda e: e.activation(out=out, in_=in_, func=func, **kw), R, W)

    def tt(self, eng, out, a, b, op, R, W):
        self.S.op(eng, lambda e: e.tensor_tensor(out=out, in0=a, in1=b, op=op), R, W)

    def stt(self, eng, out, a, sc, b, op0, op1, R, W, accum=None):
        if accum is None:
            self.S.op(eng, lambda e: e.scalar_tensor_tensor(out=out, in0=a, scalar=sc, in1=b, op0=op0, op1=op1), R, W)
        else:
            self.S.op(eng, lambda e: e.scalar_tensor_tensor(out=out, in0=a, scalar=sc, in1=b, op0=op0, op1=op1,
                                                           accum_out=accum), R, W)

    def ts(self, eng, out, a, s1, op0, R, W, s2=None, op1=None):
        if s2 is None:
            self.S.op(eng, lambda e: e.tensor_scalar(out=out, in0=a, scalar1=s1, scalar2=None, op0=op0), R, W)
        else:
            self.S.op(eng, lambda e: e.tensor_scalar(out=out, in0=a, scalar1=s1, scalar2=s2, op0=op0, op1=op1), R, W)

    def copy(self, eng, out, in_, R, W):
        if eng == "act":
            self.S.op("act", lambda e: e.copy(out=out, in_=in_), R, W)
        else:
            self.S.op(eng, lambda e: e.tensor_copy(out=out, in_=in_), R, W)

    def cast_any(self, out, in_, R, W):
        eng = ("pool", "dve", "act")[self.cast_rr % 3]
        self.cast_rr += 1
        self.copy(eng, out, in_, R, W)

    def memset(self, eng, ap, val, W):
        self.S.op(eng, lambda e: e.memset(ap, val), (), W)

    def rstd(self, ssq, tmp, out, n, eps_ap, R, W):
        self.act(tmp, ssq, AF.Ln, R, W, scale=1.0 / n, bias=eps_ap)
        self.act(out, tmp, AF.Exp, W, W, scale=-0.5)


def v3(ap, a):
    return ap.rearrange("p (a b) -> p a b", a=a)


def load_consts(k, dr):
    S = k.S
    c = {}
    c["ident"] = k.b16(128, "ident")
    c["rperm"] = k.b16(128, "rperm")
    c["ones"] = k.b16(128, "ones")
    c["maskT2"] = k.b16(256, "maskT2")
    c["negm"] = k.b16(128, "negm")
    c["selb"] = k.b16(256, "selb")
    for i, nm in enumerate(("ident", "rperm", "ones")):
        S.dma("sp", c[nm].ap, dr["cst_bf"][:, i * 128:(i + 1) * 128], W=[c[nm].b])
    S.dma("sp", c["maskT2"].ap, dr["cst_bf"][:, 384:640], W=[c["maskT2"].b])
    S.dma("sp", c["negm"].ap, dr["cst_bf"][:, 640:768], W=[c["negm"].b])
    S.dma("sp", c["selb"].ap, dr["cst_bf"][:, 768:1024], W=[c["selb"].b])
    c["triU"] = k.f32(128, "triU")
    c["triL"] = k.f32(128, "triL")
    c["maskU4"] = k.f32(512, "maskU4")
    c["sel"] = k.f32(256, "sel")
    S.dma("sp", c["triU"].ap, dr["cst_f"][:, 0:128], W=[c["triU"].b])
    S.dma("sp", c["triL"].ap, dr["cst_f"][:, 128:256], W=[c["triL"].b])
    S.dma("sp", c["maskU4"].ap, dr["cst_f"][:, 256:768], W=[c["maskU4"].b])
    S.dma("sp", c["sel"].ap, dr["cst_f"][:, 768:1024], W=[c["sel"].b])
    c["eps"] = k.f32(16, "eps")
    k.memset("pool", c["eps"].ap[:, 0:1], EPS, [c["eps"].b])
    c["small"] = k.f32(64, "small")
    c["junk"] = k.b16(2048, "junk")
    return c


def bcast_load(k, tile_ap, b, dram_row):
    k.S.dma("sp", tile_ap, dram_row.partition_broadcast(128), W=[b])


def phase_A0(k, c, dr):
    S = k.S
    mF, mB = k.AF.mark(), k.AB.mark()
    w_in = k.b16(8 * GIN, "w_in")
    w3 = v3(w_in.ap, 8)
    m2 = k.AF.mark()
    HW = GIN // 2
    stg = [k.f32(HW, "stg%d" % i) for i in range(4)]
    n = 0
    for kc in range(8):
        for half in range(2):
            st = stg[n % 4]
            n += 1
            S.dma("sp", st.ap, dr["gla_w_in"][kc * 128:(kc + 1) * 128, half * HW:(half + 1) * HW], W=[st.b])
            k.cast_any(w3[:, kc, half * HW:(half + 1) * HW], st.ap, [st.b], [w_in.b])
    S.barrier()
    k.AF.release(m2)
    pre0 = k.f32(D, "pre0")
    bcast_load(k, pre0.ap, pre0.b, dr["pre_g"][0:1, :])
    ngf = k.f32(BR, "ngf")
    for h in range(4):
        bcast_load(k, ngf.ap[:, h * 512:(h + 1) * 512], ngf.b, dr["gla_norm_g"][0:1, :])
    wg2 = k.f32(512, "wg2")
    S.dma("sp", wg2.ap[0:16, :], dr["gla_w_g2"], W=[wg2.b])
    S.dma("sp", wg2.ap[16:17, :], dr["gla_b_g"], W=[wg2.b])
    lowT = [k.f32(128, "lowT%d" % i) for i in range(2)]
    for t in lowT:
        k.memset("pool", t.ap[0:32, :], 1.0, [t.b])
    xs = [k.f32(D, "xs%d" % i) for i in range(2)]
    ez = k.f32(512, "ez")
    sp = k.f32(512, "sp")
    ebT = k.f32(512, "ebT")
    enT = k.f32(512, "enT")
    erev = k.f32(512, "erev")
    Sf = [k.f32(512, "Sf%d" % h) for h in range(4)]
    Sb = [k.b16(512, "Sb%d" % h) for h in range(4)]
    for h in range(4):
        k.memset("pool", Sf[h].ap, 0.0, [Sf[h].b])
        k.memset("pool", Sb[h].ap, 0.0, [Sb[h].b])
    sm = [k.f32(16, "sm%d" % i) for i in range(2)]
    hb = [k.b16(D, "hb%d" % i) for i in range(2)]
    hT = [k.b16(D, "hT%d" % i) for i in range(2)]
    qtT = [k.b16(512, "qtT")] * 2
    ktT = [k.b16(512, "ktT")] * 2
    kend = [k.b16(512, "kend")] * 2
    vb = [k.b16(BR, "vb")] * 2
    sg = k.b16(BR, "sg")
    sgn = [k.b16(BR, "sgn")] * 2
    ATb = [k.b16(512, "ATb")] * 2
    og = [k.b16(BR, "og%d" % i) for i in range(2)]
    ogT = [k.b16(BR, "ogT%d" % i) for i in range(2)]
    ident = c["ident"]
    junk = c["junk"]
    eps = c["eps"]
    SCALE = 128 ** -0.5

    smp = [k.f32(16, "smp%d" % i) for i in range(2)]

    def pro_a(i):
        p = i % 2
        x = xs[p]
        s = smp[p]
        S.dma("sp", x.ap, dr["x"][i * 128:(i + 1) * 128, :], W=[x.b])
        k.act(junk.ap[:, 0:D], x.ap, AF.Square, [x.b], [junk.b, s.b], accum=s.ap[:, 0:1])
        k.rstd(s.ap[:, 0:1], s.ap[:, 1:2], s.ap[:, 2:3], D, eps.ap[:, 0:1], [s.b, eps.b], [s.b])
        h = hb[p]
        k.stt("dve", h.ap, x.ap, s.ap[:, 2:3], pre0.ap, ALU.mult, ALU.mult, [x.b, s.b, pre0.b], [h.b])

    def pro_b(i):
        p = i % 2
        h = hb[p]
        pb, pbb = k.banks(7)
        pv = pb.bitcast(BF16)
        for kc in range(8):
            k.tr(pv[:, kc * 128:(kc + 1) * 128], h.ap[:, kc * 128:(kc + 1) * 128], ident.ap, [h.b, ident.b], pbb)
        ht = hT[p]
        k.copy("act", ht.ap, pv, pbb, [ht.b])

    osb = k.f32(BR, "osb")
    smh = [k.f32(16, "smh%d" % i) for i in range(4)]
    prev_tail = [None]

    def tail(i, o_):
        pt, ptb = k.banks(5, 2)
        ptv = pt.bitcast(BF16)
        for cc in range(16):
            k.tr(ptv[:, cc * 128:(cc + 1) * 128], o_.ap[:, cc * 128:(cc + 1) * 128], ident.ap, [o_.b, ident.b], ptb)
        ot = ogT[i % 2]
        k.copy("act", ot.ap, ptv, ptb, [ot.b])
        S.dma("sp", dr["ogT"][i], ot.ap, R=[ot.b])

    def low(i):
        ht_ = hT[i % 2]
        pl, plb = k.banks(5)
        for kc in range(8):
            k.mm(pl[0:16, 0:128], w3[:, kc, 5120:5136], ht_.ap[:, kc * 128:(kc + 1) * 128], kc == 0, kc == 7,
                 [ht_.b, w_in.b], plb)
        k.copy("act", lowT[i % 2].ap[0:16, :], pl[0:16, 0:128], plb, [lowT[i % 2].b])

    pro_a(0)
    pro_b(0)
    low(0)
    for i in range(NT):
        p = i % 2
        ht = hT[p]
        hk = lambda kc: ht.ap[:, kc * 128:(kc + 1) * 128]
        lt = lowT[p]
        pkt, pktb = k.banks(4)
        for kc in range(8):
            k.mm(pkt, hk(kc), w3[:, kc, 512:1024], kc == 0, kc == 7, [ht.b, w_in.b], pktb)
        pz, pzb = k.banks(5)
        k.mm(pz, lt.ap[0:17, :], wg2.ap[0:17, :], True, True, [lt.b, wg2.b], pzb)
        k.act(ez.ap, pz, AF.Exp, pzb, [ez.b], scale=-1.0)
        k.act(sp.ap, ez.ap, AF.Ln, [ez.b], [sp.b], bias=1.0)
        pv4, pv4b = k.banks(0, 4)
        for cc in range(4):
            for kc in range(8):
                k.mm(pv4[:, cc * 512:(cc + 1) * 512], hk(kc), w3[:, kc, 1024 + cc * 512:1024 + (cc + 1) * 512],
                     kc == 0, kc == 7, [ht.b, w_in.b], [pv4b[cc]])
        v = vb[p]
        k.copy("act", v.ap, pv4, pv4b, [v.b])
        if prev_tail[0] is not None:
            tail(*prev_tail[0])
            prev_tail[0] = None
        pc, pcb = k.banks(5)
        for hh in range(4):
            k.mm(pc[:, hh * 128:(hh + 1) * 128], sp.ap[:, hh * 128:(hh + 1) * 128], c["triU"].ap, True, True,
                 [sp.b, c["triU"].b], pcb)
        pr, prb = k.banks(6)
        k.mm(pr, c["triL"].ap, sp.ap, True, True, [sp.b, c["triL"].b], prb)
        k.act(ebT.ap, pc, AF.Exp, pcb, [ebT.b])
        k.act(enT.ap, pc, AF.Exp, pcb, [enT.b], scale=-1.0)
        k.act(erev.ap, pr, AF.Exp, prb, [erev.b])
        ke = kend[p]
        k.tt("dve", ke.ap, pkt, erev.ap, ALU.mult, pktb + [erev.b], [ke.b])
        if i + 1 < NT:
            pro_a(i + 1)
        pq, pqb = k.banks(5)
        for hh in range(4):
            for kc in range(8):
                k.mm(pq[:, hh * 128:(hh + 1) * 128], w3[:, kc, hh * 128:(hh + 1) * 128], hk(kc), kc == 0, kc == 7,
                     [ht.b, w_in.b], pqb)
        pk, pkb = k.banks(6)
        for hh in range(4):
            for kc in range(8):
                k.mm(pk[:, hh * 128:(hh + 1) * 128], w3[:, kc, 512 + hh * 128:512 + (hh + 1) * 128], hk(kc), kc == 0,
                     kc == 7, [ht.b, w_in.b], pkb)
        qt_, kt_ = qtT[p], ktT[p]
        k.stt("dve", qt_.ap, pq, SCALE, ebT.ap, ALU.mult, ALU.mult, pqb + [ebT.b], [qt_.b])
        k.tt("dve", kt_.ap, pk, enT.ap, ALU.mult, pkb + [enT.b], [kt_.b])
        pg4, pg4b = k.banks(0, 4)
        for cc in range(4):
            for kc in range(8):
                k.mm(pg4[:, cc * 512:(cc + 1) * 512], hk(kc), w3[:, kc, 3072 + cc * 512:3072 + (cc + 1) * 512],
                     kc == 0, kc == 7, [ht.b, w_in.b], [pg4b[cc]])
        if i + 1 < NT:
            pro_b(i + 1)
        k.act(sg.ap, pg4, AF.Silu, pg4b, [sg.b])
        sn = sgn[p]
        k.tt("pool", sn.ap, sg.ap, ngf.ap, ALU.mult, [sg.b, ngf.b], [sn.b])
        pa, pab = k.banks(4)
        for hh in range(4):
            sl = slice(hh * 128, (hh + 1) * 128)
            k.mm(pa[:, sl], kt_.ap[:, sl], qt_.ap[:, sl], True, True, [kt_.b, qt_.b], pab)
        at = ATb[p]
        k.tt("dve", at.ap, pa, c["maskU4"].ap, ALU.mult, pab + [c["maskU4"].b], [at.b])
        if i + 1 < NT:
            low(i + 1)
        pkvs = []
        if i < NT - 1:
            for hh in range(4):
                sl = slice(hh * 128, (hh + 1) * 128)
                vs = slice(hh * 512, (hh + 1) * 512)
                pkv, pkvb = k.banks(4 + hh)
                k.mm(pkv, ke.ap[:, sl], v.ap[:, vs], True, True, [ke.b, v.b], pkvb)
                pkvs.append((pkv, pkvb))
        po, pob = k.banks(0, 4)
        for hh in range(4):
            sl = slice(hh * 128, (hh + 1) * 128)
            vs = slice(hh * 512, (hh + 1) * 512)
            k.mm(po[:, vs], at.ap[:, sl], v.ap[:, vs], True, False, [at.b, v.b], [pob[hh]])
            k.mm(po[:, vs], qt_.ap[:, sl], Sb[hh].ap, False, True, [qt_.b, Sb[hh].b], [pob[hh]])
        for hh, (pkv, pkvb) in enumerate(pkvs):
            k.stt("dve", Sf[hh].ap, Sf[hh].ap, ebT.ap[:, hh * 128 + 127:hh * 128 + 128], pkv, ALU.mult, ALU.add,
                  [Sf[hh].b, ebT.b] + pkvb, [Sf[hh].b])
            k.copy("pool", Sb[hh].ap, Sf[hh].ap, [Sf[hh].b], [Sb[hh].b])
        o_ = og[p]
        for hh in range(4):
            vs = slice(hh * 512, (hh + 1) * 512)
            sh = smh[hh]
            k.act(junk.ap[:, 0:512], po[:, vs], AF.Square, [pob[hh]], [junk.b, sh.b], accum=sh.ap[:, 0:1])
            k.copy("dve", osb.ap[:, vs], po[:, vs], [pob[hh]], [osb.b])
            k.rstd(sh.ap[:, 0:1], sh.ap[:, 1:2], sh.ap[:, 2:3], 512, eps.ap[:, 0:1], [sh.b, eps.b], [sh.b])
        for hh in range(4):
            vs = slice(hh * 512, (hh + 1) * 512)
            k.stt("dve", o_.ap[:, vs], osb.ap[:, vs], smh[hh].ap[:, 2:3], sn.ap[:, vs], ALU.mult, ALU.mult,
                  [osb.b, smh[hh].b, sn.b], [o_.b])
        prev_tail[0] = (i, o_)
    tail(*prev_tail[0])
    S.barrier()
    k.AF.release(mF)
    k.AB.release(mB)


def phase_F(k, c, dr, w_out_d, post_row, xin_d, out_d, pre_row=None, h1T=None):
    S = k.S
    mF, mB = k.AF.mark(), k.AB.mark()
    w_out = k.b16(16 * D, "w_out")
    wo3 = v3(w_out.ap, 16)
    wbuf = [S.buf("w_out_p%d" % j) for j in range(8)]
    stg = [k.f32(2 * D, "stgF%d" % i) for i in range(2)]
    for j in range(8):
        st = stg[j % 2]
        S.dma("sp", v3(st.ap, 2), w_out_d[j * 256:(j + 1) * 256, :].rearrange("(c p) n -> p c n", p=128), W=[st.b])
        k.copy(("dve", "act")[j % 2], w_out.ap[:, j * 2 * D:(j + 1) * 2 * D], st.ap, [st.b], [wbuf[j]])
    postg = k.f32(D, "postg")
    bcast_load(k, postg.ap, postg.b, post_row)
    if h1T is not None:
        preg = k.f32(D, "preg")
        bcast_load(k, preg.ap, preg.b, pre_row)
        h1 = [k.b16(D, "h1_%d" % i) for i in range(2)]
    ogs = [k.b16(BR, "ogs%d" % i) for i in range(4)]
    xs = [k.f32(D, "xsF%d" % i) for i in range(4)]
    tmp = [k.f32(D, "tmpF%d" % i) for i in range(2)]
    x1 = [k.f32(D, "x1F%d" % i) for i in range(2)]
    sm = [k.f32(16, "smF%d" % i) for i in range(2)]
    junk, eps, ident = c["junk"], c["eps"], c["ident"]

    def load(i):
        o = ogs[i % 4]
        S.dma("sp", o.ap, dr["ogT"][i], W=[o.b])
        x = xs[i % 4]
        S.dma("sp", x.ap, xin_d[i * 128:(i + 1) * 128, :], W=[x.b])

    def ymm(i):
        o = ogs[i % 4]
        py, pyb = k.banks(2 * (i % 3), 2)
        for half in range(2):
            for cc in range(16):
                k.mm(py[:, half * 512:(half + 1) * 512], o.ap[:, cc * 128:(cc + 1) * 128],
                     wo3[:, cc, half * 512:(half + 1) * 512], cc == 0, cc == 15, [o.b, wbuf[cc // 2]], [pyb[half]])
        return py, pyb

    def h1_transposes(i):
        hh = h1[i % 2]
        pb, pbb = k.banks(6 + (i % 2))
        pv = pb.bitcast(BF16)
        for kc in range(8):
            k.tr(pv[:, kc * 128:(kc + 1) * 128], hh.ap[:, kc * 128:(kc + 1) * 128], ident.ap, [hh.b, ident.b], pbb)
        k.copy("act", v3(h1T.ap, 8)[:, :, i * 128:(i + 1) * 128], v3(pv, 8), pbb, [h1T.b])

    load(0)
    load(1)
    load(2)
    yq = [ymm(0), ymm(1)]
    for i in range(NT):
        if i + 3 < NT:
            load(i + 3)
        x, s = xs[i % 4], sm[i % 2]
        py, pyb = yq.pop(0)
        if i + 2 < NT:
            yq.append(ymm(i + 2))
        k.act(junk.ap[:, 0:D], py, AF.Square, pyb, [junk.b, s.b], accum=s.ap[:, 0:1])
        k.rstd(s.ap[:, 0:1], s.ap[:, 1:2], s.ap[:, 2:3], D, eps.ap[:, 0:1], [s.b, eps.b], [s.b])
        t = tmp[i % 2]
        k.stt("dve", t.ap, py, s.ap[:, 2:3], postg.ap, ALU.mult, ALU.mult, pyb + [s.b, postg.b], [t.b])
        xo = x1[i % 2]
        k.tt("pool", xo.ap, t.ap, x.ap, ALU.add, [t.b, x.b], [xo.b])
        S.dma("sp", out_d[i * 128:(i + 1) * 128, :], xo.ap, R=[xo.b])
        if h1T is not None:
            k.act(junk.ap[:, 0:D], xo.ap, AF.Square, [xo.b], [junk.b, s.b], accum=s.ap[:, 4:5])
            k.rstd(s.ap[:, 4:5], s.ap[:, 5:6], s.ap[:, 6:7], D, eps.ap[:, 0:1], [s.b, eps.b], [s.b])
            hh = h1[i % 2]
            k.stt("dve", hh.ap, xo.ap, s.ap[:, 6:7], preg.ap, ALU.mult, ALU.mult, [xo.b, s.b, preg.b], [hh.b])
            if i >= 1:
                h1_transposes(i - 1)
    if h1T is not None:
        h1_transposes(NT - 1)
    S.barrier()
    k.AF.release(mF)
    k.AB.release(mB)


def phase_A1(k, c, dr, h1T):
    S = k.S
    mF, mB = k.AF.mark(), k.AB.mark()
    h3 = v3(h1T.ap, 8)
    cst = [k.f32(512, "cosT%d" % i) for i in range(2)]
    snt = [k.f32(512, "sinT%d" % i) for i in range(2)]
    lv = k.f32(256, "lv")
    for j, nm in enumerate(("diff_lam_q1", "diff_lam_k1", "diff_lam_q2", "diff_lam_k2")):
        bcast_load(k, lv.ap[:, j * 64:(j + 1) * 64], lv.b, dr[nm])
    sc = k.f32(16, "scA1")
    junkf = k.f32(64, "junkf")
    k.stt("dve", junkf.ap, lv.ap[:, 0:64], 1.0, lv.ap[:, 64:128], ALU.mult, ALU.mult, [lv.b], [junkf.b, sc.b],
          accum=sc.ap[:, 0:1])
    k.stt("dve", junkf.ap, lv.ap[:, 128:192], 1.0, lv.ap[:, 192:256], ALU.mult, ALU.mult, [lv.b], [junkf.b, sc.b],
          accum=sc.ap[:, 1:2])
    k.act(sc.ap[:, 2:4], sc.ap[:, 0:2], AF.Exp, [sc.b], [sc.b])
    k.tt("dve", sc.ap[:, 4:5], sc.ap[:, 3:4], sc.ap[:, 2:3], ALU.subtract, [sc.b], [sc.b])
    k.ts("dve", sc.ap[:, 5:6], sc.ap[:, 4:5], -LAM_INIT, ALU.add, [sc.b], [sc.b])
    S.dma("sp", sc.ap[:, 6:7], dr["diff_norm_g"].rearrange("o v -> v o"), W=[sc.b])
    k.ts("dve", sc.ap[:, 7:8], sc.ap[:, 6:7], 1.0 - LAM_INIT, ALU.mult, [sc.b], [sc.b])
    neglam = sc.ap[:, 5:6]
    gcol = sc.ap[:, 7:8]

    stg = [k.f32(4 * 512, "stgA%d" % i) for i in range(2)]
    wh = [k.b16(8 * 512, "wh%d" % i) for i in range(2)]
    QT = k.b16(T, "QT")
    KT = k.b16(T, "KT")
    V = k.b16(T, "V")
    sgT = k.b16(T, "sgT")
    qraw = [k.b16(512, "qraw%d" % i) for i in range(2)]
    t1 = [k.f32(512, "t1_%d" % i) for i in range(2)]
    t2 = [k.f32(512, "t2_%d" % i) for i in range(2)]
    PT = [k.b16(1024, "PT%d" % i) for i in range(3)]
    rlb = k.b16(512, "rlb")
    lnl = k.f32(1024, "lnl")
    o01 = k.f32(1024, "o01")
    of = k.f32(512, "of")
    sq = k.b16(512, "sq")
    lnt = k.f32(512, "lnt")
    rst = k.f32(512, "rst")
    tf = k.f32(512, "tf")
    ogh = [k.b16(512, "ogh%d" % i) for i in range(2)]
    ones, rperm, maskT2, eps = c["ones"], c["rperm"], c["maskT2"], c["eps"]
    ident, negm = c["ident"], c["negm"]
    pO, pOb = k.banks(4, 2)
    pL, pLb = k.banks(6, 1)
    pX, pXb = k.banks(7, 1)
    selb = c["selb"]
    nq = 0
    npt = 0
    nog = 0

    def dma_w(h):
        for half in range(2):
            st = stg[half]
            s3 = v3(st.ap, 4)
            for blk in range(4):
                S.dma("sp", s3[:, :, blk * 128:(blk + 1) * 128],
                      dr["diff_w_in"][half * 512:(half + 1) * 512, blk * BR + h * 128:blk * BR + (h + 1) * 128]
                      .rearrange("(k p) n -> p k n", p=128), W=[st.b])

    def cast_w(h):
        for half in range(2):
            st = stg[half]
            k.copy("dve", wh[h % 2].ap[:, half * 2048:(half + 1) * 2048], st.ap, [st.b], [wh[h % 2].b])

    ntab = [0]

    def load_tab(tt_):
        j = tt_ % 2
        S.dma("sp", cst[j].ap, dr["cosT"][:, tt_ * 512:(tt_ + 1) * 512], W=[cst[j].b])
        S.dma("sp", snt[j].ap, dr["sinT"][:, tt_ * 512:(tt_ + 1) * 512], W=[snt[j].b])

    dma_w(0)
    cast_w(0)
    load_tab(0)
    load_tab(1)
    pend = []
    for h in range(DBG["heads"]):
        w3 = v3(wh[h % 2].ap, 8)
        wb = wh[h % 2].b
        for tt_ in range(8 if DBG["inproj"] else 0):
            ts_ = slice(tt_ * 512, (tt_ + 1) * 512)
            cosT, sinT = cst[tt_ % 2], snt[tt_ % 2]
            if tt_ >= 1:
                for _ in range(3):
                    if pend:
                        _, fn, q_, h_ = pend.pop(0)
                        fn(q_, h_)
            pqs = []
            for blk in (0, 1):
                pq, pqb = k.nextbank(1, 0, 7)
                for kc in range(8):
                    k.mm(pq, w3[:, kc, blk * 128:(blk + 1) * 128], h3[:, kc, ts_], kc == 0, kc == 7, [wb, h1T.b], pqb)
                pqs.append((pq, pqb))
            pg, pgb = k.nextbank(1, 0, 7)
            for kc in range(8):
                k.mm(pg, w3[:, kc, 384:512], h3[:, kc, ts_], kc == 0, kc == 7, [wb, h1T.b], pgb)
            pv, pvb = k.nextbank(1, 0, 7)
            for s_ in range(4):
                for kc in range(8):
                    k.mm(pv[:, s_ * 128:(s_ + 1) * 128], h3[:, kc, tt_ * 512 + s_ * 128:tt_ * 512 + (s_ + 1) * 128],
                         w3[:, kc, 256:384], kc == 0, kc == 7, [wb, h1T.b], pvb)
            rots = []
            for blk, dst in ((0, QT), (1, KT)):
                pq, pqb = pqs[blk]
                qr = qraw[nq % 2]
                a1, a2 = t1[nq % 2], t2[nq % 2]
                nq += 1
                k.copy("act", qr.ap, pq, pqb, [qr.b])
                rots.append((pq, pqb, qr, a1, a2, dst))
            k.act(sgT.ap[:, ts_], pg, AF.Silu, pgb, [sgT.b])
            k.copy("act", V.ap[:, ts_], pv, pvb, [V.b])
            for (pq, pqb, qr, a1, a2, dst) in rots:
                pr, prb = k.nextbank(1, 0, 7)
                k.mm(pr, rperm.ap, qr.ap, True, True, [rperm.b, qr.b], prb)
                k.tt("dve", a1.ap, pq, cosT.ap, ALU.mult, pqb + [cosT.b], [a1.b])
                k.tt("dve", a2.ap, pr, sinT.ap, ALU.mult, prb + [sinT.b], [a2.b])
                k.tt("pool", dst.ap[:, ts_], a1.ap, a2.ap, ALU.add, [a1.b, a2.b], [dst.b])
            if tt_ + 2 < 8:
                load_tab(tt_ + 2)
        if h + 1 < 16:
            dma_w(h + 1)
        blocks = [(qt, kb) for qt in range(DBG["nqt"] if DBG["att"] else 0) for kb in range(4 * qt + 4)]

        def qk_exp(qt, kb):
            nonlocal npt
            j = kb - 4 * qt
            c0 = max(j, 0) * 128
            ps2, ps2b = k.banks(2 * (npt % 2), 2)
            pt_ = PT[npt % 3]
            npt += 1
            for cc in range(2):
                k.mm(ps2[:, cc * 512 + c0:(cc + 1) * 512], KT.ap[cc * 64:(cc + 1) * 64, kb * 128:(kb + 1) * 128],
                     QT.ap[cc * 64:(cc + 1) * 64, qt * 512 + c0:(qt + 1) * 512], True, j < 0, [KT.b, QT.b],
                     [ps2b[cc]])
            if j >= 0:
                for cc in range(2):
                    k.mm(ps2[:, cc * 512 + c0:cc * 512 + c0 + 128], ident.ap, negm.ap, False, True,
                         [ident.b, negm.b], [ps2b[cc]])
            p3 = v3(pt_.ap, 2)
            k.act(p3[:, :, c0:512], v3(ps2, 2)[:, :, c0:512], AF.Exp, ps2b, [pt_.b], scale=0.125)
            return pt_, c0

        def av(qt, kb, pt_, c0):
            nkb = 4 * qt + 4
            for cc in range(2):
                k.mmt(pL[32 * cc:32 * cc + 32, c0:512], ones.ap[:, 0:32], pt_.ap[:, cc * 512 + c0:(cc + 1) * 512],
                      kb == 0, kb == nkb - 1, (0, 32 * cc), [ones.b, pt_.b], pLb)
            for cc in range(2):
                k.mm(pO[:, cc * 512 + c0:(cc + 1) * 512], V.ap[:, kb * 128:(kb + 1) * 128],
                     pt_.ap[:, cc * 512 + c0:(cc + 1) * 512], kb == 0, kb == nkb - 1, [V.b, pt_.b], [pOb[cc]])

        def epi1(qt, hd):
            k.act(lnl.ap[0:64, 0:512], pL[0:64, :], AF.Ln, pLb, [lnl.b])
            k.copy("dve", o01.ap[:, 0:512], pO[:, 0:512], [pOb[0]], [o01.b])
            k.copy("dve", o01.ap[:, 512:1024], pO[:, 512:1024], [pOb[1]], [o01.b])
            k.act(rlb.ap[0:64, 0:512], lnl.ap[0:64, 0:512], AF.Exp, [lnl.b], [rlb.b], scale=-1.0)

        def st_b0(qt, hd):
            k.mm(pX, selb.ap[0:64, 0:128], rlb.ap[0:64, 0:512], True, True, [selb.b, rlb.b], pXb)

        def st_m0(qt, hd):
            k.tt("dve", o01.ap[:, 0:512], o01.ap[:, 0:512], pX, ALU.mult, [o01.b] + pXb, [o01.b])

        def st_b1(qt, hd):
            k.mm(pX, selb.ap[0:64, 128:256], rlb.ap[0:64, 0:512], True, True, [selb.b, rlb.b], pXb)

        def st_m1(qt, hd):
            k.tt("dve", o01.ap[:, 512:1024], o01.ap[:, 512:1024], pX, ALU.mult, [o01.b] + pXb, [o01.b])
            k.stt("dve", of.ap, o01.ap[:, 512:1024], neglam, o01.ap[:, 0:512], ALU.mult, ALU.add, [o01.b, sc.b], [of.b])

        def st_sq(qt, hd):
            k.act(sq.ap, of.ap, AF.Square, [of.b], [sq.b])

        def st_ss(qt, hd):
            k.mm(pX, ones.ap, sq.ap, True, True, [ones.b, sq.b], pXb)

        def st_ln(qt, hd):
            k.act(lnt.ap, pX, AF.Ln, pXb + [eps.b], [lnt.b], scale=1.0 / 128, bias=eps.ap[:, 0:1])

        def st_ex(qt, hd):
            k.act(rst.ap, lnt.ap, AF.Exp, [lnt.b], [rst.b], scale=-0.5)

        def st_fin(qt, hd):
            nonlocal nog
            qs = slice(qt * 512, (qt + 1) * 512)
            k.tt("dve", tf.ap, of.ap, rst.ap, ALU.mult, [of.b, rst.b], [tf.b])
            og_ = ogh[nog % 2]
            nog += 1
            k.stt("dve", og_.ap, tf.ap, gcol, sgT.ap[:, qs], ALU.mult, ALU.mult, [tf.b, sc.b, sgT.b], [og_.b])
            S.dma("sp", dr["ogT"][qt * 4:(qt + 1) * 4, :, hd * 128:(hd + 1) * 128].rearrange("s p v -> p s v"),
                  v3(og_.ap, 4), R=[og_.b])

        STAGES0 = ((2, st_b0), (3, st_m0), (4, st_b1), (5, st_m1), (6, st_sq), (7, st_ss), (8, st_ln), (8, st_ex),
                   (8, st_fin))
        STAGES = ((2, st_b0), (3, st_m0), (4, st_b1), (5, st_m1), (6, st_sq), (7, st_ss), (8, st_ln), (9, st_ex),
                  (10, st_fin))

        cur = qk_exp(*blocks[0]) if blocks else None
        for bi, (qt, kb) in enumerate(blocks):
            nxt = qk_exp(*blocks[bi + 1]) if bi + 1 < len(blocks) else None
            av(qt, kb, *cur)
            while pend and bi >= pend[0][0]:
                _, fn, q_, h_ = pend.pop(0)
                fn(q_, h_)
            if kb == 4 * qt + 3 and DBG["epi"]:
                epi1(qt, h)
                for off, fn in (STAGES0 if qt == 0 else STAGES):
                    pend.append((bi + off, fn, qt, h))
            if bi == 10 and h + 1 < 16:
                cast_w(h + 1)
            if bi == 20 and h + 1 < 16:
                load_tab(0)
                load_tab(1)
            cur = nxt
        pend = [(-1, fn, q_, h_) for (_, fn, q_, h_) in pend]
    for _, fn, q_, h_ in pend:
        fn(q_, h_)
    S.barrier()
    k.AF.release(mF)
    k.AB.release(mB)


NAF = 16 * 1024
NAB = 68 * 1024

_INPUT_NAMES = ("x", "pre_g", "post_g", "gla_w_in", "gla_w_g2", "gla_b_g", "gla_norm_g", "gla_w_out",
                "diff_w_in", "diff_lam_q1", "diff_lam_k1", "diff_lam_q2", "diff_lam_k2", "diff_norm_g", "diff_w_out")
_SHAPES = {
    "x": [T, D], "pre_g": [2, D], "post_g": [2, D], "gla_w_in": [D, GIN], "gla_w_g2": [16, 512],
    "gla_b_g": [1, 512], "gla_norm_g": [1, 512], "gla_w_out": [BR, D], "diff_w_in": [D, 4 * BR],
    "diff_lam_q1": [1, 64], "diff_lam_k1": [1, 64], "diff_lam_q2": [1, 64], "diff_lam_k2": [1, 64],
    "diff_norm_g": [1, 128], "diff_w_out": [BR, D],
}


def build(mode="both"):
    nc = bass.Bass("TRN2", target_bir_lowering=False)
    dr = {}
    for nm in _INPUT_NAMES:
        if mode == "L1" and nm == "x":
            continue
        dr[nm] = nc.dram_tensor(nm, _SHAPES[nm], F32, kind="ExternalInput").ap()
    dr["cst_bf"] = nc.dram_tensor("cst_bf", [128, 1024], BF16, kind="ExternalInput").ap()
    dr["cst_f"] = nc.dram_tensor("cst_f", [128, 1024], F32, kind="ExternalInput").ap()
    dr["cosT"] = nc.dram_tensor("cosT", [128, T], F32, kind="ExternalInput").ap()
    dr["sinT"] = nc.dram_tensor("sinT", [128, T], F32, kind="ExternalInput").ap()
    dr["ogT"] = nc.dram_tensor("ogT_scr", [NT, 128, BR], BF16, kind="Internal").ap()
    if mode == "both":
        dr["x1"] = nc.dram_tensor("x1_scr", [T, D], F32, kind="Internal").ap()
        dr["out"] = nc.dram_tensor("out", [T, D], F32, kind="ExternalOutput").ap()
    elif mode == "L0":
        dr["x1"] = nc.dram_tensor("x1", [T, D], F32, kind="ExternalOutput").ap()
        dr["h1T_d"] = nc.dram_tensor("h1T", [128, 8 * T], BF16, kind="ExternalOutput").ap()
    else:
        dr["x1"] = nc.dram_tensor("x1", [T, D], F32, kind="ExternalInput").ap()
        dr["h1T_d"] = nc.dram_tensor("h1T", [128, 8 * T], BF16, kind="ExternalInput").ap()
        dr["out"] = nc.dram_tensor("out", [T, D], F32, kind="ExternalOutput").ap()

    with ExitStack() as st:
        S = Sched(nc, st)
        af = st.enter_context(nc.sbuf_tensor("arena_f", [128, NAF], F32))
        ab = st.enter_context(nc.sbuf_tensor("arena_b", [128, NAB], BF16))
        ps = st.enter_context(nc.psum_tensor("ps", [128, 4096], F32))
        k = K(nc, S, Arena(af, NAF), Arena(ab, NAB), ps)
        c = load_consts(k, dr)
        if mode in ("both", "L0"):
            phase_A0(k, c, dr)
        h1T = k.b16(8 * T, "h1T")
        if mode in ("both", "L0"):
            phase_F(k, c, dr, dr["gla_w_out"], dr["post_g"][0:1, :], dr["x"], dr["x1"],
                    pre_row=dr["pre_g"][1:2, :], h1T=h1T)
        if mode == "L0":
            for j in range(8):
                S.dma("sp", dr["h1T_d"][:, j * T:(j + 1) * T], h1T.ap[:, j * T:(j + 1) * T], R=[h1T.b])
            S.barrier()
        if mode == "L1":
            for j in range(8):
                S.dma("sp", h1T.ap[:, j * T:(j + 1) * T], dr["h1T_d"][:, j * T:(j + 1) * T], W=[h1T.b])
        if mode in ("both", "L1"):
            phase_A1(k, c, dr, h1T)
            if DBG["F1"]:
                phase_F(k, c, dr, dr["diff_w_out"], dr["post_g"][1:2, :], dr["x1"], dr["out"])
        S.barrier()
        print("ops", S.nops, "arena peaks f32 %d bf16 %d" % (k.AF.peak, k.AB.peak), "sems", sum(1 for b in S.allbufs if b.dsem is not None))
        with nc.Block() as block:
            S.replay(block)
    return nc


def _consts():
    bf = ml_dtypes.bfloat16
    p = np.arange(128)
    ident = np.eye(128, dtype=np.float32)
    perm = np.where((p % 64) < 32, p + 32, p - 32)
    rperm = np.zeros((128, 128), np.float32)
    rperm[perm, p] = 1.0
    ones = np.ones((128, 128), np.float32)
    maskT = (p[None, :] >= p[:, None]).astype(np.float32)
    negm = np.where(p[None, :] >= p[:, None], 0.0, -30000.0).astype(np.float32)
    selb = np.zeros((128, 256), np.float32)
    selb[0, 0:128] = 1.0
    selb[32, 128:256] = 1.0
    cst_bf = np.concatenate([ident, rperm, ones, maskT, maskT, negm, selb], 1).astype(bf)
    triU = np.where(p[:, None] <= p[None, :], -1.0 / 16.0, 0.0).astype(np.float32)
    triL = np.where(p[:, None] > p[None, :], -1.0 / 16.0, 0.0).astype(np.float32)
    maskU = (p[:, None] <= p[None, :]).astype(np.float32)
    sel = np.zeros((128, 256), np.float32)
    sel[0, 0:128] = 1.0
    sel[32, 128:256] = 1.0
    cst_f = np.concatenate([triU, triL, maskU, maskU, maskU, maskU, sel], 1).astype(np.float32)
    inv_freq = (1.0 / (np.float32(10000.0) ** (np.arange(0, 64, 2, dtype=np.float32) / np.float32(64)))).astype(np.float32)
    pos = np.arange(T, dtype=np.float32)
    ang = (pos[:, None] * inv_freq[None, :]).astype(np.float32)
    cos = np.cos(ang).astype(np.float32).T
    sin = np.sin(ang).astype(np.float32).T
    cosT = np.concatenate([cos, cos, cos, cos], 0)
    sinT = np.concatenate([-sin, sin, -sin, sin], 0)
    return {"cst_bf": np.ascontiguousarray(cst_bf), "cst_f": np.ascontiguousarray(cst_f),
            "cosT": np.ascontiguousarray(cosT), "sinT": np.ascontiguousarray(sinT)}


_NC_CACHE = {}


def _get_nc(mode):
    if mode not in _NC_CACHE:
        _NC_CACHE[mode] = build(mode)
    return _NC_CACHE[mode]


def _shared_maps(inputs):
    m = dict(_consts())
    for nm in _INPUT_NAMES:
        if nm == "x":
            continue
        a = np.asarray(inputs[nm], dtype=np.float32)
        if nm in ("pre_g", "post_g"):
            m[nm] = np.ascontiguousarray(a)
        else:
            m[nm] = np.ascontiguousarray(a.reshape(_SHAPES[nm]))
    return m


FUSED = True


def kernel(**inputs):
    x = np.asarray(inputs["x"], dtype=np.float32)
    shared = _shared_maps(inputs)
    cores = list(range(NCORES))
    if FUSED:
        nc = _get_nc("both")
        in_maps = [dict(shared, x=np.ascontiguousarray(x[b])) for b in cores]
        res = run_bass_kernel_spmd(nc, in_maps, core_ids=cores)
        return np.stack([np.asarray(r["out"]) for r in res.results], 0).astype(np.float32)
    nc0 = _get_nc("L0")
    in_maps = [dict(shared, x=np.ascontiguousarray(x[b])) for b in cores]
    r0 = run_bass_kernel_spmd(nc0, in_maps, core_ids=cores).results
    nc1 = _get_nc("L1")
    in_maps = [dict(shared, x1=np.asarray(r0[b]["x1"]), h1T=np.asarray(r0[b]["h1T"])) for b in cores]
    r1 = run_bass_kernel_spmd(nc1, in_maps, core_ids=cores).results
    return np.stack([np.asarray(r["out"]) for r in r1], 0).astype(np.float32)
```

```python
import math
from contextlib import ExitStack

import numpy as np
import ml_dtypes
import concourse.bass as bass
import concourse.mybir as mybir
from concourse.bass_utils import run_bass_kernel_spmd

F32 = mybir.dt.float32
BF16 = mybir.dt.bfloat16
AF = mybir.ActivationFunctionType
ALU = mybir.AluOpType

T = 4096
D = 1024
BR = 2048
NT = T // 128
GIN = 5136
EPS = 1e-6
LAM_INIT = 0.8 - 0.6 * math.exp(-0.3 * 1)
NCORES = 8
DBG = {"heads": 16, "nqt": 8, "inproj": True, "att": True, "epi": True, "F1": True, "nkb": 99, "ipp": "qgv", "rope": True}

ENGS = ("pe", "act", "dve", "pool", "sp")


class Buf:
    __slots__ = ("name", "w", "r", "dsem", "dcnt", "excl")

    def __init__(self, name):
        self.name = name
        self.excl = False
        self.w = None
        self.r = []
        self.dsem = None
        self.dcnt = 0


class Rec:
    __slots__ = ("waits", "fn", "inc", "dma_inc", "pos", "val")

    def __init__(self, waits, fn):
        self.waits = waits
        self.fn = fn
        self.inc = False
        self.dma_inc = None
        self.pos = -1
        self.val = 0


class Sched:
    def __init__(self, nc, stack):
        self.nc = nc
        self.stack = stack
        self.ops = {e: [] for e in ENGS}
        self.sem = {e: stack.enter_context(nc.semaphore("sem_" + e)) for e in ENGS}
        self.waited = {e: {} for e in ENGS}
        self.allbufs = []
        self.same_engine_sync = True
        self.same_engine_all = True
        self.nops = 0
        self.nbar = 0

    def buf(self, name):
        b = Buf("%s_%d" % (name, len(self.allbufs)))
        self.allbufs.append(b)
        return b

    def _dsem(self, b):
        if b.dsem is None:
            b.dsem = self.stack.enter_context(self.nc.semaphore("d_" + b.name))
        return b.dsem

    def _push(self, eng, rec):
        rec.pos = len(self.ops[eng])
        self.ops[eng].append(rec)

    def _resolve(self, eng, tok, waits, is_raw):
        if tok[0] == "e":
            _, e2, rec = tok
            if e2 == eng:
                if eng in ("pe", "sp") or not (self.same_engine_sync and (is_raw or self.same_engine_all)):
                    return
            key = "e_" + e2
            if self.waited[eng].get(key, -1) >= rec.pos:
                return
            self.waited[eng][key] = rec.pos
            rec.inc = True
            waits.append(("e", e2, rec))
        else:
            _, b, val = tok
            key = "d_" + b.name
            if self.waited[eng].get(key, 0) >= val:
                return
            self.waited[eng][key] = val
            waits.append(("d", b.dsem, val))

    def _deps(self, eng, R, W):
        waits = []
        for b in R:
            if b.w is not None:
                self._resolve(eng, b.w, waits, True)
            if b.excl:
                for t in b.r:
                    if t[0] != "e" or t[1] != eng:
                        self._resolve(eng, t, waits, False)
        for b in W:
            if b.w is not None:
                self._resolve(eng, b.w, waits, False)
            for t in b.r:
                self._resolve(eng, t, waits, False)
        return waits

    def op(self, eng, fn, R=(), W=()):
        waits = self._deps(eng, R, W)
        rec = Rec(waits, fn)
        self._push(eng, rec)
        tok = ("e", eng, rec)
        for b in R:
            b.r = [t for t in b.r if not (t[0] == "e" and t[1] == eng)]
            b.r.append(tok)
        for b in W:
            b.w = tok
            b.r = []
        self.nops += 1
        return rec

    def dma(self, q, out_ap, in_ap, R=(), W=()):
        waits = self._deps(q, R, W)
        owner = W[0] if W else R[0]
        sem = self._dsem(owner)
        if not W and owner.dcnt > 0:
            self._resolve(q, ("d", owner, owner.dcnt), waits, False)
        owner.dcnt += 16
        tok = ("d", owner, owner.dcnt)
        rec = Rec(waits, lambda e: e.dma_start(out=out_ap, in_=in_ap))
        rec.dma_inc = sem
        self._push(q, rec)
        for b in R:
            b.r.append(tok)
        for b in W:
            b.w = tok
            b.r = []
        self.nops += 1
        return rec

    def barrier(self):
        waits = []
        comp = ("pe", "act", "dve", "pool")
        last = {}
        for e in comp:
            recs = [r for r in self.ops[e] if r.fn is not None]
            if recs:
                last[e] = recs[-1]
                if self.waited["sp"].get("e_" + e, -1) < recs[-1].pos:
                    self.waited["sp"]["e_" + e] = recs[-1].pos
                    recs[-1].inc = True
                    waits.append(("e", e, recs[-1]))
        for b in self.allbufs:
            if b.dsem is not None and b.dcnt > self.waited["sp"].get("d_" + b.name, 0):
                self.waited["sp"]["d_" + b.name] = b.dcnt
                waits.append(("d", b.dsem, b.dcnt))
        sem_sp = self.sem["sp"]
        rec = Rec(waits, lambda e: e.sem_inc(sem_sp, 1))
        self._push("sp", rec)
        self.nbar += 1
        v = self.nbar
        for e in comp:
            self._push(e, Rec([("s", sem_sp, v)], None))
            for e2 in comp:
                if e2 in last:
                    self.waited[e]["e_" + e2] = last[e2].pos
            for b in self.allbufs:
                if b.dsem is not None:
                    self.waited[e]["d_" + b.name] = b.dcnt
        for b in self.allbufs:
            b.w = None
            b.r = []

    def replay(self, block):
        S = self
        for eng in ENGS:
            n = 0
            for rec in S.ops[eng]:
                if rec.inc:
                    n += 1
                    rec.val = n
        self.nsig = {eng: sum(1 for r in S.ops[eng] if r.inc) for eng in ENGS}

        def run(eng, e):
            for rec in S.ops[eng]:
                for w in rec.waits:
                    if w[0] == "e":
                        assert w[2].inc and w[2].val > 0
                        e.wait_ge(S.sem[w[1]], w[2].val)
                    else:
                        e.wait_ge(w[1], w[2])
                if rec.fn is None:
                    continue
                ins = rec.fn(e)
                if rec.dma_inc is not None:
                    ins.then_inc(rec.dma_inc, 16)
                elif rec.inc:
                    ins.then_inc(S.sem[eng], 1)

        @block.tensor
        def _(e):
            run("pe", e)

        @block.scalar
        def _(e):
            run("act", e)

        @block.vector
        def _(e):
            run("dve", e)

        @block.gpsimd
        def _(e):
            run("pool", e)

        @block.sync
        def _(e):
            run("sp", e)


class Arena:
    def __init__(self, ap, n):
        self.ap = ap
        self.n = n
        self.top = 0
        self.peak = 0

    def alloc(self, n):
        a = self.ap[:, self.top:self.top + n]
        self.top += (n + 15) // 16 * 16
        self.peak = max(self.peak, self.top)
        assert self.top <= self.n, ("arena overflow", self.top, self.n)
        return a

    def mark(self):
        return self.top

    def release(self, m):
        self.top = m


class Tile:
    __slots__ = ("ap", "b")

    def __init__(self, ap, b):
        self.ap = ap
        self.b = b


class K:
    def __init__(self, nc, S, AFa, ABa, ps):
        self.nc, self.S, self.AF, self.AB, self.ps = nc, S, AFa, ABa, ps
        self.bankb = [S.buf("bank%d" % i) for i in range(8)]
        for b in self.bankb:
            b.excl = True
        self.rr = 0
        self.cast_rr = 0

    def f32(self, n, name):
        return Tile(self.AF.alloc(n), self.S.buf(name))

    def b16(self, n, name):
        return Tile(self.AB.alloc(n), self.S.buf(name))

    def banks(self, i, n=1):
        return self.ps[:, i * 512:(i + n) * 512], [self.bankb[j] for j in range(i, i + n)]

    def nextbank(self, n=1, lo=0, hi=8):
        if self.rr < lo or self.rr + n > hi:
            self.rr = lo
        i = self.rr
        self.rr += n
        return self.banks(i, n)

    def mm(self, out, lhsT, rhs, start, stop, R, W):
        self.S.op("pe", lambda e: e.matmul(out, lhsT=lhsT, rhs=rhs, start=start, stop=stop), R, W)

    def mmt(self, out, lhsT, rhs, start, stop, tpos, R, W):
        self.S.op("pe", lambda e: e.matmul(out, lhsT=lhsT, rhs=rhs, start=start, stop=stop, tile_position=tpos), R, W)

    def tr(self, out, in_, ident, R, W):
        self.S.op("pe", lambda e: e.transpose(out=out, in_=in_, identity=ident), R, W)

    def act(self, out, in_, func, R, W, scale=None, bias=None, accum=None):
        kw = {}
        if scale is not None:
            kw["scale"] = scale
        if bias is not None:
            kw["bias"] = bias
        if accum is not None:
            kw["accum_out"] = accum
        self.S.op("act", lambda e: e.activation(out=out, in_=in_, func=func, **kw), R, W)

    def tt(self, eng, out, a, b, op, R, W):
        self.S.op(eng, lambda e: e.tensor_tensor(out=out, in0=a, in1=b, op=op), R, W)

    def stt(self, eng, out, a, sc, b, op0, op1, R, W, accum=None):
        if accum is None:
            self.S.op(eng, lambda e: e.scalar_tensor_tensor(out=out, in0=a, scalar=sc, in1=b, op0=op0, op1=op1), R, W)
        else:
            self.S.op(eng, lambda e: e.scalar_tensor_tensor(out=out, in0=a, scalar=sc, in1=b, op0=op0, op1=op1,
                                                           accum_out=accum), R, W)

    def ts(self, eng, out, a, s1, op0, R, W, s2=None, op1=None):
        if s2 is None:
            self.S.op(eng, lambda e: e.tensor_scalar(out=out, in0=a, scalar1=s1, scalar2=None, op0=op0), R, W)
        else:
            self.S.op(eng, lambda e: e.tensor_scalar(out=out, in0=a, scalar1=s1, scalar2=s2, op0=op0, op1=op1), R, W)

    def copy(self, eng, out, in_, R, W):
        if eng == "act":
            self.S.op("act", lambda e: e.copy(out=out, in_=in_), R, W)
        else:
            self.S.op(eng, lambda e: e.tensor_copy(out=out, in_=in_), R, W)

    def cast_any(self, out, in_, R, W):
        eng = ("pool", "dve", "act")[self.cast_rr % 3]
        self.cast_rr += 1
        self.copy(eng, out, in_, R, W)

    def memset(self, eng, ap, val, W):
        self.S.op(eng, lambda e: e.memset(ap, val), (), W)

    def rstd(self, ssq, tmp, out, n, eps_ap, R, W):
        self.act(tmp, ssq, AF.Ln, R, W, scale=1.0 / n, bias=eps_ap)
        self.act(out, tmp, AF.Exp, W, W, scale=-0.5)


def v3(ap, a):
    return ap.rearrange("p (a b) -> p a b", a=a)


def load_consts(k, dr):
    S = k.S
    c = {}
    c["ident"] = k.b16(128, "ident")
    c["rperm"] = k.b16(128, "rperm")
    c["ones"] = k.b16(128, "ones")
    c["maskT2"] = k.b16(256, "maskT2")
    c["negm"] = k.b16(128, "negm")
    c["selb"] = k.b16(256, "selb")
    for i, nm in enumerate(("ident", "rperm", "ones")):
        S.dma("sp", c[nm].ap, dr["cst_bf"][:, i * 128:(i + 1) * 128], W=[c[nm].b])
    S.dma("sp", c["maskT2"].ap, dr["cst_bf"][:, 384:640], W=[c["maskT2"].b])
    S.dma("sp", c["negm"].ap, dr["cst_bf"][:, 640:768], W=[c["negm"].b])
    S.dma("sp", c["selb"].ap, dr["cst_bf"][:, 768:1024], W=[c["selb"].b])
    c["triU"] = k.f32(128, "triU")
    c["triL"] = k.f32(128, "triL")
    c["maskU4"] = k.f32(512, "maskU4")
    c["sel"] = k.f32(256, "sel")
    S.dma("sp", c["triU"].ap, dr["cst_f"][:, 0:128], W=[c["triU"].b])
    S.dma("sp", c["triL"].ap, dr["cst_f"][:, 128:256], W=[c["triL"].b])
    S.dma("sp", c["maskU4"].ap, dr["cst_f"][:, 256:768], W=[c["maskU4"].b])
    S.dma("sp", c["sel"].ap, dr["cst_f"][:, 768:1024], W=[c["sel"].b])
    c["eps"] = k.f32(16, "eps")
    k.memset("pool", c["eps"].ap[:, 0:1], EPS, [c["eps"].b])
    c["small"] = k.f32(64, "small")
    c["junk"] = k.b16(2048, "junk")
    return c


def bcast_load(k, tile_ap, b, dram_row):
    k.S.dma("sp", tile_ap, dram_row.partition_broadcast(128), W=[b])


def phase_A0(k, c, dr):
    S = k.S
    mF, mB = k.AF.mark(), k.AB.mark()
    w_in = k.b16(8 * GIN, "w_in")
    w3 = v3(w_in.ap, 8)
    m2 = k.AF.mark()
    HW = GIN // 2
    stg = [k.f32(HW, "stg%d" % i) for i in range(4)]
    n = 0
    for kc in range(8):
        for half in range(2):
            st = stg[n % 4]
            n += 1
            S.dma("sp", st.ap, dr["gla_w_in"][kc * 128:(kc + 1) * 128, half * HW:(half + 1) * HW], W=[st.b])
            k.cast_any(w3[:, kc, half * HW:(half + 1) * HW], st.ap, [st.b], [w_in.b])
    S.barrier()
    k.AF.release(m2)
    pre0 = k.f32(D, "pre0")
    bcast_load(k, pre0.ap, pre0.b, dr["pre_g"][0:1, :])
    ngf = k.f32(BR, "ngf")
    for h in range(4):
        bcast_load(k, ngf.ap[:, h * 512:(h + 1) * 512], ngf.b, dr["gla_norm_g"][0:1, :])
    wg2 = k.f32(512, "wg2")
    S.dma("sp", wg2.ap[0:16, :], dr["gla_w_g2"], W=[wg2.b])
    S.dma("sp", wg2.ap[16:17, :], dr["gla_b_g"], W=[wg2.b])
    lowT = [k.f32(128, "lowT%d" % i) for i in range(2)]
    for t in lowT:
        k.memset("pool", t.ap[0:32, :], 1.0, [t.b])
    xs = [k.f32(D, "xs%d" % i) for i in range(2)]
    ez = k.f32(512, "ez")
    sp = k.f32(512, "sp")
    ebT = k.f32(512, "ebT")
    enT = k.f32(512, "enT")
    erev = k.f32(512, "erev")
    Sf = [k.f32(512, "Sf%d" % h) for h in range(4)]
    Sb = [k.b16(512, "Sb%d" % h) for h in range(4)]
    for h in range(4):
        k.memset("pool", Sf[h].ap, 0.0, [Sf[h].b])
        k.memset("pool", Sb[h].ap, 0.0, [Sb[h].b])
    sm = [k.f32(16, "sm%d" % i) for i in range(2)]
    hb = [k.b16(D, "hb%d" % i) for i in range(2)]
    hT = [k.b16(D, "hT%d" % i) for i in range(2)]
    qtT = [k.b16(512, "qtT")] * 2
    ktT = [k.b16(512, "ktT")] * 2
    kend = [k.b16(512, "kend")] * 2
    vb = [k.b16(BR, "vb")] * 2
    sg = k.b16(BR, "sg")
    sgn = [k.b16(BR, "sgn")] * 2
    ATb = [k.b16(512, "ATb")] * 2
    og = [k.b16(BR, "og%d" % i) for i in range(2)]
    ogT = [k.b16(BR, "ogT%d" % i) for i in range(2)]
    ident = c["ident"]
    junk = c["junk"]
    eps = c["eps"]
    SCALE = 128 ** -0.5

    smp = [k.f32(16, "smp%d" % i) for i in range(2)]

    def pro_a(i):
        p = i % 2
        x = xs[p]
        s = smp[p]
        S.dma("sp", x.ap, dr["x"][i * 128:(i + 1) * 128, :], W=[x.b])
        k.act(junk.ap[:, 0:D], x.ap, AF.Square, [x.b], [junk.b, s.b], accum=s.ap[:, 0:1])
        k.rstd(s.ap[:, 0:1], s.ap[:, 1:2], s.ap[:, 2:3], D, eps.ap[:, 0:1], [s.b, eps.b], [s.b])
        h = hb[p]
        k.stt("dve", h.ap, x.ap, s.ap[:, 2:3], pre0.ap, ALU.mult, ALU.mult, [x.b, s.b, pre0.b], [h.b])

    def pro_b(i):
        p = i % 2
        h = hb[p]
        pb, pbb = k.banks(7)
        pv = pb.bitcast(BF16)
        for kc in range(8):
            k.tr(pv[:, kc * 128:(kc + 1) * 128], h.ap[:, kc * 128:(kc + 1) * 128], ident.ap, [h.b, ident.b], pbb)
        ht = hT[p]
        k.copy("act", ht.ap, pv, pbb, [ht.b])

    osb = k.f32(BR, "osb")
    smh = [k.f32(16, "smh%d" % i) for i in range(4)]
    prev_tail = [None]

    def tail(i, o_):
        pt, ptb = k.banks(5, 2)
        ptv = pt.bitcast(BF16)
        for cc in range(16):
            k.tr(ptv[:, cc * 128:(cc + 1) * 128], o_.ap[:, cc * 128:(cc + 1) * 128], ident.ap, [o_.b, ident.b], ptb)
        ot = ogT[i % 2]
        k.copy("act", ot.ap, ptv, ptb, [ot.b])
        S.dma("sp", dr["ogT"][i], ot.ap, R=[ot.b])

    def low(i):
        ht_ = hT[i % 2]
        pl, plb = k.banks(5)
        for kc in range(8):
            k.mm(pl[0:16, 0:128], w3[:, kc, 5120:5136], ht_.ap[:, kc * 128:(kc + 1) * 128], kc == 0, kc == 7,
                 [ht_.b, w_in.b], plb)
        k.copy("act", lowT[i % 2].ap[0:16, :], pl[0:16, 0:128], plb, [lowT[i % 2].b])

    pro_a(0)
    pro_b(0)
    low(0)
    for i in range(NT):
        p = i % 2
        ht = hT[p]
        hk = lambda kc: ht.ap[:, kc * 128:(kc + 1) * 128]
        lt = lowT[p]
        pkt, pktb = k.banks(4)
        for kc in range(8):
            k.mm(pkt, hk(kc), w3[:, kc, 512:1024], kc == 0, kc == 7, [ht.b, w_in.b], pktb)
        pz, pzb = k.banks(5)
        k.mm(pz, lt.ap[0:17, :], wg2.ap[0:17, :], True, True, [lt.b, wg2.b], pzb)
        k.act(ez.ap, pz, AF.Exp, pzb, [ez.b], scale=-1.0)
        k.act(sp.ap, ez.ap, AF.Ln, [ez.b], [sp.b], bias=1.0)
        pv4, pv4b = k.banks(0, 4)
        for cc in range(4):
            for kc in range(8):
                k.mm(pv4[:, cc * 512:(cc + 1) * 512], hk(kc), w3[:, kc, 1024 + cc * 512:1024 + (cc + 1) * 512],
                     kc == 0, kc == 7, [ht.b, w_in.b], [pv4b[cc]])
        v = vb[p]
        k.copy("act", v.ap, pv4, pv4b, [v.b])
        pc, pcb = k.banks(5)
        for hh in range(4):
            k.mm(pc[:, hh * 128:(hh + 1) * 128], sp.ap[:, hh * 128:(hh + 1) * 128], c["triU"].ap, True, True,
                 [sp.b, c["triU"].b], pcb)
        pr, prb = k.banks(6)
        k.mm(pr, c["triL"].ap, sp.ap, True, True, [sp.b, c["triL"].b], prb)
        k.act(ebT.ap, pc, AF.Exp, pcb, [ebT.b])
        k.act(enT.ap, pc, AF.Exp, pcb, [enT.b], scale=-1.0)
        k.act(erev.ap, pr, AF.Exp, prb, [erev.b])
        ke = kend[p]
        k.tt("dve", ke.ap, pkt, erev.ap, ALU.mult, pktb + [erev.b], [ke.b])
        if i + 1 < NT:
            pro_a(i + 1)
        pq, pqb = k.banks(5)
        for hh in range(4):
            for kc in range(8):
                k.mm(pq[:, hh * 128:(hh + 1) * 128], w3[:, kc, hh * 128:(hh + 1) * 128], hk(kc), kc == 0, kc == 7,
                     [ht.b, w_in.b], pqb)
        pk, pkb = k.banks(6)
        for hh in range(4):
            for kc in range(8):
                k.mm(pk[:, hh * 128:(hh + 1) * 128], w3[:, kc, 512 + hh * 128:512 + (hh + 1) * 128], hk(kc), kc == 0,
                     kc == 7, [ht.b, w_in.b], pkb)
        qt_, kt_ = qtT[p], ktT[p]
        k.stt("dve", qt_.ap, pq, SCALE, ebT.ap, ALU.mult, ALU.mult, pqb + [ebT.b], [qt_.b])
        k.tt("dve", kt_.ap, pk, enT.ap, ALU.mult, pkb + [enT.b], [kt_.b])
        pg4, pg4b = k.banks(0, 4)
        for cc in range(4):
            for kc in range(8):
                k.mm(pg4[:, cc * 512:(cc + 1) * 512], hk(kc), w3[:, kc, 3072 + cc * 512:3072 + (cc + 1) * 512],
                     kc == 0, kc == 7, [ht.b, w_in.b], [pg4b[cc]])
        if i + 1 < NT:
            pro_b(i + 1)
        k.act(sg.ap, pg4, AF.Silu, pg4b, [sg.b])
        sn = sgn[p]
        k.tt("pool", sn.ap, sg.ap, ngf.ap, ALU.mult, [sg.b, ngf.b], [sn.b])
        if prev_tail[0] is not None:
            tail(*prev_tail[0])
            prev_tail[0] = None
        pa, pab = k.banks(4)
        for hh in range(4):
            sl = slice(hh * 128, (hh + 1) * 128)
            k.mm(pa[:, sl], kt_.ap[:, sl], qt_.ap[:, sl], True, True, [kt_.b, qt_.b], pab)
        at = ATb[p]
        k.tt("dve", at.ap, pa, c["maskU4"].ap, ALU.mult, pab + [c["maskU4"].b], [at.b])
        if i + 1 < NT:
            low(i + 1)
        pkvs = []
        if i < NT - 1:
            for hh in range(4):
                sl = slice(hh * 128, (hh + 1) * 128)
                vs = slice(hh * 512, (hh + 1) * 512)
                pkv, pkvb = k.banks(4 + hh)
                k.mm(pkv, ke.ap[:, sl], v.ap[:, vs], True, True, [ke.b, v.b], pkvb)
                pkvs.append((pkv, pkvb))
        po, pob = k.banks(0, 4)
        for hh in range(4):
            sl = slice(hh * 128, (hh + 1) * 128)
            vs = slice(hh * 512, (hh + 1) * 512)
            k.mm(po[:, vs], at.ap[:, sl], v.ap[:, vs], True, False, [at.b, v.b], [pob[hh]])
            k.mm(po[:, vs], qt_.ap[:, sl], Sb[hh].ap, False, True, [qt_.b, Sb[hh].b], [pob[hh]])
        for hh, (pkv, pkvb) in enumerate(pkvs):
            k.stt("dve", Sf[hh].ap, Sf[hh].ap, ebT.ap[:, hh * 128 + 127:hh * 128 + 128], pkv, ALU.mult, ALU.add,
                  [Sf[hh].b, ebT.b] + pkvb, [Sf[hh].b])
            k.copy("pool", Sb[hh].ap, Sf[hh].ap, [Sf[hh].b], [Sb[hh].b])
        o_ = og[p]
        for hh in range(4):
            vs = slice(hh * 512, (hh + 1) * 512)
            sh = smh[hh]
            k.act(junk.ap[:, 0:512], po[:, vs], AF.Square, [pob[hh]], [junk.b, sh.b], accum=sh.ap[:, 0:1])
            k.copy("dve", osb.ap[:, vs], po[:, vs], [pob[hh]], [osb.b])
            k.rstd(sh.ap[:, 0:1], sh.ap[:, 1:2], sh.ap[:, 2:3], 512, eps.ap[:, 0:1], [sh.b, eps.b], [sh.b])
        for hh in range(4):
            vs = slice(hh * 512, (hh + 1) * 512)
            k.stt("dve", o_.ap[:, vs], osb.ap[:, vs], smh[hh].ap[:, 2:3], sn.ap[:, vs], ALU.mult, ALU.mult,
                  [osb.b, smh[hh].b, sn.b], [o_.b])
        prev_tail[0] = (i, o_)
    tail(*prev_tail[0])
    S.barrier()
    k.AF.release(mF)
    k.AB.release(mB)


def phase_F(k, c, dr, w_out_d, post_row, xin_d, out_d, pre_row=None, h1T=None):
    S = k.S
    mF, mB = k.AF.mark(), k.AB.mark()
    w_out = k.b16(16 * D, "w_out")
    wo3 = v3(w_out.ap, 16)
    wbuf = [S.buf("w_out_p%d" % j) for j in range(8)]
    stg = [k.f32(2 * D, "stgF%d" % i) for i in range(2)]
    for j in range(8):
        st = stg[j % 2]
        S.dma("sp", v3(st.ap, 2), w_out_d[j * 256:(j + 1) * 256, :].rearrange("(c p) n -> p c n", p=128), W=[st.b])
        k.copy(("dve", "act")[j % 2], w_out.ap[:, j * 2 * D:(j + 1) * 2 * D], st.ap, [st.b], [wbuf[j]])
    postg = k.f32(D, "postg")
    bcast_load(k, postg.ap, postg.b, post_row)
    if h1T is not None:
        preg = k.f32(D, "preg")
        bcast_load(k, preg.ap, preg.b, pre_row)
        h1 = [k.b16(D, "h1_%d" % i) for i in range(2)]
    ogs = [k.b16(BR, "ogs%d" % i) for i in range(4)]
    xs = [k.f32(D, "xsF%d" % i) for i in range(4)]
    tmp = [k.f32(D, "tmpF%d" % i) for i in range(2)]
    x1 = [k.f32(D, "x1F%d" % i) for i in range(2)]
    sm = [k.f32(16, "smF%d" % i) for i in range(2)]
    junk, eps, ident = c["junk"], c["eps"], c["ident"]

    def load(i):
        o = ogs[i % 4]
        S.dma("sp", o.ap, dr["ogT"][i], W=[o.b])
        x = xs[i % 4]
        S.dma("sp", x.ap, xin_d[i * 128:(i + 1) * 128, :], W=[x.b])

    def ymm(i):
        o = ogs[i % 4]
        py, pyb = k.banks(2 * (i % 3), 2)
        for half in range(2):
            for cc in range(16):
                k.mm(py[:, half * 512:(half + 1) * 512], o.ap[:, cc * 128:(cc + 1) * 128],
                     wo3[:, cc, half * 512:(half + 1) * 512], cc == 0, cc == 15, [o.b, wbuf[cc // 2]], [pyb[half]])
        return py, pyb

    def h1_transposes(i):
        hh = h1[i % 2]
        pb, pbb = k.banks(6 + (i % 2))
        pv = pb.bitcast(BF16)
        for kc in range(8):
            k.tr(pv[:, kc * 128:(kc + 1) * 128], hh.ap[:, kc * 128:(kc + 1) * 128], ident.ap, [hh.b, ident.b], pbb)
        k.copy("act", v3(h1T.ap, 8)[:, :, i * 128:(i + 1) * 128], v3(pv, 8), pbb, [h1T.b])

    load(0)
    load(1)
    load(2)
    yq = [ymm(0), ymm(1)]
    for i in range(NT):
        if i + 3 < NT:
            load(i + 3)
        x, s = xs[i % 4], sm[i % 2]
        py, pyb = yq.pop(0)
        if i + 2 < NT:
            yq.append(ymm(i + 2))
        k.act(junk.ap[:, 0:D], py, AF.Square, pyb, [junk.b, s.b], accum=s.ap[:, 0:1])
        k.rstd(s.ap[:, 0:1], s.ap[:, 1:2], s.ap[:, 2:3], D, eps.ap[:, 0:1], [s.b, eps.b], [s.b])
        t = tmp[i % 2]
        k.stt("dve", t.ap, py, s.ap[:, 2:3], postg.ap, ALU.mult, ALU.mult, pyb + [s.b, postg.b], [t.b])
        xo = x1[i % 2]
        k.tt("pool", xo.ap, t.ap, x.ap, ALU.add, [t.b, x.b], [xo.b])
        S.dma("sp", out_d[i * 128:(i + 1) * 128, :], xo.ap, R=[xo.b])
        if h1T is not None:
            k.act(junk.ap[:, 0:D], xo.ap, AF.Square, [xo.b], [junk.b, s.b], accum=s.ap[:, 4:5])
            k.rstd(s.ap[:, 4:5], s.ap[:, 5:6], s.ap[:, 6:7], D, eps.ap[:, 0:1], [s.b, eps.b], [s.b])
            hh = h1[i % 2]
            k.stt("dve", hh.ap, xo.ap, s.ap[:, 6:7], preg.ap, ALU.mult, ALU.mult, [xo.b, s.b, preg.b], [hh.b])
            if i >= 1:
                h1_transposes(i - 1)
    if h1T is not None:
        h1_transposes(NT - 1)
    S.barrier()
    k.AF.release(mF)
    k.AB.release(mB)


def phase_A1(k, c, dr, h1T):
    S = k.S
    mF, mB = k.AF.mark(), k.AB.mark()
    h3 = v3(h1T.ap, 8)
    cst = [k.f32(512, "cosT%d" % i) for i in range(2)]
    snt = [k.f32(512, "sinT%d" % i) for i in range(2)]
    lv = k.f32(256, "lv")
    for j, nm in enumerate(("diff_lam_q1", "diff_lam_k1", "diff_lam_q2", "diff_lam_k2")):
        bcast_load(k, lv.ap[:, j * 64:(j + 1) * 64], lv.b, dr[nm])
    sc = k.f32(16, "scA1")
    junkf = k.f32(64, "junkf")
    k.stt("dve", junkf.ap, lv.ap[:, 0:64], 1.0, lv.ap[:, 64:128], ALU.mult, ALU.mult, [lv.b], [junkf.b, sc.b],
          accum=sc.ap[:, 0:1])
    k.stt("dve", junkf.ap, lv.ap[:, 128:192], 1.0, lv.ap[:, 192:256], ALU.mult, ALU.mult, [lv.b], [junkf.b, sc.b],
          accum=sc.ap[:, 1:2])
    k.act(sc.ap[:, 2:4], sc.ap[:, 0:2], AF.Exp, [sc.b], [sc.b])
    k.tt("dve", sc.ap[:, 4:5], sc.ap[:, 3:4], sc.ap[:, 2:3], ALU.subtract, [sc.b], [sc.b])
    k.ts("dve", sc.ap[:, 5:6], sc.ap[:, 4:5], -LAM_INIT, ALU.add, [sc.b], [sc.b])
    S.dma("sp", sc.ap[:, 6:7], dr["diff_norm_g"].rearrange("o v -> v o"), W=[sc.b])
    k.ts("dve", sc.ap[:, 7:8], sc.ap[:, 6:7], 1.0 - LAM_INIT, ALU.mult, [sc.b], [sc.b])
    neglam = sc.ap[:, 5:6]
    gcol = sc.ap[:, 7:8]

    stg = [k.f32(4 * 512, "stgA%d" % i) for i in range(2)]
    wh = [k.b16(8 * 512, "wh%d" % i) for i in range(2)]
    QT = k.b16(T, "QT")
    KT = k.b16(T, "KT")
    V = k.b16(T, "V")
    sgT = k.b16(T, "sgT")
    qraw = [k.b16(512, "qraw%d" % i) for i in range(2)]
    t1 = [k.f32(512, "t1_%d" % i) for i in range(2)]
    t2 = [k.f32(512, "t2_%d" % i) for i in range(2)]
    PT = [k.b16(1024, "PT%d" % i) for i in range(3)]
    rlb = k.b16(512, "rlb")
    lnl = k.f32(1024, "lnl")
    o01 = k.f32(1024, "o01")
    of = k.f32(512, "of")
    sq = k.b16(512, "sq")
    lnt = k.f32(512, "lnt")
    rst = k.f32(512, "rst")
    tf = k.f32(512, "tf")
    ogh = [k.b16(512, "ogh%d" % i) for i in range(2)]
    ones, rperm, maskT2, eps = c["ones"], c["rperm"], c["maskT2"], c["eps"]
    ident, negm = c["ident"], c["negm"]
    pO, pOb = k.banks(4, 2)
    pL, pLb = k.banks(6, 1)
    pX, pXb = k.banks(7, 1)
    selb = c["selb"]
    nq = 0
    npt = 0
    nog = 0

    def dma_w(h):
        for half in range(2):
            st = stg[half]
            s3 = v3(st.ap, 4)
            for blk in range(4):
                S.dma("sp", s3[:, :, blk * 128:(blk + 1) * 128],
                      dr["diff_w_in"][half * 512:(half + 1) * 512, blk * BR + h * 128:blk * BR + (h + 1) * 128]
                      .rearrange("(k p) n -> p k n", p=128), W=[st.b])

    def cast_w(h):
        for half in range(2):
            st = stg[half]
            k.copy("dve", wh[h % 2].ap[:, half * 2048:(half + 1) * 2048], st.ap, [st.b], [wh[h % 2].b])

    ntab = [0]

    def load_tab(tt_):
        j = tt_ % 2
        S.dma("sp", cst[j].ap, dr["cosT"][:, tt_ * 512:(tt_ + 1) * 512], W=[cst[j].b])
        S.dma("sp", snt[j].ap, dr["sinT"][:, tt_ * 512:(tt_ + 1) * 512], W=[snt[j].b])

    dma_w(0)
    cast_w(0)
    load_tab(0)
    load_tab(1)
    pend = []
    for h in range(DBG["heads"]):
        w3 = v3(wh[h % 2].ap, 8)
        wb = wh[h % 2].b
        for tt_ in range(8 if DBG["inproj"] else 0):
            ts_ = slice(tt_ * 512, (tt_ + 1) * 512)
            cosT, sinT = cst[tt_ % 2], snt[tt_ % 2]
            if tt_ >= 1:
                for _ in range(3):
                    if pend:
                        _, fn, q_, h_ = pend.pop(0)
                        fn(q_, h_)
            pqs = []
            for blk in (0, 1):
                pq, pqb = k.nextbank(1, 0, 7)
                for kc in range(8):
                    k.mm(pq, w3[:, kc, blk * 128:(blk + 1) * 128], h3[:, kc, ts_], kc == 0, kc == 7, [wb, h1T.b], pqb)
                pqs.append((pq, pqb))
            pg, pgb = k.nextbank(1, 0, 7)
            for kc in range(8):
                k.mm(pg, w3[:, kc, 384:512], h3[:, kc, ts_], kc == 0, kc == 7, [wb, h1T.b], pgb)
            pv, pvb = k.nextbank(1, 0, 7)
            for s_ in range(4):
                for kc in range(8):
                    k.mm(pv[:, s_ * 128:(s_ + 1) * 128], h3[:, kc, tt_ * 512 + s_ * 128:tt_ * 512 + (s_ + 1) * 128],
                         w3[:, kc, 256:384], kc == 0, kc == 7, [wb, h1T.b], pvb)
            rots = []
            for blk, dst in ((0, QT), (1, KT)):
                pq, pqb = pqs[blk]
                qr = qraw[nq % 2]
                a1, a2 = t1[nq % 2], t2[nq % 2]
                nq += 1
                k.copy("act", qr.ap, pq, pqb, [qr.b])
                rots.append((pq, pqb, qr, a1, a2, dst))
            k.act(sgT.ap[:, ts_], pg, AF.Silu, pgb, [sgT.b])
            k.copy("act", V.ap[:, ts_], pv, pvb, [V.b])
            for (pq, pqb, qr, a1, a2, dst) in rots:
                pr, prb = k.nextbank(1, 0, 7)
                k.mm(pr, rperm.ap, qr.ap, True, True, [rperm.b, qr.b], prb)
                k.tt("dve", a1.ap, pq, cosT.ap, ALU.mult, pqb + [cosT.b], [a1.b])
                k.tt("dve", a2.ap, pr, sinT.ap, ALU.mult, prb + [sinT.b], [a2.b])
                k.tt("pool", dst.ap[:, ts_], a1.ap, a2.ap, ALU.add, [a1.b, a2.b], [dst.b])
            if tt_ + 2 < 8:
                load_tab(tt_ + 2)
        if h + 1 < 16:
            dma_w(h + 1)
        blocks = [(qt, kb) for qt in range(DBG["nqt"] if DBG["att"] else 0) for kb in range(4 * qt + 4)]

        def qk_exp(qt, kb):
            nonlocal npt
            j = kb - 4 * qt
            c0 = max(j, 0) * 128
            ps2, ps2b = k.banks(2 * (npt % 2), 2)
            pt_ = PT[npt % 3]
            npt += 1
            for cc in range(2):
                k.mm(ps2[:, cc * 512 + c0:(cc + 1) * 512], KT.ap[cc * 64:(cc + 1) * 64, kb * 128:(kb + 1) * 128],
                     QT.ap[cc * 64:(cc + 1) * 64, qt * 512 + c0:(qt + 1) * 512], True, j < 0, [KT.b, QT.b],
                     [ps2b[cc]])
            if j >= 0:
                for cc in range(2):
                    k.mm(ps2[:, cc * 512 + c0:cc * 512 + c0 + 128], ident.ap, negm.ap, False, True,
                         [ident.b, negm.b], [ps2b[cc]])
            p3 = v3(pt_.ap, 2)
            k.act(p3[:, :, c0:512], v3(ps2, 2)[:, :, c0:512], AF.Exp, ps2b, [pt_.b], scale=0.125)
            return pt_, c0

        def av(qt, kb, pt_, c0):
            nkb = 4 * qt + 4
            for cc in range(2):
                k.mmt(pL[32 * cc:32 * cc + 32, c0:512], ones.ap[:, 0:32], pt_.ap[:, cc * 512 + c0:(cc + 1) * 512],
                      kb == 0, kb == nkb - 1, (0, 32 * cc), [ones.b, pt_.b], pLb)
            for cc in range(2):
                k.mm(pO[:, cc * 512 + c0:(cc + 1) * 512], V.ap[:, kb * 128:(kb + 1) * 128],
                     pt_.ap[:, cc * 512 + c0:(cc + 1) * 512], kb == 0, kb == nkb - 1, [V.b, pt_.b], [pOb[cc]])

        def epi1(qt, hd):
            k.act(lnl.ap[0:64, 0:512], pL[0:64, :], AF.Ln, pLb, [lnl.b])
            k.copy("dve", o01.ap[:, 0:512], pO[:, 0:512], [pOb[0]], [o01.b])
            k.copy("dve", o01.ap[:, 512:1024], pO[:, 512:1024], [pOb[1]], [o01.b])
            k.act(rlb.ap[0:64, 0:512], lnl.ap[0:64, 0:512], AF.Exp, [lnl.b], [rlb.b], scale=-1.0)

        def st_b0(qt, hd):
            k.mm(pX, selb.ap[0:64, 0:128], rlb.ap[0:64, 0:512], True, True, [selb.b, rlb.b], pXb)

        def st_m0(qt, hd):
            k.tt("dve", o01.ap[:, 0:512], o01.ap[:, 0:512], pX, ALU.mult, [o01.b] + pXb, [o01.b])

        def st_b1(qt, hd):
            k.mm(pX, selb.ap[0:64, 128:256], rlb.ap[0:64, 0:512], True, True, [selb.b, rlb.b], pXb)

        def st_m1(qt, hd):
            k.tt("dve", o01.ap[:, 512:1024], o01.ap[:, 512:1024], pX, ALU.mult, [o01.b] + pXb, [o01.b])
            k.stt("dve", of.ap, o01.ap[:, 512:1024], neglam, o01.ap[:, 0:512], ALU.mult, ALU.add, [o01.b, sc.b], [of.b])

        def st_sq(qt, hd):
            k.act(sq.ap, of.ap, AF.Square, [of.b], [sq.b])

        def st_ss(qt, hd):
            k.mm(pX, ones.ap, sq.ap, True, True, [ones.b, sq.b], pXb)

        def st_ln(qt, hd):
            k.act(lnt.ap, pX, AF.Ln, pXb + [eps.b], [lnt.b], scale=1.0 / 128, bias=eps.ap[:, 0:1])

        def st_ex(qt, hd):
            k.act(rst.ap, lnt.ap, AF.Exp, [lnt.b], [rst.b], scale=-0.5)

        def st_fin(qt, hd):
            nonlocal nog
            qs = slice(qt * 512, (qt + 1) * 512)
            k.tt("dve", tf.ap, of.ap, rst.ap, ALU.mult, [of.b, rst.b], [tf.b])
            og_ = ogh[nog % 2]
            nog += 1
            k.stt("dve", og_.ap, tf.ap, gcol, sgT.ap[:, qs], ALU.mult, ALU.mult, [tf.b, sc.b, sgT.b], [og_.b])
            S.dma("sp", dr["ogT"][qt * 4:(qt + 1) * 4, :, hd * 128:(hd + 1) * 128].rearrange("s p v -> p s v"),
                  v3(og_.ap, 4), R=[og_.b])

        STAGES0 = ((2, st_b0), (3, st_m0), (4, st_b1), (5, st_m1), (6, st_sq), (7, st_ss), (8, st_ln), (8, st_ex),
                   (8, st_fin))
        STAGES = ((2, st_b0), (3, st_m0), (4, st_b1), (5, st_m1), (6, st_sq), (7, st_ss), (8, st_ln), (9, st_ex),
                  (10, st_fin))

        cur = qk_exp(*blocks[0]) if blocks else None
        for bi, (qt, kb) in enumerate(blocks):
            nxt = qk_exp(*blocks[bi + 1]) if bi + 1 < len(blocks) else None
            av(qt, kb, *cur)
            while pend and bi >= pend[0][0]:
                _, fn, q_, h_ = pend.pop(0)
                fn(q_, h_)
            if kb == 4 * qt + 3 and DBG["epi"]:
                epi1(qt, h)
                for off, fn in (STAGES0 if qt == 0 else STAGES):
                    pend.append((bi + off, fn, qt, h))
            if bi == 10 and h + 1 < 16:
                cast_w(h + 1)
            if bi == 20 and h + 1 < 16:
                load_tab(0)
                load_tab(1)
            cur = nxt
        pend = [(-1, fn, q_, h_) for (_, fn, q_, h_) in pend]
    for _, fn, q_, h_ in pend:
        fn(q_, h_)
    S.barrier()
    k.AF.release(mF)
    k.AB.release(mB)


NAF = 16 * 1024
NAB = 68 * 1024

_INPUT_NAMES = ("x", "pre_g", "post_g", "gla_w_in", "gla_w_g2", "gla_b_g", "gla_norm_g", "gla_w_out",
                "diff_w_in", "diff_lam_q1", "diff_lam_k1", "diff_lam_q2", "diff_lam_k2", "diff_norm_g", "diff_w_out")
_SHAPES = {
    "x": [T, D], "pre_g": [2, D], "post_g": [2, D], "gla_w_in": [D, GIN], "gla_w_g2": [16, 512],
    "gla_b_g": [1, 512], "gla_norm_g": [1, 512], "gla_w_out": [BR, D], "diff_w_in": [D, 4 * BR],
    "diff_lam_q1": [1, 64], "diff_lam_k1": [1, 64], "diff_lam_q2": [1, 64], "diff_lam_k2": [1, 64],
    "diff_norm_g": [1, 128], "diff_w_out": [BR, D],
}


def build(mode="both"):
    nc = bass.Bass("TRN2", target_bir_lowering=False)
    dr = {}
    for nm in _INPUT_NAMES:
        if mode == "L1" and nm == "x":
            continue
        dr[nm] = nc.dram_tensor(nm, _SHAPES[nm], F32, kind="ExternalInput").ap()
    dr["cst_bf"] = nc.dram_tensor("cst_bf", [128, 1024], BF16, kind="ExternalInput").ap()
    dr["cst_f"] = nc.dram_tensor("cst_f", [128, 1024], F32, kind="ExternalInput").ap()
    dr["cosT"] = nc.dram_tensor("cosT", [128, T], F32, kind="ExternalInput").ap()
    dr["sinT"] = nc.dram_tensor("sinT", [128, T], F32, kind="ExternalInput").ap()
    dr["ogT"] = nc.dram_tensor("ogT_scr", [NT, 128, BR], BF16, kind="Internal").ap()
    if mode == "both":
        dr["x1"] = nc.dram_tensor("x1_scr", [T, D], F32, kind="Internal").ap()
        dr["out"] = nc.dram_tensor("out", [T, D], F32, kind="ExternalOutput").ap()
    elif mode == "L0":
        dr["x1"] = nc.dram_tensor("x1", [T, D], F32, kind="ExternalOutput").ap()
        dr["h1T_d"] = nc.dram_tensor("h1T", [128, 8 * T], BF16, kind="ExternalOutput").ap()
    else:
        dr["x1"] = nc.dram_tensor("x1", [T, D], F32, kind="ExternalInput").ap()
        dr["h1T_d"] = nc.dram_tensor("h1T", [128, 8 * T], BF16, kind="ExternalInput").ap()
        dr["out"] = nc.dram_tensor("out", [T, D], F32, kind="ExternalOutput").ap()

    with ExitStack() as st:
        S = Sched(nc, st)
        af = st.enter_context(nc.sbuf_tensor("arena_f", [128, NAF], F32))
        ab = st.enter_context(nc.sbuf_tensor("arena_b", [128, NAB], BF16))
        ps = st.enter_context(nc.psum_tensor("ps", [128, 4096], F32))
        k = K(nc, S, Arena(af, NAF), Arena(ab, NAB), ps)
        c = load_consts(k, dr)
        if mode in ("both", "L0"):
            phase_A0(k, c, dr)
        h1T = k.b16(8 * T, "h1T")
        if mode in ("both", "L0"):
            phase_F(k, c, dr, dr["gla_w_out"], dr["post_g"][0:1, :], dr["x"], dr["x1"],
                    pre_row=dr["pre_g"][1:2, :], h1T=h1T)
        if mode == "L0":
            for j in range(8):
                S.dma("sp", dr["h1T_d"][:, j * T:(j + 1) * T], h1T.ap[:, j * T:(j + 1) * T], R=[h1T.b])
            S.barrier()
        if mode == "L1":
            for j in range(8):
                S.dma("sp", h1T.ap[:, j * T:(j + 1) * T], dr["h1T_d"][:, j * T:(j + 1) * T], W=[h1T.b])
        if mode in ("both", "L1"):
            phase_A1(k, c, dr, h1T)
            if DBG["F1"]:
                phase_F(k, c, dr, dr["diff_w_out"], dr["post_g"][1:2, :], dr["x1"], dr["out"])
        S.barrier()
        print("ops", S.nops, "arena peaks f32 %d bf16 %d" % (k.AF.peak, k.AB.peak), "sems", sum(1 for b in S.allbufs if b.dsem is not None))
        with nc.Block() as block:
            S.replay(block)
    return nc


def _consts():
    bf = ml_dtypes.bfloat16
    p = np.arange(128)
    ident = np.eye(128, dtype=np.float32)
    perm = np.where((p % 64) < 32, p + 32, p - 32)
    rperm = np.zeros((128, 128), np.float32)
    rperm[perm, p] = 1.0
    ones = np.ones((128, 128), np.float32)
    maskT = (p[None, :] >= p[:, None]).astype(np.float32)
    negm = np.where(p[None, :] >= p[:, None], 0.0, -30000.0).astype(np.float32)
    selb = np.zeros((128, 256), np.float32)
    selb[0, 0:128] = 1.0
    selb[32, 128:256] = 1.0
    cst_bf = np.concatenate([ident, rperm, ones, maskT, maskT, negm, selb], 1).astype(bf)
    triU = np.where(p[:, None] <= p[None, :], -1.0 / 16.0, 0.0).astype(np.float32)
    triL = np.where(p[:, None] > p[None, :], -1.0 / 16.0, 0.0).astype(np.float32)
    maskU = (p[:, None] <= p[None, :]).astype(np.float32)
    sel = np.zeros((128, 256), np.float32)
    sel[0, 0:128] = 1.0
    sel[32, 128:256] = 1.0
    cst_f = np.concatenate([triU, triL, maskU, maskU, maskU, maskU, sel], 1).astype(np.float32)
    inv_freq = (1.0 / (np.float32(10000.0) ** (np.arange(0, 64, 2, dtype=np.float32) / np.float32(64)))).astype(np.float32)
    pos = np.arange(T, dtype=np.float32)
    ang = (pos[:, None] * inv_freq[None, :]).astype(np.float32)
    cos = np.cos(ang).astype(np.float32).T
    sin = np.sin(ang).astype(np.float32).T
    cosT = np.concatenate([cos, cos, cos, cos], 0)
    sinT = np.concatenate([-sin, sin, -sin, sin], 0)
    return {"cst_bf": np.ascontiguousarray(cst_bf), "cst_f": np.ascontiguousarray(cst_f),
            "cosT": np.ascontiguousarray(cosT), "sinT": np.ascontiguousarray(sinT)}


_NC_CACHE = {}


def _get_nc(mode):
    if mode not in _NC_CACHE:
        _NC_CACHE[mode] = build(mode)
    return _NC_CACHE[mode]


def _shared_maps(inputs):
    m = dict(_consts())
    for nm in _INPUT_NAMES:
        if nm == "x":
            continue
        a = np.asarray(inputs[nm], dtype=np.float32)
        if nm in ("pre_g", "post_g"):
            m[nm] = np.ascontiguousarray(a)
        else:
            m[nm] = np.ascontiguousarray(a.reshape(_SHAPES[nm]))
    return m


FUSED = True


def kernel(**inputs):
    x = np.asarray(inputs["x"], dtype=np.float32)
    shared = _shared_maps(inputs)
    cores = list(range(NCORES))
    if FUSED:
        nc = _get_nc("both")
        in_maps = [dict(shared, x=np.ascontiguousarray(x[b])) for b in cores]
        res = run_bass_kernel_spmd(nc, in_maps, core_ids=cores)
        return np.stack([np.asarray(r["out"]) for r in res.results], 0).astype(np.float32)
    nc0 = _get_nc("L0")
    in_maps = [dict(shared, x=np.ascontiguousarray(x[b])) for b in cores]
    r0 = run_bass_kernel_spmd(nc0, in_maps, core_ids=cores).results
    nc1 = _get_nc("L1")
    in_maps = [dict(shared, x1=np.asarray(r0[b]["x1"]), h1T=np.asarray(r0[b]["h1T"])) for b in cores]
    r1 = run_bass_kernel_spmd(nc1, in_maps, core_ids=cores).results
    return np.stack([np.asarray(r["out"]) for r in r1], 0).astype(np.float32)
```

```python
import math
from contextlib import ExitStack

import numpy as np
import ml_dtypes
import concourse.bass as bass
import concourse.mybir as mybir
from concourse.bass_utils import run_bass_kernel_spmd

F32 = mybir.dt.float32
BF16 = mybir.dt.bfloat16
AF = mybir.ActivationFunctionType
ALU = mybir.AluOpType

T = 4096
D = 1024
BR = 2048
NT = T // 128
GIN = 5136
EPS = 1e-6
LAM_INIT = 0.8 - 0.6 * math.exp(-0.3 * 1)
NCORES = 8
DBG = {"heads": 16, "nqt": 8, "inproj": True, "att": True, "epi": True, "F1": True, "nkb": 99, "ipp": "qgv", "rope": True}

ENGS = ("pe", "act", "dve", "pool", "sp")


class Buf:
    __slots__ = ("name", "w", "r", "dsem", "dcnt", "excl")

    def __init__(self, name):
        self.name = name
        self.excl = False
        self.w = None
        self.r = []
        self.dsem = None
        self.dcnt = 0


class Rec:
    __slots__ = ("waits", "fn", "inc", "dma_inc", "pos", "val")

    def __init__(self, waits, fn):
        self.waits = waits
        self.fn = fn
        self.inc = False
        self.dma_inc = None
        self.pos = -1
        self.val = 0


class Sched:
    def __init__(self, nc, stack):
        self.nc = nc
        self.stack = stack
        self.ops = {e: [] for e in ENGS}
        self.sem = {e: stack.enter_context(nc.semaphore("sem_" + e)) for e in ENGS}
        self.waited = {e: {} for e in ENGS}
        self.allbufs = []
        self.same_engine_sync = True
        self.same_engine_all = True
        self.nops = 0
        self.nbar = 0

    def buf(self, name):
        b = Buf("%s_%d" % (name, len(self.allbufs)))
        self.allbufs.append(b)
        return b

    def _dsem(self, b):
        if b.dsem is None:
            b.dsem = self.stack.enter_context(self.nc.semaphore("d_" + b.name))
        return b.dsem

    def _push(self, eng, rec):
        rec.pos = len(self.ops[eng])
        self.ops[eng].append(rec)

    def _resolve(self, eng, tok, waits, is_raw):
        if tok[0] == "e":
            _, e2, rec = tok
            if e2 == eng:
                if eng in ("pe", "sp") or not (self.same_engine_sync and (is_raw or self.same_engine_all)):
                    return
            key = "e_" + e2
            if self.waited[eng].get(key, -1) >= rec.pos:
                return
            self.waited[eng][key] = rec.pos
            rec.inc = True
            waits.append(("e", e2, rec))
        else:
            _, b, val = tok
            key = "d_" + b.name
            if self.waited[eng].get(key, 0) >= val:
                return
            self.waited[eng][key] = val
            waits.append(("d", b.dsem, val))

    def _deps(self, eng, R, W):
        waits = []
        for b in R:
            if b.w is not None:
                self._resolve(eng, b.w, waits, True)
            if b.excl:
                for t in b.r:
                    if t[0] != "e" or t[1] != eng:
                        self._resolve(eng, t, waits, False)
        for b in W:
            if b.w is not None:
                self._resolve(eng, b.w, waits, False)
            for t in b.r:
                self._resolve(eng, t, waits, False)
        return waits

    def op(self, eng, fn, R=(), W=()):
        waits = self._deps(eng, R, W)
        rec = Rec(waits, fn)
        self._push(eng, rec)
        tok = ("e", eng, rec)
        for b in R:
            b.r = [t for t in b.r if not (t[0] == "e" and t[1] == eng)]
            b.r.append(tok)
        for b in W:
            b.w = tok
            b.r = []
        self.nops += 1
        return rec

    def dma(self, q, out_ap, in_ap, R=(), W=()):
        waits = self._deps(q, R, W)
        owner = W[0] if W else R[0]
        sem = self._dsem(owner)
        if not W and owner.dcnt > 0:
            self._resolve(q, ("d", owner, owner.dcnt), waits, False)
        owner.dcnt += 16
        tok = ("d", owner, owner.dcnt)
        rec = Rec(waits, lambda e: e.dma_start(out=out_ap, in_=in_ap))
        rec.dma_inc = sem
        self._push(q, rec)
        for b in R:
            b.r.append(tok)
        for b in W:
            b.w = tok
            b.r = []
        self.nops += 1
        return rec

    def barrier(self):
        waits = []
        comp = ("pe", "act", "dve", "pool")
        last = {}
        for e in comp:
            recs = [r for r in self.ops[e] if r.fn is not None]
            if recs:
                last[e] = recs[-1]
                if self.waited["sp"].get("e_" + e, -1) < recs[-1].pos:
                    self.waited["sp"]["e_" + e] = recs[-1].pos
                    recs[-1].inc = True
                    waits.append(("e", e, recs[-1]))
        for b in self.allbufs:
            if b.dsem is not None and b.dcnt > self.waited["sp"].get("d_" + b.name, 0):
                self.waited["sp"]["d_" + b.name] = b.dcnt
                waits.append(("d", b.dsem, b.dcnt))
        sem_sp = self.sem["sp"]
        rec = Rec(waits, lambda e: e.sem_inc(sem_sp, 1))
        self._push("sp", rec)
        self.nbar += 1
        v = self.nbar
        for e in comp:
            self._push(e, Rec([("s", sem_sp, v)], None))
            for e2 in comp:
                if e2 in last:
                    self.waited[e]["e_" + e2] = last[e2].pos
            for b in self.allbufs:
                if b.dsem is not None:
                    self.waited[e]["d_" + b.name] = b.dcnt
        for b in self.allbufs:
            b.w = None
            b.r = []

    def replay(self, block):
        S = self
        for eng in ENGS:
            n = 0
            for rec in S.ops[eng]:
                if rec.inc:
                    n += 1
                    rec.val = n
        self.nsig = {eng: sum(1 for r in S.ops[eng] if r.inc) for eng in ENGS}

        def run(eng, e):
            for rec in S.ops[eng]:
                for w in rec.waits:
                    if w[0] == "e":
                        assert w[2].inc and w[2].val > 0
                        e.wait_ge(S.sem[w[1]], w[2].val)
                    else:
                        e.wait_ge(w[1], w[2])
                if rec.fn is None:
                    continue
                ins = rec.fn(e)
                if rec.dma_inc is not None:
                    ins.then_inc(rec.dma_inc, 16)
                elif rec.inc:
                    ins.then_inc(S.sem[eng], 1)

        @block.tensor
        def _(e):
            run("pe", e)

        @block.scalar
        def _(e):
            run("act", e)

        @block.vector
        def _(e):
            run("dve", e)

        @block.gpsimd
        def _(e):
            run("pool", e)

        @block.sync
        def _(e):
            run("sp", e)


class Arena:
    def __init__(self, ap, n):
        self.ap = ap
        self.n = n
        self.top = 0
        self.peak = 0

    def alloc(self, n):
        a = self.ap[:, self.top:self.top + n]
        self.top += (n + 15) // 16 * 16
        self.peak = max(self.peak, self.top)
        assert self.top <= self.n, ("arena overflow", self.top, self.n)
        return a

    def mark(self):
        return self.top

    def release(self, m):
        self.top = m


class Tile:
    __slots__ = ("ap", "b")

    def __init__(self, ap, b):
        self.ap = ap
        self.b = b


class K:
    def __init__(self, nc, S, AFa, ABa, ps):
        self.nc, self.S, self.AF, self.AB, self.ps = nc, S, AFa, ABa, ps
        self.bankb = [S.buf("bank%d" % i) for i in range(8)]
        for b in self.bankb:
            b.excl = True
        self.rr = 0
        self.cast_rr = 0

    def f32(self, n, name):
        return Tile(self.AF.alloc(n), self.S.buf(name))

    def b16(self, n, name):
        return Tile(self.AB.alloc(n), self.S.buf(name))

    def banks(self, i, n=1):
        return self.ps[:, i * 512:(i + n) * 512], [self.bankb[j] for j in range(i, i + n)]

    def nextbank(self, n=1, lo=0, hi=8):
        if self.rr < lo or self.rr + n > hi:
            self.rr = lo
        i = self.rr
        self.rr += n
        return self.banks(i, n)

    def mm(self, out, lhsT, rhs, start, stop, R, W):
        self.S.op("pe", lambda e: e.matmul(out, lhsT=lhsT, rhs=rhs, start=start, stop=stop), R, W)

    def mmt(self, out, lhsT, rhs, start, stop, tpos, R, W):
        self.S.op("pe", lambda e: e.matmul(out, lhsT=lhsT, rhs=rhs, start=start, stop=stop, tile_position=tpos), R, W)

    def tr(self, out, in_, ident, R, W):
        self.S.op("pe", lambda e: e.transpose(out=out, in_=in_, identity=ident), R, W)

    def act(self, out, in_, func, R, W, scale=None, bias=None, accum=None):
        kw = {}
        if scale is not None:
            kw["scale"] = scale
        if bias is not None:
            kw["bias"] = bias
        if accum is not None:
            kw["accum_out"] = accum
        self.S.op("act", lambda e: e.activation(out=out, in_=in_, func=func, **kw), R, W)

    def tt(self, eng, out, a, b, op, R, W):
        self.S.op(eng, lambda e: e.tensor_tensor(out=out, in0=a, in1=b, op=op), R, W)

    def stt(self, eng, out, a, sc, b, op0, op1, R, W, accum=None):
        if accum is None:
            self.S.op(eng, lambda e: e.scalar_tensor_tensor(out=out, in0=a, scalar=sc, in1=b, op0=op0, op1=op1), R, W)
        else:
            self.S.op(eng, lambda e: e.scalar_tensor_tensor(out=out, in0=a, scalar=sc, in1=b, op0=op0, op1=op1,
                                                           accum_out=accum), R, W)

    def ts(self, eng, out, a, s1, op0, R, W, s2=None, op1=None):
        if s2 is None:
            self.S.op(eng, lambda e: e.tensor_scalar(out=out, in0=a, scalar1=s1, scalar2=None, op0=op0), R, W)
        else:
            self.S.op(eng, lambda e: e.tensor_scalar(out=out, in0=a, scalar1=s1, scalar2=s2, op0=op0, op1=op1), R, W)

    def copy(self, eng, out, in_, R, W):
        if eng == "act":
            self.S.op("act", lambda e: e.copy(out=out, in_=in_), R, W)
        else:
            self.S.op(eng, lambda e: e.tensor_copy(out=out, in_=in_), R, W)

    def cast_any(self, out, in_, R, W):
        eng = ("pool", "dve", "act")[self.cast_rr % 3]
        self.cast_rr += 1
        self.copy(eng, out, in_, R, W)

    def memset(self, eng, ap, val, W):
        self.S.op(eng, lambda e: e.memset(ap, val), (), W)

    def rstd(self, ssq, tmp, out, n, eps_ap, R, W):
        self.act(tmp, ssq, AF.Ln, R, W, scale=1.0 / n, bias=eps_ap)
        self.act(out, tmp, AF.Exp, W, W, scale=-0.5)


def v3(ap, a):
    return ap.rearrange("p (a b) -> p a b", a=a)


def load_consts(k, dr):
    S = k.S
    c = {}
    c["ident"] = k.b16(128, "ident")
    c["rperm"] = k.b16(128, "rperm")
    c["ones"] = k.b16(128, "ones")
    c["maskT2"] = k.b16(256, "maskT2")
    c["negm"] = k.b16(128, "negm")
    c["selb"] = k.b16(256, "selb")
    for i, nm in enumerate(("ident", "rperm", "ones")):
        S.dma("sp", c[nm].ap, dr["cst_bf"][:, i * 128:(i + 1) * 128], W=[c[nm].b])
    S.dma("sp", c["maskT2"].ap, dr["cst_bf"][:, 384:640], W=[c["maskT2"].b])
    S.dma("sp", c["negm"].ap, dr["cst_bf"][:, 640:768], W=[c["negm"].b])
    S.dma("sp", c["selb"].ap, dr["cst_bf"][:, 768:1024], W=[c["selb"].b])
    c["triU"] = k.f32(128, "triU")
    c["triL"] = k.f32(128, "triL")
    c["maskU4"] = k.f32(512, "maskU4")
    c["sel"] = k.f32(256, "sel")
    S.dma("sp", c["triU"].ap, dr["cst_f"][:, 0:128], W=[c["triU"].b])
    S.dma("sp", c["triL"].ap, dr["cst_f"][:, 128:256], W=[c["triL"].b])
    S.dma("sp", c["maskU4"].ap, dr["cst_f"][:, 256:768], W=[c["maskU4"].b])
    S.dma("sp", c["sel"].ap, dr["cst_f"][:, 768:1024], W=[c["sel"].b])
    c["eps"] = k.f32(16, "eps")
    k.memset("pool", c["eps"].ap[:, 0:1], EPS, [c["eps"].b])
    c["small"] = k.f32(64, "small")
    c["junk"] = k.b16(2048, "junk")
    return c


def bcast_load(k, tile_ap, b, dram_row):
    k.S.dma("sp", tile_ap, dram_row.partition_broadcast(128), W=[b])


def phase_A0(k, c, dr):
    S = k.S
    mF, mB = k.AF.mark(), k.AB.mark()
    w_in = k.b16(8 * GIN, "w_in")
    w3 = v3(w_in.ap, 8)
    m2 = k.AF.mark()
    HW = GIN // 2
    stg = [k.f32(HW, "stg%d" % i) for i in range(4)]
    n = 0
    for kc in range(8):
        for half in range(2):
            st = stg[n % 4]
            n += 1
            S.dma("sp", st.ap, dr["gla_w_in"][kc * 128:(kc + 1) * 128, half * HW:(half + 1) * HW], W=[st.b])
            k.cast_any(w3[:, kc, half * HW:(half + 1) * HW], st.ap, [st.b], [w_in.b])
    S.barrier()
    k.AF.release(m2)
    pre0 = k.f32(D, "pre0")
    bcast_load(k, pre0.ap, pre0.b, dr["pre_g"][0:1, :])
    ngf = k.f32(BR, "ngf")
    for h in range(4):
        bcast_load(k, ngf.ap[:, h * 512:(h + 1) * 512], ngf.b, dr["gla_norm_g"][0:1, :])
    wg2 = k.f32(512, "wg2")
    S.dma("sp", wg2.ap[0:16, :], dr["gla_w_g2"], W=[wg2.b])
    S.dma("sp", wg2.ap[16:17, :], dr["gla_b_g"], W=[wg2.b])
    lowT = [k.f32(128, "lowT%d" % i) for i in range(2)]
    for t in lowT:
        k.memset("pool", t.ap[0:32, :], 1.0, [t.b])
    xs = [k.f32(D, "xs%d" % i) for i in range(2)]
    ez = k.f32(512, "ez")
    sp = k.f32(512, "sp")
    ebT = k.f32(512, "ebT")
    enT = k.f32(512, "enT")
    erev = k.f32(512, "erev")
    Sf = [k.f32(512, "Sf%d" % h) for h in range(4)]
    Sb = [k.b16(512, "Sb%d" % h) for h in range(4)]
    for h in range(4):
        k.memset("pool", Sf[h].ap, 0.0, [Sf[h].b])
        k.memset("pool", Sb[h].ap, 0.0, [Sb[h].b])
    sm = [k.f32(16, "sm%d" % i) for i in range(2)]
    hb = [k.b16(D, "hb%d" % i) for i in range(2)]
    hT = [k.b16(D, "hT%d" % i) for i in range(2)]
    qtT = [k.b16(512, "qtT")] * 2
    ktT = [k.b16(512, "ktT")] * 2
    kend = [k.b16(512, "kend")] * 2
    vb = [k.b16(BR, "vb")] * 2
    sg = k.b16(BR, "sg")
    sgn = [k.b16(BR, "sgn")] * 2
    ATb = [k.b16(512, "ATb")] * 2
    og = [k.b16(BR, "og%d" % i) for i in range(2)]
    ogT = [k.b16(BR, "ogT%d" % i) for i in range(2)]
    ident = c["ident"]
    junk = c["junk"]
    eps = c["eps"]
    SCALE = 128 ** -0.5

    smp = [k.f32(16, "smp%d" % i) for i in range(2)]

    def pro_a(i):
        p = i % 2
        x = xs[p]
        s = smp[p]
        S.dma("sp", x.ap, dr["x"][i * 128:(i + 1) * 128, :], W=[x.b])
        k.act(junk.ap[:, 0:D], x.ap, AF.Square, [x.b], [junk.b, s.b], accum=s.ap[:, 0:1])
        k.rstd(s.ap[:, 0:1], s.ap[:, 1:2], s.ap[:, 2:3], D, eps.ap[:, 0:1], [s.b, eps.b], [s.b])
        h = hb[p]
        k.stt("dve", h.ap, x.ap, s.ap[:, 2:3], pre0.ap, ALU.mult, ALU.mult, [x.b, s.b, pre0.b], [h.b])

    def pro_b(i):
        p = i % 2
        h = hb[p]
        pb, pbb = k.banks(7)
        pv = pb.bitcast(BF16)
        for kc in range(8):
            k.tr(pv[:, kc * 128:(kc + 1) * 128], h.ap[:, kc * 128:(kc + 1) * 128], ident.ap, [h.b, ident.b], pbb)
        ht = hT[p]
        k.copy("act", ht.ap, pv, pbb, [ht.b])

    osb = k.f32(BR, "osb")
    smh = [k.f32(16, "smh%d" % i) for i in range(4)]
    prev_tail = [None]

    def tail(i, o_):
        pt, ptb = k.banks(5, 2)
        ptv = pt.bitcast(BF16)
        for cc in range(16):
            k.tr(ptv[:, cc * 128:(cc + 1) * 128], o_.ap[:, cc * 128:(cc + 1) * 128], ident.ap, [o_.b, ident.b], ptb)
        ot = ogT[i % 2]
        k.copy("act", ot.ap, ptv, ptb, [ot.b])
        S.dma("sp", dr["ogT"][i], ot.ap, R=[ot.b])

    def low(i, bank=5):
        ht_ = hT[i % 2]
        pl, plb = k.banks(bank)
        for kc in range(8):
            k.mm(pl[0:16, 0:128], w3[:, kc, 5120:5136], ht_.ap[:, kc * 128:(kc + 1) * 128], kc == 0, kc == 7,
                 [ht_.b, w_in.b], plb)
        k.copy("act", lowT[i % 2].ap[0:16, :], pl[0:16, 0:128], plb, [lowT[i % 2].b])

    pro_a(0)
    pro_b(0)
    low(0)
    for i in range(NT):
        p = i % 2
        ht = hT[p]
        hk = lambda kc: ht.ap[:, kc * 128:(kc + 1) * 128]
        lt = lowT[p]
        pkt, pktb = k.banks(4)
        for kc in range(8):
            k.mm(pkt, hk(kc), w3[:, kc, 512:1024], kc == 0, kc == 7, [ht.b, w_in.b], pktb)
        pz, pzb = k.banks(5)
        k.mm(pz, lt.ap[0:17, :], wg2.ap[0:17, :], True, True, [lt.b, wg2.b], pzb)
        k.act(ez.ap, pz, AF.Exp, pzb, [ez.b], scale=-1.0)
        k.act(sp.ap, ez.ap, AF.Ln, [ez.b], [sp.b], bias=1.0)
        pv4, pv4b = k.banks(0, 4)
        for cc in range(4):
            for kc in range(8):
                k.mm(pv4[:, cc * 512:(cc + 1) * 512], hk(kc), w3[:, kc, 1024 + cc * 512:1024 + (cc + 1) * 512],
                     kc == 0, kc == 7, [ht.b, w_in.b], [pv4b[cc]])
        v = vb[p]
        k.copy("act", v.ap, pv4, pv4b, [v.b])
        pc, pcb = k.banks(5)
        for hh in range(4):
            k.mm(pc[:, hh * 128:(hh + 1) * 128], sp.ap[:, hh * 128:(hh + 1) * 128], c["triU"].ap, True, True,
                 [sp.b, c["triU"].b], pcb)
        pr, prb = k.banks(6)
        k.mm(pr, c["triL"].ap, sp.ap, True, True, [sp.b, c["triL"].b], prb)
        k.act(ebT.ap, pc, AF.Exp, pcb, [ebT.b])
        k.act(enT.ap, pc, AF.Exp, pcb, [enT.b], scale=-1.0)
        k.act(erev.ap, pr, AF.Exp, prb, [erev.b])
        ke = kend[p]
        k.tt("dve", ke.ap, pkt, erev.ap, ALU.mult, pktb + [erev.b], [ke.b])
        if i + 1 < NT:
            pro_a(i + 1)
        pq, pqb = k.banks(5)
        for hh in range(4):
            for kc in range(8):
                k.mm(pq[:, hh * 128:(hh + 1) * 128], w3[:, kc, hh * 128:(hh + 1) * 128], hk(kc), kc == 0, kc == 7,
                     [ht.b, w_in.b], pqb)
        pk, pkb = k.banks(6)
        for hh in range(4):
            for kc in range(8):
                k.mm(pk[:, hh * 128:(hh + 1) * 128], w3[:, kc, 512 + hh * 128:512 + (hh + 1) * 128], hk(kc), kc == 0,
                     kc == 7, [ht.b, w_in.b], pkb)
        qt_, kt_ = qtT[p], ktT[p]
        k.stt("dve", qt_.ap, pq, SCALE, ebT.ap, ALU.mult, ALU.mult, pqb + [ebT.b], [qt_.b])
        k.tt("dve", kt_.ap, pk, enT.ap, ALU.mult, pkb + [enT.b], [kt_.b])
        pg4, pg4b = k.banks(0, 4)
        for cc in range(4):
            for kc in range(8):
                k.mm(pg4[:, cc * 512:(cc + 1) * 512], hk(kc), w3[:, kc, 3072 + cc * 512:3072 + (cc + 1) * 512],
                     kc == 0, kc == 7, [ht.b, w_in.b], [pg4b[cc]])
        if i + 1 < NT:
            pro_b(i + 1)
        k.act(sg.ap, pg4, AF.Silu, pg4b, [sg.b])
        sn = sgn[p]
        k.tt("pool", sn.ap, sg.ap, ngf.ap, ALU.mult, [sg.b, ngf.b], [sn.b])
        pa, pab = k.banks(4)
        for hh in range(4):
            sl = slice(hh * 128, (hh + 1) * 128)
            k.mm(pa[:, sl], kt_.ap[:, sl], qt_.ap[:, sl], True, True, [kt_.b, qt_.b], pab)
        at = ATb[p]
        k.tt("dve", at.ap, pa, c["maskU4"].ap, ALU.mult, pab + [c["maskU4"].b], [at.b])
        if i + 1 < NT:
            low(i + 1, 7)
        if prev_tail[0] is not None:
            tail(*prev_tail[0])
            prev_tail[0] = None
        pkvs = []
        if i < NT - 1:
            for hh in range(4):
                sl = slice(hh * 128, (hh + 1) * 128)
                vs = slice(hh * 512, (hh + 1) * 512)
                pkv, pkvb = k.banks((4, 7, 5, 6)[hh])
                k.mm(pkv, ke.ap[:, sl], v.ap[:, vs], True, True, [ke.b, v.b], pkvb)
                pkvs.append((pkv, pkvb))
        po, pob = k.banks(0, 4)
        for hh in range(4):
            sl = slice(hh * 128, (hh + 1) * 128)
            vs = slice(hh * 512, (hh + 1) * 512)
            k.mm(po[:, vs], at.ap[:, sl], v.ap[:, vs], True, False, [at.b, v.b], [pob[hh]])
            k.mm(po[:, vs], qt_.ap[:, sl], Sb[hh].ap, False, True, [qt_.b, Sb[hh].b], [pob[hh]])
        for hh, (pkv, pkvb) in enumerate(pkvs):
            k.stt("dve", Sf[hh].ap, Sf[hh].ap, ebT.ap[:, hh * 128 + 127:hh * 128 + 128], pkv, ALU.mult, ALU.add,
                  [Sf[hh].b, ebT.b] + pkvb, [Sf[hh].b])
            k.copy("pool", Sb[hh].ap, Sf[hh].ap, [Sf[hh].b], [Sb[hh].b])
        o_ = og[p]
        for hh in range(4):
            vs = slice(hh * 512, (hh + 1) * 512)
            sh = smh[hh]
            k.act(junk.ap[:, 0:512], po[:, vs], AF.Square, [pob[hh]], [junk.b, sh.b], accum=sh.ap[:, 0:1])
            k.copy("dve", osb.ap[:, vs], po[:, vs], [pob[hh]], [osb.b])
            k.rstd(sh.ap[:, 0:1], sh.ap[:, 1:2], sh.ap[:, 2:3], 512, eps.ap[:, 0:1], [sh.b, eps.b], [sh.b])
        for hh in range(4):
            vs = slice(hh * 512, (hh + 1) * 512)
            k.stt("dve", o_.ap[:, vs], osb.ap[:, vs], smh[hh].ap[:, 2:3], sn.ap[:, vs], ALU.mult, ALU.mult,
                  [osb.b, smh[hh].b, sn.b], [o_.b])
        prev_tail[0] = (i, o_)
    tail(*prev_tail[0])
    S.barrier()
    k.AF.release(mF)
    k.AB.release(mB)


def phase_F(k, c, dr, w_out_d, post_row, xin_d, out_d, pre_row=None, h1T=None):
    S = k.S
    mF, mB = k.AF.mark(), k.AB.mark()
    w_out = k.b16(16 * D, "w_out")
    wo3 = v3(w_out.ap, 16)
    wbuf = [S.buf("w_out_p%d" % j) for j in range(8)]
    stg = [k.f32(2 * D, "stgF%d" % i) for i in range(2)]
    for j in range(8):
        st = stg[j % 2]
        S.dma("sp", v3(st.ap, 2), w_out_d[j * 256:(j + 1) * 256, :].rearrange("(c p) n -> p c n", p=128), W=[st.b])
        k.copy(("dve", "act")[j % 2], w_out.ap[:, j * 2 * D:(j + 1) * 2 * D], st.ap, [st.b], [wbuf[j]])
    postg = k.f32(D, "postg")
    bcast_load(k, postg.ap, postg.b, post_row)
    if h1T is not None:
        preg = k.f32(D, "preg")
        bcast_load(k, preg.ap, preg.b, pre_row)
        h1 = [k.b16(D, "h1_%d" % i) for i in range(2)]
    ogs = [k.b16(BR, "ogs%d" % i) for i in range(4)]
    xs = [k.f32(D, "xsF%d" % i) for i in range(4)]
    tmp = [k.f32(D, "tmpF%d" % i) for i in range(2)]
    x1 = [k.f32(D, "x1F%d" % i) for i in range(2)]
    sm = [k.f32(16, "smF%d" % i) for i in range(2)]
    junk, eps, ident = c["junk"], c["eps"], c["ident"]

    def load(i):
        o = ogs[i % 4]
        S.dma("sp", o.ap, dr["ogT"][i], W=[o.b])
        x = xs[i % 4]
        S.dma("sp", x.ap, xin_d[i * 128:(i + 1) * 128, :], W=[x.b])

    def ymm(i):
        o = ogs[i % 4]
        py, pyb = k.banks(2 * (i % 3), 2)
        for half in range(2):
            for cc in range(16):
                k.mm(py[:, half * 512:(half + 1) * 512], o.ap[:, cc * 128:(cc + 1) * 128],
                     wo3[:, cc, half * 512:(half + 1) * 512], cc == 0, cc == 15, [o.b, wbuf[cc // 2]], [pyb[half]])
        return py, pyb

    def h1_transposes(i):
        hh = h1[i % 2]
        pb, pbb = k.banks(6 + (i % 2))
        pv = pb.bitcast(BF16)
        for kc in range(8):
            k.tr(pv[:, kc * 128:(kc + 1) * 128], hh.ap[:, kc * 128:(kc + 1) * 128], ident.ap, [hh.b, ident.b], pbb)
        k.copy("act", v3(h1T.ap, 8)[:, :, i * 128:(i + 1) * 128], v3(pv, 8), pbb, [h1T.b])

    load(0)
    load(1)
    load(2)
    yq = [ymm(0), ymm(1)]
    for i in range(NT):
        if i + 3 < NT:
            load(i + 3)
        x, s = xs[i % 4], sm[i % 2]
        py, pyb = yq.pop(0)
        if i + 2 < NT:
            yq.append(ymm(i + 2))
        k.act(junk.ap[:, 0:D], py, AF.Square, pyb, [junk.b, s.b], accum=s.ap[:, 0:1])
        k.rstd(s.ap[:, 0:1], s.ap[:, 1:2], s.ap[:, 2:3], D, eps.ap[:, 0:1], [s.b, eps.b], [s.b])
        t = tmp[i % 2]
        k.stt("dve", t.ap, py, s.ap[:, 2:3], postg.ap, ALU.mult, ALU.mult, pyb + [s.b, postg.b], [t.b])
        xo = x1[i % 2]
        k.tt("pool", xo.ap, t.ap, x.ap, ALU.add, [t.b, x.b], [xo.b])
        S.dma("sp", out_d[i * 128:(i + 1) * 128, :], xo.ap, R=[xo.b])
        if h1T is not None:
            k.act(junk.ap[:, 0:D], xo.ap, AF.Square, [xo.b], [junk.b, s.b], accum=s.ap[:, 4:5])
            k.rstd(s.ap[:, 4:5], s.ap[:, 5:6], s.ap[:, 6:7], D, eps.ap[:, 0:1], [s.b, eps.b], [s.b])
            hh = h1[i % 2]
            k.stt("dve", hh.ap, xo.ap, s.ap[:, 6:7], preg.ap, ALU.mult, ALU.mult, [xo.b, s.b, preg.b], [hh.b])
            if i >= 1:
                h1_transposes(i - 1)
    if h1T is not None:
        h1_transposes(NT - 1)
    S.barrier()
    k.AF.release(mF)
    k.AB.release(mB)


def phase_A1(k, c, dr, h1T):
    S = k.S
    mF, mB = k.AF.mark(), k.AB.mark()
    h3 = v3(h1T.ap, 8)
    cst = [k.f32(512, "cosT%d" % i) for i in range(2)]
    snt = [k.f32(512, "sinT%d" % i) for i in range(2)]
    lv = k.f32(256, "lv")
    for j, nm in enumerate(("diff_lam_q1", "diff_lam_k1", "diff_lam_q2", "diff_lam_k2")):
        bcast_load(k, lv.ap[:, j * 64:(j + 1) * 64], lv.b, dr[nm])
    sc = k.f32(16, "scA1")
    junkf = k.f32(64, "junkf")
    k.stt("dve", junkf.ap, lv.ap[:, 0:64], 1.0, lv.ap[:, 64:128], ALU.mult, ALU.mult, [lv.b], [junkf.b, sc.b],
          accum=sc.ap[:, 0:1])
    k.stt("dve", junkf.ap, lv.ap[:, 128:192], 1.0, lv.ap[:, 192:256], ALU.mult, ALU.mult, [lv.b], [junkf.b, sc.b],
          accum=sc.ap[:, 1:2])
    k.act(sc.ap[:, 2:4], sc.ap[:, 0:2], AF.Exp, [sc.b], [sc.b])
    k.tt("dve", sc.ap[:, 4:5], sc.ap[:, 3:4], sc.ap[:, 2:3], ALU.subtract, [sc.b], [sc.b])
    k.ts("dve", sc.ap[:, 5:6], sc.ap[:, 4:5], -LAM_INIT, ALU.add, [sc.b], [sc.b])
    S.dma("sp", sc.ap[:, 6:7], dr["diff_norm_g"].rearrange("o v -> v o"), W=[sc.b])
    k.ts("dve", sc.ap[:, 7:8], sc.ap[:, 6:7], 1.0 - LAM_INIT, ALU.mult, [sc.b], [sc.b])
    neglam = sc.ap[:, 5:6]
    gcol = sc.ap[:, 7:8]

    stg = [k.f32(4 * 512, "stgA%d" % i) for i in range(2)]
    wh = [k.b16(8 * 512, "wh%d" % i) for i in range(2)]
    QT = k.b16(T, "QT")
    KT = k.b16(T, "KT")
    V = k.b16(T, "V")
    sgT = k.b16(T, "sgT")
    qraw = [k.b16(512, "qraw%d" % i) for i in range(2)]
    t1 = [k.f32(512, "t1_%d" % i) for i in range(2)]
    t2 = [k.f32(512, "t2_%d" % i) for i in range(2)]
    PT = [k.b16(1024, "PT%d" % i) for i in range(3)]
    rlb = k.b16(512, "rlb")
    lnl = k.f32(1024, "lnl")
    o01 = k.f32(1024, "o01")
    of = k.f32(512, "of")
    sq = k.b16(512, "sq")
    lnt = k.f32(512, "lnt")
    rst = k.f32(512, "rst")
    tf = k.f32(512, "tf")
    ogh = [k.b16(512, "ogh%d" % i) for i in range(2)]
    ones, rperm, maskT2, eps = c["ones"], c["rperm"], c["maskT2"], c["eps"]
    ident, negm = c["ident"], c["negm"]
    pO, pOb = k.banks(4, 2)
    pL, pLb = k.banks(6, 1)
    pX, pXb = k.banks(7, 1)
    selb = c["selb"]
    nq = 0
    npt = 0
    nog = 0

    def dma_w(h):
        for half in range(2):
            st = stg[half]
            s3 = v3(st.ap, 4)
            for blk in range(4):
                S.dma("sp", s3[:, :, blk * 128:(blk + 1) * 128],
                      dr["diff_w_in"][half * 512:(half + 1) * 512, blk * BR + h * 128:blk * BR + (h + 1) * 128]
                      .rearrange("(k p) n -> p k n", p=128), W=[st.b])

    def cast_w(h):
        for half in range(2):
            st = stg[half]
            k.copy("dve", wh[h % 2].ap[:, half * 2048:(half + 1) * 2048], st.ap, [st.b], [wh[h % 2].b])

    ntab = [0]

    def load_tab(tt_):
        j = tt_ % 2
        S.dma("sp", cst[j].ap, dr["cosT"][:, tt_ * 512:(tt_ + 1) * 512], W=[cst[j].b])
        S.dma("sp", snt[j].ap, dr["sinT"][:, tt_ * 512:(tt_ + 1) * 512], W=[snt[j].b])

    dma_w(0)
    cast_w(0)
    load_tab(0)
    load_tab(1)
    pend = []
    for h in range(DBG["heads"]):
        w3 = v3(wh[h % 2].ap, 8)
        wb = wh[h % 2].b
        for tt_ in range(8 if DBG["inproj"] else 0):
            ts_ = slice(tt_ * 512, (tt_ + 1) * 512)
            cosT, sinT = cst[tt_ % 2], snt[tt_ % 2]
            if tt_ >= 1:
                for _ in range(3):
                    if pend:
                        _, fn, q_, h_ = pend.pop(0)
                        fn(q_, h_)
            pqs = []
            for blk in (0, 1):
                pq, pqb = k.nextbank(1, 0, 7)
                for kc in range(8):
                    k.mm(pq, w3[:, kc, blk * 128:(blk + 1) * 128], h3[:, kc, ts_], kc == 0, kc == 7, [wb, h1T.b], pqb)
                pqs.append((pq, pqb))
            pg, pgb = k.nextbank(1, 0, 7)
            for kc in range(8):
                k.mm(pg, w3[:, kc, 384:512], h3[:, kc, ts_], kc == 0, kc == 7, [wb, h1T.b], pgb)
            pv, pvb = k.nextbank(1, 0, 7)
            for s_ in range(4):
                for kc in range(8):
                    k.mm(pv[:, s_ * 128:(s_ + 1) * 128], h3[:, kc, tt_ * 512 + s_ * 128:tt_ * 512 + (s_ + 1) * 128],
                         w3[:, kc, 256:384], kc == 0, kc == 7, [wb, h1T.b], pvb)
            rots = []
            for blk, dst in ((0, QT), (1, KT)):
                pq, pqb = pqs[blk]
                qr = qraw[nq % 2]
                a1, a2 = t1[nq % 2], t2[nq % 2]
                nq += 1
                k.copy("act", qr.ap, pq, pqb, [qr.b])
                rots.append((pq, pqb, qr, a1, a2, dst))
            k.act(sgT.ap[:, ts_], pg, AF.Silu, pgb, [sgT.b])
            k.copy("act", V.ap[:, ts_], pv, pvb, [V.b])
            for (pq, pqb, qr, a1, a2, dst) in rots:
                pr, prb = k.nextbank(1, 0, 7)
                k.mm(pr, rperm.ap, qr.ap, True, True, [rperm.b, qr.b], prb)
                k.tt("dve", a1.ap, pq, cosT.ap, ALU.mult, pqb + [cosT.b], [a1.b])
                k.tt("dve", a2.ap, pr, sinT.ap, ALU.mult, prb + [sinT.b], [a2.b])
                k.tt("pool", dst.ap[:, ts_], a1.ap, a2.ap, ALU.add, [a1.b, a2.b], [dst.b])
            if tt_ + 2 < 8:
                load_tab(tt_ + 2)
        if h + 1 < 16:
            dma_w(h + 1)
        blocks = [(qt, kb) for qt in range(DBG["nqt"] if DBG["att"] else 0) for kb in range(4 * qt + 4)]

        def qk_exp(qt, kb):
            nonlocal npt
            j = kb - 4 * qt
            c0 = max(j, 0) * 128
            ps2, ps2b = k.banks(2 * (npt % 2), 2)
            pt_ = PT[npt % 3]
            npt += 1
            for cc in range(2):
                k.mm(ps2[:, cc * 512 + c0:(cc + 1) * 512], KT.ap[cc * 64:(cc + 1) * 64, kb * 128:(kb + 1) * 128],
                     QT.ap[cc * 64:(cc + 1) * 64, qt * 512 + c0:(qt + 1) * 512], True, j < 0, [KT.b, QT.b],
                     [ps2b[cc]])
            if j >= 0:
                for cc in range(2):
                    k.mm(ps2[:, cc * 512 + c0:cc * 512 + c0 + 128], ident.ap, negm.ap, False, True,
                         [ident.b, negm.b], [ps2b[cc]])
            p3 = v3(pt_.ap, 2)
            k.act(p3[:, :, c0:512], v3(ps2, 2)[:, :, c0:512], AF.Exp, ps2b, [pt_.b], scale=0.125)
            return pt_, c0

        def av(qt, kb, pt_, c0):
            nkb = 4 * qt + 4
            for cc in range(2):
                k.mmt(pL[32 * cc:32 * cc + 32, c0:512], ones.ap[:, 0:32], pt_.ap[:, cc * 512 + c0:(cc + 1) * 512],
                      kb == 0, kb == nkb - 1, (0, 32 * cc), [ones.b, pt_.b], pLb)
            for cc in range(2):
                k.mm(pO[:, cc * 512 + c0:(cc + 1) * 512], V.ap[:, kb * 128:(kb + 1) * 128],
                     pt_.ap[:, cc * 512 + c0:(cc + 1) * 512], kb == 0, kb == nkb - 1, [V.b, pt_.b], [pOb[cc]])

        def epi1(qt, hd):
            k.act(lnl.ap[0:64, 0:512], pL[0:64, :], AF.Ln, pLb, [lnl.b])
            k.copy("dve", o01.ap[:, 0:512], pO[:, 0:512], [pOb[0]], [o01.b])
            k.copy("dve", o01.ap[:, 512:1024], pO[:, 512:1024], [pOb[1]], [o01.b])
            k.act(rlb.ap[0:64, 0:512], lnl.ap[0:64, 0:512], AF.Exp, [lnl.b], [rlb.b], scale=-1.0)

        def st_b0(qt, hd):
            k.mm(pX, selb.ap[0:64, 0:128], rlb.ap[0:64, 0:512], True, True, [selb.b, rlb.b], pXb)

        def st_m0(qt, hd):
            k.tt("dve", o01.ap[:, 0:512], o01.ap[:, 0:512], pX, ALU.mult, [o01.b] + pXb, [o01.b])

        def st_b1(qt, hd):
            k.mm(pX, selb.ap[0:64, 128:256], rlb.ap[0:64, 0:512], True, True, [selb.b, rlb.b], pXb)

        def st_m1(qt, hd):
            k.tt("dve", o01.ap[:, 512:1024], o01.ap[:, 512:1024], pX, ALU.mult, [o01.b] + pXb, [o01.b])
            k.stt("dve", of.ap, o01.ap[:, 512:1024], neglam, o01.ap[:, 0:512], ALU.mult, ALU.add, [o01.b, sc.b], [of.b])

        def st_sq(qt, hd):
            k.act(sq.ap, of.ap, AF.Square, [of.b], [sq.b])

        def st_ss(qt, hd):
            k.mm(pX, ones.ap, sq.ap, True, True, [ones.b, sq.b], pXb)

        def st_ln(qt, hd):
            k.act(lnt.ap, pX, AF.Ln, pXb + [eps.b], [lnt.b], scale=1.0 / 128, bias=eps.ap[:, 0:1])

        def st_ex(qt, hd):
            k.act(rst.ap, lnt.ap, AF.Exp, [lnt.b], [rst.b], scale=-0.5)

        def st_fin(qt, hd):
            nonlocal nog
            qs = slice(qt * 512, (qt + 1) * 512)
            k.tt("dve", tf.ap, of.ap, rst.ap, ALU.mult, [of.b, rst.b], [tf.b])
            og_ = ogh[nog % 2]
            nog += 1
            k.stt("dve", og_.ap, tf.ap, gcol, sgT.ap[:, qs], ALU.mult, ALU.mult, [tf.b, sc.b, sgT.b], [og_.b])
            S.dma("sp", dr["ogT"][qt * 4:(qt + 1) * 4, :, hd * 128:(hd + 1) * 128].rearrange("s p v -> p s v"),
                  v3(og_.ap, 4), R=[og_.b])

        STAGES0 = ((2, st_b0), (3, st_m0), (4, st_b1), (5, st_m1), (6, st_sq), (7, st_ss), (8, st_ln), (8, st_ex),
                   (8, st_fin))
        STAGES = ((2, st_b0), (3, st_m0), (4, st_b1), (5, st_m1), (6, st_sq), (7, st_ss), (8, st_ln), (9, st_ex),
                  (10, st_fin))

        cur = qk_exp(*blocks[0]) if blocks else None
        for bi, (qt, kb) in enumerate(blocks):
            nxt = qk_exp(*blocks[bi + 1]) if bi + 1 < len(blocks) else None
            av(qt, kb, *cur)
            while pend and bi >= pend[0][0]:
                _, fn, q_, h_ = pend.pop(0)
                fn(q_, h_)
            if kb == 4 * qt + 3 and DBG["epi"]:
                epi1(qt, h)
                for off, fn in (STAGES0 if qt == 0 else STAGES):
                    pend.append((bi + off, fn, qt, h))
            if bi == 10 and h + 1 < 16:
                cast_w(h + 1)
            if bi == 20 and h + 1 < 16:
                load_tab(0)
                load_tab(1)
            cur = nxt
        pend = [(-1, fn, q_, h_) for (_, fn, q_, h_) in pend]
    for _, fn, q_, h_ in pend:
        fn(q_, h_)
    S.barrier()
    k.AF.release(mF)
    k.AB.release(mB)


NAF = 16 * 1024
NAB = 68 * 1024

_INPUT_NAMES = ("x", "pre_g", "post_g", "gla_w_in", "gla_w_g2", "gla_b_g", "gla_norm_g", "gla_w_out",
                "diff_w_in", "diff_lam_q1", "diff_lam_k1", "diff_lam_q2", "diff_lam_k2", "diff_norm_g", "diff_w_out")
_SHAPES = {
    "x": [T, D], "pre_g": [2, D], "post_g": [2, D], "gla_w_in": [D, GIN], "gla_w_g2": [16, 512],
    "gla_b_g": [1, 512], "gla_norm_g": [1, 512], "gla_w_out": [BR, D], "diff_w_in": [D, 4 * BR],
    "diff_lam_q1": [1, 64], "diff_lam_k1": [1, 64], "diff_lam_q2": [1, 64], "diff_lam_k2": [1, 64],
    "diff_norm_g": [1, 128], "diff_w_out": [BR, D],
}


def build(mode="both"):
    nc = bass.Bass("TRN2", target_bir_lowering=False)
    dr = {}
    for nm in _INPUT_NAMES:
        if mode == "L1" and nm == "x":
            continue
        dr[nm] = nc.dram_tensor(nm, _SHAPES[nm], F32, kind="ExternalInput").ap()
    dr["cst_bf"] = nc.dram_tensor("cst_bf", [128, 1024], BF16, kind="ExternalInput").ap()
    dr["cst_f"] = nc.dram_tensor("cst_f", [128, 1024], F32, kind="ExternalInput").ap()
    dr["cosT"] = nc.dram_tensor("cosT", [128, T], F32, kind="ExternalInput").ap()
    dr["sinT"] = nc.dram_tensor("sinT", [128, T], F32, kind="ExternalInput").ap()
    dr["ogT"] = nc.dram_tensor("ogT_scr", [NT, 128, BR], BF16, kind="Internal").ap()
    if mode == "both":
        dr["x1"] = nc.dram_tensor("x1_scr", [T, D], F32, kind="Internal").ap()
        dr["out"] = nc.dram_tensor("out", [T, D], F32, kind="ExternalOutput").ap()
    elif mode == "L0":
        dr["x1"] = nc.dram_tensor("x1", [T, D], F32, kind="ExternalOutput").ap()
        dr["h1T_d"] = nc.dram_tensor("h1T", [128, 8 * T], BF16, kind="ExternalOutput").ap()
    else:
        dr["x1"] = nc.dram_tensor("x1", [T, D], F32, kind="ExternalInput").ap()
        dr["h1T_d"] = nc.dram_tensor("h1T", [128, 8 * T], BF16, kind="ExternalInput").ap()
        dr["out"] = nc.dram_tensor("out", [T, D], F32, kind="ExternalOutput").ap()

    with ExitStack() as st:
        S = Sched(nc, st)
        af = st.enter_context(nc.sbuf_tensor("arena_f", [128, NAF], F32))
        ab = st.enter_context(nc.sbuf_tensor("arena_b", [128, NAB], BF16))
        ps = st.enter_context(nc.psum_tensor("ps", [128, 4096], F32))
        k = K(nc, S, Arena(af, NAF), Arena(ab, NAB), ps)
        c = load_consts(k, dr)
        if mode in ("both", "L0"):
            phase_A0(k, c, dr)
        h1T = k.b16(8 * T, "h1T")
        if mode in ("both", "L0"):
            phase_F(k, c, dr, dr["gla_w_out"], dr["post_g"][0:1, :], dr["x"], dr["x1"],
                    pre_row=dr["pre_g"][1:2, :], h1T=h1T)
        if mode == "L0":
            for j in range(8):
                S.dma("sp", dr["h1T_d"][:, j * T:(j + 1) * T], h1T.ap[:, j * T:(j + 1) * T], R=[h1T.b])
            S.barrier()
        if mode == "L1":
            for j in range(8):
                S.dma("sp", h1T.ap[:, j * T:(j + 1) * T], dr["h1T_d"][:, j * T:(j + 1) * T], W=[h1T.b])
        if mode in ("both", "L1"):
            phase_A1(k, c, dr, h1T)
            if DBG["F1"]:
                phase_F(k, c, dr, dr["diff_w_out"], dr["post_g"][1:2, :], dr["x1"], dr["out"])
        S.barrier()
        print("ops", S.nops, "arena peaks f32 %d bf16 %d" % (k.AF.peak, k.AB.peak), "sems", sum(1 for b in S.allbufs if b.dsem is not None))
        with nc.Block() as block:
            S.replay(block)
    return nc


def _consts():
    bf = ml_dtypes.bfloat16
    p = np.arange(128)
    ident = np.eye(128, dtype=np.float32)
    perm = np.where((p % 64) < 32, p + 32, p - 32)
    rperm = np.zeros((128, 128), np.float32)
    rperm[perm, p] = 1.0
    ones = np.ones((128, 128), np.float32)
    maskT = (p[None, :] >= p[:, None]).astype(np.float32)
    negm = np.where(p[None, :] >= p[:, None], 0.0, -30000.0).astype(np.float32)
    selb = np.zeros((128, 256), np.float32)
    selb[0, 0:128] = 1.0
    selb[32, 128:256] = 1.0
    cst_bf = np.concatenate([ident, rperm, ones, maskT, maskT, negm, selb], 1).astype(bf)
    triU = np.where(p[:, None] <= p[None, :], -1.0 / 16.0, 0.0).astype(np.float32)
    triL = np.where(p[:, None] > p[None, :], -1.0 / 16.0, 0.0).astype(np.float32)
    maskU = (p[:, None] <= p[None, :]).astype(np.float32)
    sel = np.zeros((128, 256), np.float32)
    sel[0, 0:128] = 1.0
    sel[32, 128:256] = 1.0
    cst_f = np.concatenate([triU, triL, maskU, maskU, maskU, maskU, sel], 1).astype(np.float32)
    inv_freq = (1.0 / (np.float32(10000.0) ** (np.arange(0, 64, 2, dtype=np.float32) / np.float32(64)))).astype(np.float32)
    pos = np.arange(T, dtype=np.float32)
    ang = (pos[:, None] * inv_freq[None, :]).astype(np.float32)
    cos = np.cos(ang).astype(np.float32).T
    sin = np.sin(ang).astype(np.float32).T
    cosT = np.concatenate([cos, cos, cos, cos], 0)
    sinT = np.concatenate([-sin, sin, -sin, sin], 0)
    return {"cst_bf": np.ascontiguousarray(cst_bf), "cst_f": np.ascontiguousarray(cst_f),
            "cosT": np.ascontiguousarray(cosT), "sinT": np.ascontiguousarray(sinT)}


_NC_CACHE = {}


def _get_nc(mode):
    if mode not in _NC_CACHE:
        _NC_CACHE[mode] = build(mode)
    return _NC_CACHE[mode]


def _shared_maps(inputs):
    m = dict(_consts())
    for nm in _INPUT_NAMES:
        if nm == "x":
            continue
        a = np.asarray(inputs[nm], dtype=np.float32)
        if nm in ("pre_g", "post_g"):
            m[nm] = np.ascontiguousarray(a)
        else:
            m[nm] = np.ascontiguousarray(a.reshape(_SHAPES[nm]))
    return m


FUSED = True


def kernel(**inputs):
    x = np.asarray(inputs["x"], dtype=np.float32)
    shared = _shared_maps(inputs)
    cores = list(range(NCORES))
    if FUSED:
        nc = _get_nc("both")
        in_maps = [dict(shared, x=np.ascontiguousarray(x[b])) for b in cores]
        res = run_bass_kernel_spmd(nc, in_maps, core_ids=cores)
        return np.stack([np.asarray(r["out"]) for r in res.results], 0).astype(np.float32)
    nc0 = _get_nc("L0")
    in_maps = [dict(shared, x=np.ascontiguousarray(x[b])) for b in cores]
    r0 = run_bass_kernel_spmd(nc0, in_maps, core_ids=cores).results
    nc1 = _get_nc("L1")
    in_maps = [dict(shared, x1=np.asarray(r0[b]["x1"]), h1T=np.asarray(r0[b]["h1T"])) for b in cores]
    r1 = run_bass_kernel_spmd(nc1, in_maps, core_ids=cores).results
    return np.stack([np.asarray(r["out"]) for r in r1], 0).astype(np.float32)
```
